# Optimizing a Trainium2 kernel written in Bass

```python
import jax, jax.numpy as jnp
from jax import lax
import numpy as np


D_MODEL = 2048
BATCH = 4
SEQ = 4096
DEPTH = 4

BRANCH_WIDTH = D_MODEL // 2
N_BRANCH = 3
HEAD_A = 64
N_HEADS_A = BRANCH_WIDTH // HEAD_A
LORA_DECAY = 64
LORA_AAA = 64
LORA_GATE = 160
DECAY_SCALE = 0.606531
GN_EPS = 64e-5
HEAD_FB = 128
HEAD_IB = 128
N_HEADS_B = BRANCH_WIDTH // HEAD_IB
CHUNK = 64
F_TINY = 1e-30
QK_NOPE = 128
QK_ROPE = 64
V_HEAD = 128
N_HEADS_C = BRANCH_WIDTH // V_HEAD
Q_LORA = 3 * D_MODEL // 8
KV_LORA = D_MODEL // 4
Q_BLOCK = 128
ROPE_THETA = 10000.0
D_FF = 4 * D_MODEL
PLE_DIM = 256
NORM_EPS = 1e-6

RWKV_W = 3 * BRANCH_WIDTH + 2 * LORA_DECAY + 2 * LORA_AAA + LORA_GATE
HGRN_W = 5 * BRANCH_WIDTH
MLA_W = Q_LORA + KV_LORA + QK_ROPE
GATE_W = N_BRANCH * D_MODEL
IN_W = RWKV_W + HGRN_W + MLA_W + GATE_W

kernel_name = 'hybrid_rwkv7_hgrn2_mla_encoder'


def _split(t, sizes):
    out, start = [], 0
    for s in sizes:
        out.append(t[..., start:start + s])
        start += s
    return out


def rms_norm(t, g):
    tf = t.astype(jnp.float32)
    tf = tf * lax.rsqrt(jnp.mean(tf * tf, axis=-1, keepdims=True) + NORM_EPS)
    return (tf * g.astype(jnp.float32)).astype(t.dtype)


def centred_shift(u, mu):
    prev = jnp.pad(u, ((0, 0), (1, 0), (0, 0)))[:, :-1]
    nxt = jnp.pad(u, ((0, 0), (0, 1), (0, 0)))[:, 1:]
    return u + mu[0] * (prev - u) + mu[1] * (nxt - u)


def rwkv7_scan(r, w, k, v, kk, a, reverse):
    Bn, S, H, N = r.shape

    def step(state, inp):
        r_t, w_t, k_t, v_t, kk_t, a_t = inp
        sa = jnp.einsum('bhvk,bhk->bhv', state, -kk_t)
        state = (state * w_t[:, :, None, :]
                 + sa[..., None] * (kk_t * a_t)[:, :, None, :]
                 + v_t[..., None] * k_t[:, :, None, :])
        return state, jnp.einsum('bhvk,bhk->bhv', state, r_t)

    xs = tuple(jnp.moveaxis(t, 1, 0) for t in (r, w, k, v, kk, a))
    init = jnp.zeros((Bn, H, N, N), jnp.float32)
    _, ys = lax.scan(step, init, xs, reverse=reverse)
    return jnp.moveaxis(ys, 0, 1)


def rwkv7_mixer(u, mu, w0, w2, a0, a2, g2, k_k, k_a, r_k, gn_w, gn_b):
    dt = u.dtype
    Bn, S, _ = u.shape
    f32 = jnp.float32
    u = centred_shift(u.astype(f32), mu.astype(f32))
    r, k, v, wd, ad, gd = _split(u, [BRANCH_WIDTH] * 3 + [2 * LORA_DECAY, 2 * LORA_AAA, LORA_GATE])
    wd = wd.reshape(Bn, S, 2, LORA_DECAY)
    ad = ad.reshape(Bn, S, 2, LORA_AAA)
    w_raw = w0 + jnp.einsum('bsnl,nlc->bsnc', jnp.tanh(wd), w2)
    decay = jnp.exp(-DECAY_SCALE * jax.nn.sigmoid(w_raw))
    a = jax.nn.sigmoid(a0 + jnp.einsum('bsnl,nlc->bsnc', ad, a2))
    g = jax.nn.sigmoid(gd) @ g2
    heads = lambda t: t.reshape(Bn, S, N_HEADS_A, HEAD_A)
    kk = heads(k * k_k)
    kk = kk / jnp.maximum(jnp.sqrt(jnp.sum(kk * kk, axis=-1, keepdims=True)), 1e-12)
    kd = k[:, :, None, :] * (1.0 + (a - 1.0) * k_a)
    rh, vh = heads(r), heads(v)
    y = (rwkv7_scan(rh, heads(decay[:, :, 0]), heads(kd[:, :, 0]), vh, kk, heads(a[:, :, 0]), False)
         + rwkv7_scan(rh, heads(decay[:, :, 1]), heads(kd[:, :, 1]), vh, kk, heads(a[:, :, 1]), True))
    mean = jnp.mean(y, axis=-1, keepdims=True)
    var = jnp.mean(jnp.square(y - mean), axis=-1, keepdims=True)
    y = ((y - mean) * lax.rsqrt(var + GN_EPS)).reshape(Bn, S, BRANCH_WIDTH) * gn_w + gn_b
    bonus = jnp.sum(rh * heads(kd[:, :, 0] + kd[:, :, 1]) * r_k, axis=-1, keepdims=True) * vh
    y = y + bonus.reshape(Bn, S, BRANCH_WIDTH)
    return (y * g).astype(dt)


def gla_chunk_scan(q, k, v, logf):
    Bn, H, S, DK = q.shape
    DV = v.shape[-1]
    nc = S // CHUNK
    chunks = lambda t: jnp.moveaxis(t.reshape(Bn, H, nc, CHUNK, t.shape[-1]), 2, 0)
    lower = jnp.tril(jnp.ones((CHUNK, CHUNK), dtype=bool))[:, :, None]

    def step(state, inp):
        qc, kc, vc, gc = inp
        b = jnp.cumsum(gc, axis=2)
        b_last = b[:, :, -1:, :]
        inter = jnp.einsum('bhtk,bhkv->bhtv', qc * jnp.exp(b), state)
        diff = b[:, :, :, None, :] - b[:, :, None, :, :]
        rel = jnp.where(lower, jnp.exp(jnp.where(lower, diff, 0.0)), 0.0)
        scores = jnp.einsum('bhtk,bhsk,bhtsk->bhts', qc, kc, rel)
        intra = jnp.einsum('bhts,bhsv->bhtv', scores, vc)
        state = (state * jnp.exp(b_last)[:, :, 0, :, None]
                 + jnp.einsum('bhsk,bhsv->bhkv', kc * jnp.exp(b_last - b), vc))
        return state, inter + intra

    init = jnp.zeros((Bn, H, DK, DV), jnp.float32)
    _, out = lax.scan(step, init, (chunks(q), chunks(k), chunks(v), chunks(logf)))
    return jnp.moveaxis(out, 0, 2).reshape(Bn, H, S, DV)


def hgrn2_mixer(u, lb, norm_g):
    dt = u.dtype
    Bn, S, _ = u.shape
    q, z_f, z_b, i, g = jnp.split(u.astype(jnp.float32), 5, axis=-1)
    q = jax.nn.silu(q)
    lb = lb.astype(jnp.float32)

    def gate(zz):
        f = lb + (1.0 - lb) * jax.nn.sigmoid(zz)
        return (1.0 - lb) * jax.nn.sigmoid(-zz), jnp.log(jnp.maximum(f, F_TINY))

    k_f, logf_f = gate(z_f)
    k_b, logf_b = gate(z_b)
    heads = lambda t: t.reshape(Bn, S, N_HEADS_B, -1).transpose(0, 2, 1, 3)
    flip = lambda t: jnp.flip(t, axis=2)
    qh, ih = heads(q), heads(i)
    o = (gla_chunk_scan(qh, heads(k_f), ih, heads(logf_f))
         + flip(gla_chunk_scan(flip(qh), flip(heads(k_b)), flip(ih), flip(heads(logf_b)))))
    o = o.transpose(0, 2, 1, 3)
    o = rms_norm(o, norm_g) * jax.nn.silu(g.reshape(Bn, S, N_HEADS_B, HEAD_IB))
    return o.reshape(Bn, S, BRANCH_WIDTH).astype(dt)


def apply_rope(t, cos, sin):
    t1, t2 = jnp.split(t.astype(jnp.float32), 2, axis=-1)
    return jnp.concatenate([t1 * cos - t2 * sin, t2 * cos + t1 * sin], axis=-1).astype(t.dtype)


def mla_mixer(u, positions, q_norm_g, kv_norm_g, w_uq, w_ukv):
    Bn, S, _ = u.shape
    H = N_HEADS_C
    cq, ckv, k_rope = _split(u, [Q_LORA, KV_LORA, QK_ROPE])
    q = (rms_norm(cq, q_norm_g) @ w_uq).reshape(Bn, S, H, QK_NOPE + QK_ROPE)
    kv = (rms_norm(ckv, kv_norm_g) @ w_ukv).reshape(Bn, S, H, QK_NOPE + V_HEAD)
    q_nope, q_rope = q[..., :QK_NOPE], q[..., QK_NOPE:]
    k_nope, v = kv[..., :QK_NOPE], kv[..., QK_NOPE:]
    inv_freq = 1.0 / (ROPE_THETA ** (jnp.arange(0, QK_ROPE, 2, dtype=jnp.float32) / QK_ROPE))
    ang = positions.astype(jnp.float32)[..., None] * inv_freq
    cos, sin = jnp.cos(ang), jnp.sin(ang)
    q_rope = apply_rope(q_rope, cos[:, :, None], sin[:, :, None])
    k_rope = apply_rope(k_rope, cos, sin)
    scale = (QK_NOPE + QK_ROPE) ** -0.5
    nb = S // Q_BLOCK
    blocks = lambda t: jnp.moveaxis(t.reshape(Bn, nb, Q_BLOCK, H, t.shape[-1]), 1, 0)

    def attend(qb):
        qn, qr = qb
        s = (jnp.einsum('bqhd,bkhd->bhqk', qn, k_nope)
             + jnp.einsum('bqhd,bkd->bhqk', qr, k_rope))
        w = jax.nn.softmax(s.astype(jnp.float32) * scale, axis=-1).astype(v.dtype)
        return jnp.einsum('bhqk,bkhd->bqhd', w, v)

    o = lax.map(attend, (blocks(q_nope), blocks(q_rope)))
    return jnp.moveaxis(o, 0, 1).reshape(Bn, S, H * V_HEAD)


def setup_inputs(seed: int = 0) -> dict:
    key = jax.random.key(seed)
    ks = jax.random.split(key, 32)
    L = DEPTH
    nrm = lambda k, shape, s: jax.random.normal(k, shape, jnp.float32) * s
    gain = lambda k, shape: 1.0 + 0.02 * jax.random.normal(k, shape, jnp.float32)
    return {
        'x': nrm(ks[0], (BATCH, SEQ, D_MODEL), 1.0),
        'p': nrm(ks[1], (DEPTH, BATCH, SEQ, PLE_DIM), 1.0),
        'positions': (jnp.arange(SEQ, dtype=jnp.int32)[None, :]
                      + jax.random.randint(ks[2], (BATCH, 1), 0, SEQ, dtype=jnp.int32)),
        'ln1_g': gain(ks[3], (L, D_MODEL)),
        'w_in': nrm(ks[4], (L, D_MODEL, IN_W), D_MODEL ** -0.5),
        'rwkv_mu': jax.random.uniform(ks[5], (L, 2, RWKV_W), jnp.float32, 0.0, 0.5),
        'rwkv_w0': nrm(ks[6], (L, 2, BRANCH_WIDTH), 1.0),
        'rwkv_w2': nrm(ks[7], (L, 2, LORA_DECAY, BRANCH_WIDTH), 0.5 * LORA_DECAY ** -0.5),
        'rwkv_a0': nrm(ks[8], (L, 2, BRANCH_WIDTH), 0.5),
        'rwkv_a2': nrm(ks[9], (L, 2, LORA_AAA, BRANCH_WIDTH), 0.5 * LORA_AAA ** -0.5),
        'rwkv_g2': nrm(ks[10], (L, LORA_GATE, BRANCH_WIDTH), LORA_GATE ** -0.5),
        'rwkv_kk': 1.0 + nrm(ks[11], (L, BRANCH_WIDTH), 0.1),
        'rwkv_ka': 1.0 + nrm(ks[12], (L, BRANCH_WIDTH), 0.1),
        'rwkv_rk': nrm(ks[13], (L, N_HEADS_A, HEAD_A), 0.1),
        'rwkv_gn_w': gain(ks[14], (L, BRANCH_WIDTH)),
        'rwkv_gn_b': nrm(ks[15], (L, BRANCH_WIDTH), 0.02),
        'hgrn_lb': nrm(ks[16], (L, BRANCH_WIDTH), 0.1),
        'hgrn_norm_g': gain(ks[17], (L, HEAD_IB)),
        'mla_q_norm_g': gain(ks[18], (L, Q_LORA)),
        'mla_kv_norm_g': gain(ks[19], (L, KV_LORA)),
        'mla_w_uq': nrm(ks[20], (L, Q_LORA, N_HEADS_C * (QK_NOPE + QK_ROPE)), Q_LORA ** -0.5),
        'mla_w_ukv': nrm(ks[21], (L, KV_LORA, N_HEADS_C * (QK_NOPE + V_HEAD)), KV_LORA ** -0.5),
        'w_branch': nrm(ks[22], (L, N_BRANCH, BRANCH_WIDTH, D_MODEL), BRANCH_WIDTH ** -0.5),
        'w_o': nrm(ks[23], (L, D_MODEL, D_MODEL), D_MODEL ** -0.5),
        'ln2_g': gain(ks[24], (L, D_MODEL)),
        'w_mlp1': nrm(ks[25], (L, D_MODEL, D_FF), D_MODEL ** -0.5),
        'w_mlp2': nrm(ks[26], (L, D_FF, D_MODEL), D_FF ** -0.5),
        'w_pe': nrm(ks[27], (L, PLE_DIM, D_MODEL), PLE_DIM ** -0.5),
        'w_pg': nrm(ks[28], (L, D_MODEL, D_MODEL), D_MODEL ** -0.5),
        'final_g': gain(ks[29], (D_MODEL,)),
    }


def reference(x, p, positions, ln1_g, w_in, rwkv_mu, rwkv_w0, rwkv_w2, rwkv_a0, rwkv_a2,
              rwkv_g2, rwkv_kk, rwkv_ka, rwkv_rk, rwkv_gn_w, rwkv_gn_b, hgrn_lb, hgrn_norm_g,
              mla_q_norm_g, mla_kv_norm_g, mla_w_uq, mla_w_ukv, w_branch, w_o, ln2_g,
              w_mlp1, w_mlp2, w_pe, w_pg, final_g):
    Bn, S, D = x.shape
    lb_w = jax.nn.softmax(hgrn_lb.astype(jnp.float32), axis=0)
    lower_bounds = jnp.cumsum(lb_w, axis=0) - lb_w[0]
    h = x
    for l in range(DEPTH):
        hn = rms_norm(h, ln1_g[l])
        z = hn @ w_in[l]
        u_a, u_b, u_c, u_g = _split(z, [RWKV_W, HGRN_W, MLA_W, GATE_W])
        y_a = rwkv7_mixer(u_a, rwkv_mu[l], rwkv_w0[l], rwkv_w2[l], rwkv_a0[l], rwkv_a2[l],
                          rwkv_g2[l], rwkv_kk[l], rwkv_ka[l], rwkv_rk[l], rwkv_gn_w[l], rwkv_gn_b[l])
        y_b = hgrn2_mixer(u_b, lower_bounds[l], hgrn_norm_g[l])
        y_c = mla_mixer(u_c, positions, mla_q_norm_g[l], mla_kv_norm_g[l], mla_w_uq[l], mla_w_ukv[l])
        gates = jax.nn.sigmoid(u_g).reshape(Bn, S, N_BRANCH, D)
        mixed = (gates[:, :, 0] * (y_a @ w_branch[l, 0])
                 + gates[:, :, 1] * (y_b @ w_branch[l, 1])
                 + gates[:, :, 2] * (y_c @ w_branch[l, 2]))
        h = h + mixed @ w_o[l]
        hn = rms_norm(h, ln2_g[l])
        h = h + jnp.square(jax.nn.relu(hn @ w_mlp1[l])) @ w_mlp2[l]
        h = h + jax.nn.sigmoid(h @ w_pg[l]) * (p[l] @ w_pe[l])
    return rms_norm(h, final_g)
```

```python
import numpy as np
from contextlib import ExitStack
import concourse.bass as bass
import concourse.mybir as mybir
from concourse.bass_utils import run_bass_kernel_spmd

F32 = mybir.dt.float32
BF16 = mybir.dt.bfloat16
I32 = mybir.dt.int32
AF = mybir.ActivationFunctionType
ALU = mybir.AluOpType
AX = mybir.AxisListType

D_MODEL = 2048
BW = 1024
LORA = 64
LORA_G = 160
RWKV_W = 3 * BW + 4 * LORA + LORA_G
HGRN_W = 5 * BW
Q_LORA = 768
KV_LORA = 512
ROPE = 64
MLA_W = Q_LORA + KV_LORA + ROPE
GATE_W = 3 * D_MODEL
IN_W = RWKV_W + HGRN_W + MLA_W + GATE_W
D_FF = 8192
PLE = 256
DECAY_SCALE = 0.606531
GN_EPS = 64e-5
NORM_EPS = 1e-6
F_TINY = 1e-30

DMA_K = 6


class Prog:
    CE = ("pe", "act", "dve", "pool")
    QS = ("sp", "pool")

    def __init__(self, nc, es):
        self.nc = nc
        self.es = es
        self.ops = {e: [] for e in ("pe", "act", "dve", "pool", "sp")}
        self.cnt = {e: 0 for e in self.CE}
        self.waited = {e: {} for e in self.ops}
        self.dma_n = {q: 0 for q in self.QS}
        self.bufs = {}
        self.sems = {}
        for e in self.CE:
            self.sems[e] = es.enter_context(nc.semaphore("s_" + e))
        for q in self.QS:
            for k in range(DMA_K):
                self.sems[(q, k)] = es.enter_context(nc.semaphore("d_%s%d" % (q, k)))
        self.n_ops = 0

    def _collect(self, reads, writes):
        deps = {}
        def add(ev):
            if ev is None:
                return
            sk, v = ev
            if deps.get(sk, 0) < v:
                deps[sk] = v
        for b in reads:
            st = self.bufs.get(b)
            if st is not None:
                add(st[0])
        for b in writes:
            st = self.bufs.get(b)
            if st is not None:
                add(st[0])
                for sk, v in st[1].items():
                    add((sk, v))
        return deps

    def _emit_waits(self, eng, deps):
        w = self.waited[eng]
        for sk, v in deps.items():
            if sk == eng and eng == "pe":
                continue
            if w.get(sk, 0) < v:
                w[sk] = v
                self.ops[eng].append(("wait", sk, v))

    def _update(self, ev, reads, writes):
        sk, v = ev
        for b in reads:
            st = self.bufs.get(b)
            if st is None:
                st = self.bufs[b] = [None, {}]
            if st[1].get(sk, 0) < v:
                st[1][sk] = v
        for b in writes:
            self.bufs[b] = [ev, {}]

    def op(self, eng, fn, reads=(), writes=()):
        deps = self._collect(reads, writes)
        self._emit_waits(eng, deps)
        self.cnt[eng] += 1
        ev = (eng, self.cnt[eng])
        self.ops[eng].append(("op", fn))
        self._update(ev, reads, writes)
        self.n_ops += 1

    def dma(self, q, out, in_, reads=(), writes=()):
        deps = self._collect(reads, writes)
        i = self.dma_n[q]
        self.dma_n[q] += 1
        slot = i % DMA_K
        val = 16 * (i // DMA_K + 1)
        if val > 16:
            deps[(q, slot)] = max(deps.get((q, slot), 0), val - 16)
        self._emit_waits(q, deps)
        ev = ((q, slot), val)
        self.ops[q].append(("dma", out, in_, (q, slot)))
        self._update(ev, reads, writes)
        self.n_ops += 1

    def barrier(self):
        deps = {}
        for e in self.CE:
            if self.cnt[e]:
                deps[e] = self.cnt[e]
        for q in self.QS:
            n = self.dma_n[q]
            for k in range(DMA_K):
                if n > k:
                    last = ((n - 1 - k) // DMA_K) * DMA_K + k
                    deps[(q, k)] = 16 * (last // DMA_K + 1)
        for e in self.ops:
            d = {sk: v for sk, v in deps.items() if not (sk == e)}
            self._emit_waits(e, d)

    def emit(self):
        nc = self.nc
        self.barrier()
        sems = self.sems
        with nc.Block() as blk:
            def replay(eng_name, e):
                for item in self.ops[eng_name]:
                    if item[0] == "wait":
                        e.wait_ge(sems[item[1]], item[2])
                    elif item[0] == "op":
                        item[1](e).then_inc(sems[eng_name], 1)
                    else:
                        e.dma_start(out=item[1], in_=item[2], allow_slow_non_contiguous=True).then_inc(sems[item[3]], 16)

            @blk.tensor
            def _(e):
                replay("pe", e)

            @blk.scalar
            def _(e):
                replay("act", e)

            @blk.vector
            def _(e):
                replay("dve", e)

            @blk.gpsimd
            def _(e):
                replay("pool", e)

            @blk.sync
            def _(e):
                replay("sp", e)


class V:
    __slots__ = ("ap", "key")

    def __init__(self, ap, key):
        self.ap = ap
        self.key = key

    def bc(self, shape):
        return V(self.ap.broadcast_to(list(shape)), self.key)

    def __getitem__(self, idx):
        return V(self.ap[idx], self.key)

    def re(self, s, **kw):
        return V(self.ap.rearrange(s, **kw), self.key)


class T:
    def __init__(self, h, key):
        self.h = h
        self.key = key

    def __getitem__(self, idx):
        return V(self.h[idx], self.key)

    def k(self, sub):
        return T(self.h, (self.key, sub))


def _dsize(dt):
    return 2 if dt == BF16 else 4


class K(Prog):
    def __init__(self, nc, es):
        super().__init__(nc, es)
        self.sb_off = 16640
        self.sb_base = 16640
        self.uid = 0
        self.psb = [T(es.enter_context(nc.psum_tensor("psb%d" % i, [128, 512], F32)), "psb%d" % i)
                    for i in range(8)]
        self.ps_i = 0
        self.ps_pool = list(range(8))

    def ps(self):
        t = self.psb[self.ps_pool[self.ps_i % len(self.ps_pool)]]
        self.ps_i += 1
        return t

    def sb(self, name, shape, dt):
        self.uid += 1
        n = 1
        for s in shape[1:]:
            n *= s
        nbytes = (n * _dsize(dt) + 63) // 64 * 64
        off = self.sb_off
        self.sb_off += nbytes
        assert self.sb_off <= 229000, ("SBUF overflow", name, self.sb_off)
        nm = "%s_%d" % (name, self.uid)
        h = self.nc.alloc_sbuf_tensor_at(nm, list(shape), dt, offset=off)
        return T(h, nm)

    def persist(self):
        self.sb_base = self.sb_off

    def phase_end(self):
        self.barrier()
        self.sb_off = self.sb_base

    def dram(self, name, shape, dt, kind="Internal"):
        h = self.nc.dram_tensor(name, list(shape), dt, kind=kind)
        return T(h.ap(), name)

    def mm(self, out, lhsT, rhs, start=True, stop=True):
        o, l, r = out.ap, lhsT.ap, rhs.ap
        self.op("pe", lambda e: e.matmul(o, l, r, start=start, stop=stop),
                reads=[lhsT.key, rhs.key], writes=[out.key])

    def act(self, out, in_, func, bias=None, scale=None, eng="act"):
        o, i = out.ap, in_.ap
        kw = {}
        rd = [in_.key]
        if bias is not None:
            if isinstance(bias, V):
                kw["bias"] = bias.ap
                rd.append(bias.key)
            else:
                kw["bias"] = bias
        if scale is not None:
            if isinstance(scale, V):
                kw["scale"] = scale.ap
                rd.append(scale.key)
            else:
                kw["scale"] = scale
        self.op("act", lambda e: e.activation(out=o, in_=i, func=func, **kw),
                reads=rd, writes=[out.key])

    def tt(self, out, in0, in1, op, eng="dve"):
        o, a, b = out.ap, in0.ap, in1.ap
        self.op(eng, lambda e: e.tensor_tensor(o, a, b, op),
                reads=[in0.key, in1.key], writes=[out.key])

    def ts(self, out, in0, s1, op0, s2=None, op1=None, eng="dve"):
        o, a = out.ap, in0.ap
        rd = [in0.key]
        if isinstance(s1, V):
            rd.append(s1.key)
            s1 = s1.ap
        if isinstance(s2, V):
            rd.append(s2.key)
            s2 = s2.ap
        if op1 is None:
            self.op(eng, lambda e: e.tensor_scalar(o, a, s1, None, op0), reads=rd, writes=[out.key])
        else:
            self.op(eng, lambda e: e.tensor_scalar(o, a, s1, s2, op0, op1), reads=rd, writes=[out.key])

    def stt(self, out, in0, scalar, in1, op0, op1, eng="dve"):
        o, a, b = out.ap, in0.ap, in1.ap
        rd = [in0.key, in1.key]
        if isinstance(scalar, V):
            rd.append(scalar.key)
            scalar = scalar.ap
        self.op(eng, lambda e: e.scalar_tensor_tensor(o, a, scalar, b, op0, op1),
                reads=rd, writes=[out.key])

    def cp(self, out, in_, eng="dve"):
        o, i = out.ap, in_.ap
        if eng == "act":
            self.op("act", lambda e: e.activation(out=o, in_=i, func=AF.Copy),
                    reads=[in_.key], writes=[out.key])
        else:
            self.op(eng, lambda e: e.tensor_copy(o, i), reads=[in_.key], writes=[out.key])

    def recip(self, out, in_):
        o, i = out.ap, in_.ap
        self.op("dve", lambda e: e.reciprocal(o, i), reads=[in_.key], writes=[out.key])

    def memset(self, out, val, eng="dve"):
        o = out.ap
        self.op(eng, lambda e: e.memset(o, val), writes=[out.key])

    def ld(self, out, in_, q="sp"):
        self.dma(q, out.ap, in_.ap, reads=[in_.key], writes=[out.key])


def vk(v, sub):
    return V(v.ap, (v.key, sub))


W_SHAPES = {
    "ln1_g": (D_MODEL,), "w_in": (D_MODEL, IN_W), "rwkv_mu": (2, RWKV_W), "rwkv_w0": (2, BW),
    "rwkv_w2": (2, LORA, BW), "rwkv_a0": (2, BW), "rwkv_a2": (2, LORA, BW), "rwkv_g2": (LORA_G, BW),
    "rwkv_kk": (BW,), "rwkv_ka": (BW,), "rwkv_rk": (16, 64), "rwkv_gn_w": (BW,), "rwkv_gn_b": (BW,),
    "hgrn_lb": (BW,), "hgrn_norm_g": (128,), "mla_q_norm_g": (Q_LORA,), "mla_kv_norm_g": (KV_LORA,),
    "mla_w_uq": (Q_LORA, 8 * 192), "mla_w_ukv": (KV_LORA, 8 * 256), "w_branch": (3, BW, D_MODEL),
    "w_o": (D_MODEL, D_MODEL), "ln2_g": (D_MODEL,), "w_mlp1": (D_MODEL, D_FF), "w_mlp2": (D_FF, D_MODEL),
    "w_pe": (PLE, D_MODEL), "w_pg": (D_MODEL, D_MODEL),
}


def build(S, L, dbg=()):
    import os as _os
    stop = _os.environ.get("MK_STOP", "")
    nc = bass.Bass("TRN2", target_bir_lowering=False)
    es = ExitStack()
    P = K(nc, es)
    NT = S // 512
    NG = S // 128
    KC = D_MODEL // 128

    x_in = P.dram("x", [S, D_MODEL], F32, "ExternalInput")
    p_in = P.dram("p", [L, S, PLE], F32, "ExternalInput")
    pos_in = P.dram("positions", [1, S], I32, "ExternalInput")
    Wd = {}
    for nm, shp in W_SHAPES.items():
        Wd[nm] = P.dram(nm, [L] + list(shp), F32, "ExternalInput")
    fin_g = P.dram("final_g", [D_MODEL], F32, "ExternalInput")
    cst_in = P.dram("consts", [128, CONST_W], F32, "ExternalInput")
    out_d = P.dram("out", [S, D_MODEL], F32, "ExternalOutput")
    dbg_out = {}

    hT = P.dram("hT", [D_MODEL, S], F32)
    zT = P.dram("zT", [IN_W, S], F32)
    yT = [P.dram("yT%d" % n, [BW, S], BF16) for n in range(3)]

    def wscr(name, Kdim, M):
        nm_ = (M + 127) // 128
        return P.dram("ws_" + name, [nm_, 128, Kdim // 128, 128], BF16)

    WS = {
        "w_in": wscr("w_in", D_MODEL, IN_W), "w_b0": wscr("w_b0", BW, D_MODEL),
        "w_b1": wscr("w_b1", BW, D_MODEL), "w_b2": wscr("w_b2", BW, D_MODEL),
        "w_o": wscr("w_o", D_MODEL, D_MODEL), "w_mlp1": wscr("w_mlp1", D_MODEL, D_FF),
        "w_mlp2": wscr("w_mlp2", D_FF, D_MODEL), "w_pe": wscr("w_pe", PLE, D_MODEL),
        "w_pg": wscr("w_pg", D_MODEL, D_MODEL),
    }

    cst = P.sb("cst", [128, CONST_W], F32)
    P.ld(cst[:, :], cst_in[:, :])
    cstb = P.sb("cstb", [128, CONST_W], BF16)
    P.cp(cstb[:, :], cst[:, :])
    ident_f = cst[:, C_ID:C_ID + 128]
    ident_b = cstb[:, C_ID:C_ID + 128]
    ones_b = cstb[:, C_ONE:C_ONE + 128]
    P.persist()
    rr = [0]

    def evac_eng():
        rr[0] += 1
        return "act" if rr[0] % 2 else "dve"

    def precast(src, Kdim, M, dst):
        kc_all = Kdim // 128
        nm_ = (M + 127) // 128
        stg, stb = pcb["f"], pcb["b"]
        it = 0
        for mi in range(nm_):
            m0 = mi * 128
            msz = min(128, M - m0)
            for k0 in range(0, kc_all, 16):
                kc = min(16, kc_all - k0)
                sf, sbb = stg[it % 3], stb[it % 3]
                it += 1
                srcv = V(src.ap[k0 * 128:(k0 + kc) * 128, m0:m0 + msz].rearrange("(c p) m -> p c m", p=128), src.key)
                P.ld(sf[:, 0:kc, 0:msz], srcv)
                P.cp(sbb[:, 0:kc, 0:msz], sf[:, 0:kc, 0:msz], eng=evac_eng())
                P.ld(vk(dst[mi, :, k0:k0 + kc, 0:msz], (mi, k0)), sbb[:, 0:kc, 0:msz], q="pool")

    pcb = {}

    def precast_layer(l):
        pcb["f"] = [P.sb("pc_f", [128, 16, 128], F32) for _ in range(3)]
        pcb["b"] = [P.sb("pc_b", [128, 16, 128], BF16) for _ in range(3)]
        precast(Wd["w_in"][l], D_MODEL, IN_W, WS["w_in"])
        for n in range(3):
            precast(Wd["w_branch"][l, n], BW, D_MODEL, WS["w_b%d" % n])
        precast(Wd["w_o"][l], D_MODEL, D_MODEL, WS["w_o"])
        precast(Wd["w_mlp1"][l], D_MODEL, D_FF, WS["w_mlp1"])
        precast(Wd["w_mlp2"][l], D_FF, D_MODEL, WS["w_mlp2"])
        precast(Wd["w_pe"][l], PLE, D_MODEL, WS["w_pe"])
        precast(Wd["w_pg"][l], D_MODEL, D_MODEL, WS["w_pg"])
        P.phase_end()

    def linear(xT, kc, wt, mtiles, ntiles, evac, wbufs):
        for j, (mi, msz) in enumerate(mtiles):
            wb = wbufs[j % len(wbufs)]
            P.ld(wb[:, 0:kc, :], wt[mi, :, :, :])
            for (n0, nsz) in ntiles:
                ps = P.ps()
                for c in range(kc):
                    P.mm(ps[0:msz, 0:nsz], wb[:, c, 0:msz], xT[:, c, n0:n0 + nsz],
                         start=(c == 0), stop=(c == kc - 1))
                evac(mi, msz, n0, nsz, ps)

    def rmsnorm_tile(src, kc, nsz, g_sb, dst, dim, sq_scr, rstd_scr):
        ps = P.ps()
        for c in range(kc):
            P.act(sq_scr[:, c, 0:nsz], src[:, c, 0:nsz], AF.Square)
        for c in range(kc):
            P.mm(ps[:, 0:nsz], ones_b, sq_scr[:, c, 0:nsz], start=(c == 0), stop=(c == kc - 1))
        P.act(rstd_scr[:, 0:nsz], ps[:, 0:nsz], AF.Sqrt, bias=cst[:, C_EPS:C_EPS + 1], scale=1.0 / dim)
        P.recip(rstd_scr[:, 0:nsz], rstd_scr[:, 0:nsz])
        for c in range(kc):
            P.stt(dst(c), src[:, c, 0:nsz], g_sb[:, c:c + 1], rstd_scr[:, 0:nsz], ALU.mult, ALU.mult,
                  eng="dve")

    def load_vec(dst, src_v, kc):
        P.ld(dst[:, 0:kc], V(src_v.ap.rearrange("(c p) -> p c", p=128), src_v.key))

    def phase_in():
        xs = [P.sb("xin", [128, D_MODEL], F32) for _ in range(2)]
        ho = [P.sb("hout", [128, 512], F32) for _ in range(4)]
        it = 0
        for g in range(NG):
            xt = xs[g % 2]
            P.ld(xt[:, :], x_in[g * 128:(g + 1) * 128, :])
            for c4 in range(KC // 4):
                ps = P.ps()
                for j in range(4):
                    c = c4 * 4 + j
                    P.mm(ps[:, j * 128:(j + 1) * 128], xt[:, c * 128:(c + 1) * 128], ident_f)
                o = ho[it % 4]
                it += 1
                P.cp(o[:, :], ps[:, :], eng=evac_eng())
                dstv = V(hT.h[c4 * 512:(c4 + 1) * 512, g * 128:(g + 1) * 128].rearrange("(j p) t -> p j t", p=128),
                         (hT.key, g, c4))
                P.ld(dstv, o[:, :].re("p (j t) -> p j t", j=4), q="pool")
        P.phase_end()

    def phase_A(l):
        g_sb = P.sb("ln1g", [128, KC], F32)
        load_vec(g_sb, Wd["ln1_g"][l], KC)
        ST = min(S, 2048)
        hn = P.sb("hn", [128, KC, ST], BF16)
        hb = P.sb("hb", [128, KC, 512], F32)
        sq = P.sb("sq", [128, KC, 512], BF16)
        rstd = P.sb("rstd", [128, 512], F32)
        wbufs = [P.sb("wA", [128, KC, 128], BF16) for _ in range(2)]
        stg = [P.sb("zst", [128, 512], F32) for _ in range(4)]
        cnt = [0]
        for s0 in range(0, S, ST):
            for n0 in range(0, ST, 512):
                P.ld(hb[:, :, :], V(hT.h[:, s0 + n0:s0 + n0 + 512].rearrange("(c p) t -> p c t", p=128), hT.key))
                rmsnorm_tile(hb, KC, 512, g_sb, lambda c, n0=n0: hn[:, c, n0:n0 + 512], D_MODEL, sq, rstd)

            def evac(mi, msz, n0, nsz, ps, s0=s0):
                o = stg[cnt[0] % 4]
                cnt[0] += 1
                P.cp(o[0:msz, 0:nsz], ps[0:msz, 0:nsz], eng=evac_eng())
                P.ld(vk(zT[mi * 128:mi * 128 + msz, s0 + n0:s0 + n0 + nsz], (mi, s0 + n0)), o[0:msz, 0:nsz], q="pool")

            mt = [(mi, min(128, IN_W - mi * 128)) for mi in range((IN_W + 127) // 128)]
            linear(hn, KC, WS["w_in"], mt, [(n0, 512) for n0 in range(0, ST, 512)], evac, wbufs)
        P.phase_end()

    yfT = P.dram("yfT", [BW, S], F32)
    H = 16

    def phase_R(l):
        def hv(name, src_v):
            t = P.sb(name, [64, H], F32)
            P.ld(t[:, :], V(src_v.ap.rearrange("(h c) -> c h", c=64), src_v.key))
            return t
        kkp = hv("kkp", Wd["rwkv_kk"][l])
        kap = hv("kap", Wd["rwkv_ka"][l])
        gnw = hv("gnw", Wd["rwkv_gn_w"][l])
        gnb = hv("gnb", Wd["rwkv_gn_b"][l])
        rkp = P.sb("rkp", [64, H], F32)
        P.ld(rkp[:, :], V(Wd["rwkv_rk"][l].ap.rearrange("h c -> c h"), Wd["rwkv_rk"].key))
        omka = P.sb("omka", [64, H], F32)
        P.ts(omka[:, :], kap[:, :], -1.0, ALU.mult, 1.0, ALU.add)
        tmka = P.sb("tmka", [64, H], F32)
        P.ts(tmka[:, :], kap[:, :], -2.0, ALU.mult, 2.0, ALU.add)
        mu = Wd["rwkv_mu"]
        def mut(name, parts, nb, lo, hi, which):
            t = P.sb(name, [parts, nb], F32)
            P.ld(t[:, :], V(mu.h[l, which, lo:hi].rearrange("(j c) -> c j", c=parts), mu.key))
            return t
        m0_rkv = mut("m0rkv", 64, 48, 0, 3072, 0)
        m1_rkv = mut("m1rkv", 64, 48, 0, 3072, 1)
        m0_lo = mut("m0lo", 64, 4, 3072, 3328, 0)
        m1_lo = mut("m1lo", 64, 4, 3072, 3328, 1)
        m0_g = mut("m0g", 80, 2, 3328, 3488, 0)
        m1_g = mut("m1g", 80, 2, 3328, 3488, 1)

        def c0of(name, a, b, parts, nb):
            t = P.sb(name, [parts, nb], F32)
            P.tt(t[:, :], a[:, :], b[:, :], ALU.add)
            P.ts(t[:, :], t[:, :], -1.0, ALU.mult, 1.0, ALU.add)
            return t
        c0_rkv = c0of("c0rkv", m0_rkv, m1_rkv, 64, 48)
        c0_lo = c0of("c0lo", m0_lo, m1_lo, 64, 4)
        c0_g = c0of("c0g", m0_g, m1_g, 80, 2)

        w2a = [P.sb("w2a%d" % d, [66, BW], BF16) for d in range(2)]
        a2a = [P.sb("a2a%d" % d, [66, BW], BF16) for d in range(2)]
        g2b = P.sb("g2b", [80, 2, BW], BF16)
        off0 = P.sb_off
        st = P.sb("augst", [64, BW], F32)
        br = P.sb("augbr", [1, BW], F32)
        hif = P.sb("aughif", [1, BW], F32)
        hi = P.sb("aughi", [1, BW], BF16)
        lo = P.sb("auglo", [1, BW], BF16)

        def aug(t, w_v, b_v):
            P.ld(st[:, :], w_v)
            P.cp(t[0:64, :], st[:, :])
            P.ld(br[:, :], V(b_v.ap.rearrange("(o n) -> o n", o=1), b_v.key))
            P.cp(hi[:, :], br[:, :])
            P.cp(hif[:, :], hi[:, :])
            P.tt(hif[:, :], br[:, :], hif[:, :], ALU.subtract)
            P.cp(lo[:, :], hif[:, :])
            P.ld(t[64:65, :], hi[:, :])
            P.ld(t[65:66, :], lo[:, :])
        for d in range(2):
            aug(w2a[d], Wd["rwkv_w2"][l, d], Wd["rwkv_w0"][l, d])
            aug(a2a[d], Wd["rwkv_a2"][l, d], Wd["rwkv_a0"][l, d])
        g2f = P.sb("g2f", [80, 2, BW], F32)
        P.ld(g2f[:, :, :], V(Wd["rwkv_g2"][l].ap.rearrange("(j p) n -> p j n", p=80), Wd["rwkv_g2"].key))
        P.cp(g2b[:, :, :], g2f[:, :, :])
        P.barrier()
        P.sb_off = off0

        zin = P.sb("zin", [64, 16, 130], F32)
        zlo = P.sb("zlo", [64, 4, 130], F32)
        zg = P.sb("zg", [80, 2, 130], F32)
        u = P.sb("u", [64, 48, 128], F32)
        tmp = P.sb("tmp", [64, 16, 128], F32)
        ulo = P.sb("ulo", [64, 4, 128], F32)
        tlo = P.sb("tlo", [64, 4, 128], F32)
        ug = P.sb("ug", [80, 2, 128], F32)
        tg = P.sb("tg", [80, 2, 128], F32)
        wda = P.sb("wda", [66, 128], BF16)
        ada = [P.sb("ada%d" % d, [66, 128], BF16) for d in range(2)]
        sgd = P.sb("sgd", [80, 2, 128], BF16)
        P.memset(wda[64:66, :], 1.0)
        for d in range(2):
            P.memset(ada[d][64:66, :], 1.0)
        kk = P.sb("kk", [64, H, 128], F32)
        sqb = P.sb("sqb", [64, H, 128], BF16)
        s_tm = P.sb("s_tm", [128, BW], F32)
        E1 = P.sb("E1", [64, H, 128], F32)
        E2 = P.sb("E2", [64, H, 128], F32)
        av = [P.sb("a%d" % d, [64, H, 128], F32) for d in range(2)]
        kd = P.sb("kd", [64, H, 128], F32)
        be = P.sb("be", [64, H, 128], F32)
        KR = P.sb("KR", [64, H, 256], BF16)
        BT = P.sb("BT", [64, H, 128], BF16)
        KT = P.sb("KT", [64, H, 128], BF16)
        NBH = P.sb("NBH", [64, H, 128], BF16)
        KH = P.sb("KH", [64, H, 128], BF16)
        VB = P.sb("VB", [64, H, 128], BF16)
        TM = P.sb("TM", [128, H, 320], BF16)
        G1 = [P.sb("G1", [128, 512], BF16) for _ in range(4)]
        G2 = [P.sb("G2", [128, 128], BF16) for _ in range(4)]
        LV = [[P.sb("LV", [128, 384], BF16) for _ in range(2)] for _ in range(4)]
        UM = [P.sb("UM", [128, 128], BF16) for _ in range(4)]
        RpT = [P.sb("RpT", [64, 128], BF16) for _ in range(4)]
        PT = [P.sb("PT", [64, 64], BF16) for _ in range(4)]
        dG = [P.sb("dG", [64, 64], F32) for _ in range(4)]
        WT = [P.sb("WT", [128, 128], BF16) for _ in range(4)]
        Tst = P.sb("Tst", [64, H, 64], BF16)
        yv = kk
        yf = E1
        yab = P.sb("yab", [64, H, 128], BF16)

        def bc_t(t2, nb):
            return V(t2.ap.unsqueeze(2).broadcast_to([t2.ap.shape[0], nb, 128]), t2.key)

        def shift(dst, src, tmpb, c0, m0, m1, nb, do=0, co=0):
            dv_ = dst[:, do:do + nb, :]
            tv_ = tmpb[:, 0:nb, :]
            P.tt(dv_, src[:, 0:nb, 1:129], bc_t(c0[:, co:co + nb], nb), ALU.mult)
            P.tt(tv_, src[:, 0:nb, 0:128], bc_t(m0[:, co:co + nb], nb), ALU.mult)
            P.tt(dv_, dv_, tv_, ALU.add)
            P.tt(tv_, src[:, 0:nb, 2:130], bc_t(m1[:, co:co + nb], nb), ALU.mult)
            P.tt(dv_, dv_, tv_, ALU.add)

        def load_halo(dst, r0, r1, parts, t0):
            lo = max(t0 - 1, 0)
            hi = min(t0 + 129, S)
            if t0 == 0:
                P.memset(dst[:, :, 0:1], 0.0)
            if t0 + 129 > S:
                P.memset(dst[:, :, 129:130], 0.0)
            P.ld(dst[:, :, lo - (t0 - 1):hi - (t0 - 1)],
                 V(zT.h[r0:r1, lo:hi].rearrange("(j c) t -> c j t", c=parts), zT.key))

        def rwkv_post(l, g, t0):
            P.cp(yab[:, :, :], yv[:, :, :], eng="act")
            for b4 in range(4):
                sl = slice(b4 * 4, (b4 + 1) * 4)
                ps = P.ps()
                P.mm(ps[0:64, :], ones_b[0:64, 0:64], yab[:, sl, :].re("p a b -> p (a b)"))
                P.stt(be[:, sl, :].re("p a b -> p (a b)"), ps[0:64, :], -1.0 / 64, yv[:, sl, :].re("p a b -> p (a b)"),
                      ALU.mult, ALU.add)
            P.act(sqb[:, :, :], be[:, :, :], AF.Square)
            for b4 in range(4):
                sl = slice(b4 * 4, (b4 + 1) * 4)
                ps = P.ps()
                P.mm(ps[0:64, :], ones_b[0:64, 0:64], sqb[:, sl, :].re("p a b -> p (a b)"))
                P.act(kd[:, sl, :].re("p a b -> p (a b)"), ps[0:64, :], AF.Sqrt, bias=cst[0:64, C_GEPS:C_GEPS + 1], scale=1.0 / 64)
            P.recip(kd[:, :, :], kd[:, :, :])
            P.tt(be[:, :, :], be[:, :, :], kd[:, :, :], ALU.mult)
            P.tt(be[:, :, :], be[:, :, :], bc_t(gnw[:, :], H), ALU.mult)
            P.tt(be[:, :, :], be[:, :, :], bc_t(gnb[:, :], H), ALU.add)
            P.tt(E2[:, :, :], av[0][:, :, :], av[1][:, :, :], ALU.add)
            P.tt(E2[:, :, :], E2[:, :, :], bc_t(kap[:, :], H), ALU.mult)
            P.tt(E2[:, :, :], E2[:, :, :], bc_t(tmka[:, :], H), ALU.add)
            P.tt(E2[:, :, :], E2[:, :, :], u[:, 16:32, :], ALU.mult)
            P.tt(E2[:, :, :], E2[:, :, :], u[:, 0:16, :], ALU.mult)
            P.tt(sqb[:, :, :], E2[:, :, :], bc_t(rkp[:, :], H), ALU.mult)
            for b4 in range(4):
                sl = slice(b4 * 4, (b4 + 1) * 4)
                ps = P.ps()
                P.mm(ps[0:64, :], ones_b[0:64, 0:64], sqb[:, sl, :].re("p a b -> p (a b)"))
                P.tt(kd[:, sl, :].re("p a b -> p (a b)"), ps[0:64, :], u[:, 32 + b4 * 4:32 + (b4 + 1) * 4, :].re("p a b -> p (a b)"),
                     ALU.mult)
            P.tt(be[:, :, :], be[:, :, :], kd[:, :, :], ALU.add)
            shift(ug, zg, tg, c0_g, m0_g, m1_g, 2)
            P.act(sgd[:, :, :], ug[:, :, :], AF.Sigmoid)
            for b4 in range(4):
                ps = P.ps()
                for j in range(4):
                    h = b4 * 4 + j
                    for q in range(2):
                        P.mm(ps[0:64, j * 128:(j + 1) * 128], g2b[:, q, h * 64:(h + 1) * 64], sgd[:, q, :],
                             start=(q == 0), stop=(q == 1))
                sl = slice(b4 * 4, (b4 + 1) * 4)
                P.tt(yab[:, sl, :].re("p a b -> p (a b)"), ps[0:64, :], be[:, sl, :].re("p a b -> p (a b)"), ALU.mult)
            P.ld(vk(V(yT[0].h[:, t0:t0 + 128].rearrange("(h c) t -> c h t", c=64), yT[0].key), g), yab[:, :, :], q="pool")

        if stop == "Rs":
            P.phase_end()
            return
        for d in range(2):
            P.memset(Tst[:, :, :], 0.0)
            mk1 = cst[:, (C_RM1F if d == 0 else C_RM1B):(C_RM1F if d == 0 else C_RM1B) + 512]
            mk2 = cst[:, (C_RM2F if d == 0 else C_RM2B):(C_RM2F if d == 0 else C_RM2B) + 128]
            tri = cst[:, (C_TRIF if d == 0 else C_TRIB):(C_TRIF if d == 0 else C_TRIB) + 128]
            last = 127 if d == 0 else 0
            order = range(NG) if d == 0 else range(NG - 1, -1, -1)
            for g in order:
                t0 = g * 128
                for part in range(3):
                    load_halo(zin, part * 1024, (part + 1) * 1024, 64, t0)
                    shift(u, zin, tmp, c0_rkv, m0_rkv, m1_rkv, 16, do=part * 16, co=part * 16)
                load_halo(zlo, 3072, 3328, 64, t0)
                load_halo(zg, 3328, 3488, 80, t0)
                shift(ulo, zlo, tlo, c0_lo, m0_lo, m1_lo, 4)
                r_ = lambda sl=slice(None): u[:, 0:16, sl]
                P.tt(kk[:, :, :], u[:, 16:32, :], bc_t(kkp[:, :], H), ALU.mult)
                P.act(sqb[:, :, :], kk[:, :, :], AF.Square)
                for b4 in range(4):
                    ps = P.ps()
                    P.mm(ps[0:64, :], ones_b[0:64, 0:64], sqb[:, b4 * 4:(b4 + 1) * 4, :].re("p a b -> p (a b)"))
                    P.ts(tmp[:, b4 * 4:(b4 + 1) * 4, :].re("p a b -> p (a b)"), ps[0:64, :], 1e-24, ALU.max)
                P.act(tmp[:, :, :], tmp[:, :, :], AF.Sqrt)
                P.recip(tmp[:, :, :], tmp[:, :, :])
                P.tt(kk[:, :, :], kk[:, :, :], tmp[:, 0:16, :], ALU.mult)
                P.act(wda[0:64, :], ulo[:, d, :], AF.Tanh)
                for dd in (range(2) if d == 1 else [d]):
                    P.cp(ada[dd][0:64, :], ulo[:, 2 + dd, :])
                for hf in range(2):
                    ps = P.ps()
                    P.mm(ps[:, :], wda[:, :], w2a[d][:, hf * 512:(hf + 1) * 512])
                    P.act(s_tm[:, hf * 512:(hf + 1) * 512], ps[:, :], AF.Sigmoid)
                for dd in (range(2) if d == 1 else [d]):
                    for b4 in range(4):
                        ps = P.ps()
                        for j in range(4):
                            h = b4 * 4 + j
                            P.mm(ps[0:64, j * 128:(j + 1) * 128], a2a[dd][:, h * 64:(h + 1) * 64], ada[dd][:, :])
                        P.act(av[dd][:, b4 * 4:(b4 + 1) * 4, :].re("p a b -> p (a b)"), ps[0:64, :], AF.Sigmoid)
                for b4 in range(4):
                    ps = P.ps()
                    for j in range(4):
                        h = b4 * 4 + j
                        P.mm(ps[0:64, j * 128:(j + 1) * 128], s_tm[:, h * 64:(h + 1) * 64], tri)
                    P.act(E1[:, b4 * 4:(b4 + 1) * 4, :].re("p a b -> p (a b)"), ps[0:64, :], AF.Exp, scale=-DECAY_SCALE)
                    P.act(E2[:, b4 * 4:(b4 + 1) * 4, :].re("p a b -> p (a b)"), ps[0:64, :], AF.Exp, scale=DECAY_SCALE)
                a = av[d]
                P.tt(kd[:, :, :], a[:, :, :], bc_t(kap[:, :], H), ALU.mult)
                P.tt(kd[:, :, :], kd[:, :, :], bc_t(omka[:, :], H), ALU.add)
                P.tt(kd[:, :, :], kd[:, :, :], u[:, 16:32, :], ALU.mult)
                P.tt(be[:, :, :], kk[:, :, :], a[:, :, :], ALU.mult)
                if d == 0:
                    P.tt(KR[:, :, 1:128], kk[:, :, 1:128], E1[:, :, 0:127], ALU.mult)
                    P.cp(KR[:, :, 0:1], kk[:, :, 0:1])
                else:
                    P.tt(KR[:, :, 0:127], kk[:, :, 0:127], E1[:, :, 1:128], ALU.mult)
                    P.cp(KR[:, :, 127:128], kk[:, :, 127:128])
                P.tt(KR[:, :, 128:256], u[:, 0:16, :], E1[:, :, :], ALU.mult)
                P.tt(be[:, :, :], be[:, :, :], E2[:, :, :], ALU.mult)
                P.cp(BT[:, :, :], be[:, :, :], eng="act")
                P.tt(kd[:, :, :], kd[:, :, :], E2[:, :, :], ALU.mult)
                P.cp(KT[:, :, :], kd[:, :, :], eng="act")
                gC = V(E1.h[:, :, last:last + 1].broadcast_to([64, H, 128]), E1.key)
                P.stt(NBH[:, :, :], be[:, :, :], -1.0, gC, ALU.mult, ALU.mult)
                P.tt(KH[:, :, :], kd[:, :, :], gC, ALU.mult)
                P.cp(VB[:, :, :], u[:, 32:48, :], eng="act")
                if stop == "Re":
                    continue
                for (srcT, off) in ((KR, 64), (NBH, 128), (KH, 192), (VB, 256)):
                    for b8 in range(2):
                        ps = P.ps()
                        for j in range(8):
                            h = b8 * 8 + j
                            P.mm(ps[:, j * 64:(j + 1) * 64], srcT[:, h, 0:128], ident_b[0:64, 0:64])
                        P.cp(TM[:, b8 * 8:(b8 + 1) * 8, off:off + 64],
                             ps[:, :].re("p (a b) -> p a b", a=8), eng=evac_eng())
                if stop == "Rt":
                    continue
                for b4 in range(4):
                    hs = [b4 * 4 + j for j in range(4)]
                    psA = [P.ps() for _ in range(4)]
                    psB = P.ps()
                    for j, h in enumerate(hs):
                        P.mm(psA[j][:, 0:256], BT[:, h, :], KR[:, h, :])
                        P.mm(psA[j][:, 256:512], KT[:, h, :], KR[:, h, :])
                        P.mm(psB[:, j * 128:(j + 1) * 128], KR[:, h, 0:128], BT[:, h, :])
                    for j, h in enumerate(hs):
                        P.tt(G1[j][:, :], psA[j][:, :], mk1, ALU.mult)
                        P.tt(G2[j][:, :], psB[:, j * 128:(j + 1) * 128], mk2, ALU.mult, eng="dve")
                    if stop == "Rg":
                        continue
                    cur = [None] * 4
                    for lev in range(1, 8):
                        pl = [P.ps() for _ in range(4)]
                        for j, h in enumerate(hs):
                            if lev == 1:
                                Y, YT, Z = G1[j][:, 0:128], G2[j][:, :], ident_b
                            else:
                                c_ = cur[j]
                                Y, YT, Z = c_[:, 0:128], c_[:, 128:256], c_[:, 256:384]
                            if lev < 7:
                                P.mm(pl[j][:, 0:128], YT, Y)
                                P.mm(pl[j][:, 128:256], Y, YT)
                            P.mm(pl[j][:, 256:384], YT, Z, start=True, stop=False)
                            P.mm(pl[j][:, 256:384], ident_b, Z, start=False, stop=True)
                        for j, h in enumerate(hs):
                            nxt = LV[j][lev % 2]
                            if lev < 7:
                                P.cp(nxt[:, :], pl[j][:, 0:384], eng=evac_eng())
                            else:
                                P.cp(WT[j][:, :], pl[j][:, 256:384], eng=evac_eng())
                            cur[j] = nxt
                    if stop == "Ri":
                        continue
                    px = P.ps()
                    for j, h in enumerate(hs):
                        P.mm(px[:, j * 64:(j + 1) * 64], G1[j][:, 256:384], TM[:, h, 256:320])
                    for j, h in enumerate(hs):
                        P.cp(TM[:, h, 0:64], px[:, j * 64:(j + 1) * 64], eng=evac_eng())
                    if stop == "Rx1":
                        continue
                    pu = P.ps()
                    for j, h in enumerate(hs):
                        P.mm(pu[:, j * 128:(j + 1) * 128], WT[j][:, :], TM[:, h, 0:128])
                    for j, h in enumerate(hs):
                        P.cp(UM[j][:, :], pu[:, j * 128:(j + 1) * 128], eng="dve")
                    if stop == "Rx":
                        continue
                    pr = P.ps()
                    pp = P.ps()
                    for j, h in enumerate(hs):
                        P.mm(pr[0:64, j * 128:(j + 1) * 128], UM[j][:, 64:128], G1[j][:, 128:256], start=True, stop=False)
                        P.mm(pr[0:64, j * 128:(j + 1) * 128], ident_b[0:64, 0:64], KR[:, h, 128:256], start=False, stop=True)
                        P.mm(pp[0:64, j * 64:(j + 1) * 64], UM[j][:, 64:128], TM[:, h, 128:192])
                    for j, h in enumerate(hs):
                        P.cp(RpT[j][:, :], pr[0:64, j * 128:(j + 1) * 128], eng="act")
                        P.stt(dG[j][:, :], ident_f[0:64, 0:64], E1[:, h, last:last + 1], ident_f[0:64, 0:64], ALU.mult, ALU.mult)
                        P.tt(PT[j][:, :], pp[0:64, j * 64:(j + 1) * 64], dG[j][:, :], ALU.add)
                    if stop == "Ru":
                        continue
                    py = P.ps()
                    pt = P.ps()
                    for j, h in enumerate(hs):
                        yo = py[0:64, j * 128:(j + 1) * 128]
                        P.mm(yo, UM[j][:, 0:64], G1[j][:, 128:256], start=True, stop=False)
                        P.mm(yo, TM[:, h, 256:320], G1[j][:, 384:512], start=False, stop=False)
                        P.mm(yo, Tst[:, h, :], RpT[j][:, :], start=False, stop=True)
                        to = pt[0:64, j * 64:(j + 1) * 64]
                        P.mm(to, TM[:, h, 128:192], UM[j][:, 0:64], start=True, stop=False)
                        P.mm(to, TM[:, h, 192:256], TM[:, h, 256:320], start=False, stop=False)
                        P.mm(to, PT[j][:, :], Tst[:, h, :], start=False, stop=True)
                    P.cp(yv[:, b4 * 4:(b4 + 1) * 4, :].re("p a b -> p (a b)"), py[0:64, :], eng="act")
                    P.cp(Tst[:, b4 * 4:(b4 + 1) * 4, :].re("p a b -> p (a b)"), pt[0:64, 0:256], eng="dve")
                if stop in ("Rg", "Ri", "Ru", "Rx", "Rx1"):
                    continue
                if d == 0:
                    P.ld(vk(V(yfT.h[:, t0:t0 + 128].rearrange("(h c) t -> c h t", c=64), yfT.key), g), yv[:, :, :], q="pool")
                else:
                    P.ld(yf[:, :, :], V(yfT.h[:, t0:t0 + 128].rearrange("(h c) t -> c h t", c=64), yfT.key))
                    P.tt(yv[:, :, :], yv[:, :, :], yf[:, :, :], ALU.add)
                    rwkv_post(l, g, t0)
            P.barrier()
        P.phase_end()

    yhfT = P.dram("yhfT", [BW, S], F32)
    HB = RWKV_W

    def phase_H(l):
        HH = 8
        lball = P.sb("lball", [128, HH, L], F32)
        for i in range(L):
            P.ld(lball[:, :, i], V(Wd["hgrn_lb"].h[i].rearrange("(h c) -> c h", c=128), Wd["hgrn_lb"].key))
        P.act(lball[:, :, :], lball[:, :, :], AF.Exp)
        den = P.sb("lbden", [128, HH], F32)
        P.cp(den[:, :], lball[:, :, 0])
        for i in range(1, L):
            P.tt(den[:, :], den[:, :], lball[:, :, i], ALU.add)
        P.op("dve", (lambda e, o=den[:, :].ap: e.reciprocal(o, o)), reads=[den.key], writes=[den.key])
        lbv = P.sb("lbv", [128, HH], F32)
        P.memset(lbv[:, :], 0.0)
        for i in range(1, l + 1):
            P.tt(lbv[:, :], lbv[:, :], lball[:, :, i], ALU.add)
        P.tt(lbv[:, :], lbv[:, :], den[:, :], ALU.mult)
        oml = P.sb("oml", [128, HH], F32)
        P.ts(oml[:, :], lbv[:, :], -1.0, ALU.mult, 1.0, ALU.add)
        ng = P.sb("hng", [128, 1], F32)
        P.ld(ng[:, :], V(Wd["hgrn_norm_g"][l].ap.rearrange("(c o) -> c o", o=1), Wd["hgrn_norm_g"].key))

        def T3(name, dt):
            return P.sb(name, [128, HH, 128], dt)
        qf, zf, vf, gf = T3("qf", F32), T3("zf", F32), T3("vf", F32), T3("gf", F32)
        t1, t2 = T3("t1", F32), T3("t2", F32)
        lf_tm = T3("lf_tm", F32)
        E1, E2 = T3("hE1", F32), T3("hE2", F32)
        QT, KT, KH, VB = T3("hQT", BF16), T3("hKT", BF16), T3("hKH", BF16), T3("hVB", BF16)
        VT, KHT = T3("hVT", BF16), T3("hKHT", BF16)
        Vexp = P.sb("Vexp", [128, HH, 4, 128], BF16)
        Sc = [P.sb("Sc", [128, 128], BF16) for _ in range(2)]
        Tf = T3("hTf", F32)
        Tb = T3("hTb", BF16)
        yv = T3("hyv", F32)
        yf = T3("hyf", F32)
        sqb = T3("hsq", BF16)
        yob = T3("hyob", BF16)

        def bc_t(t2_, nb):
            return V(t2_.ap.unsqueeze(2).broadcast_to([128, nb, 128]), t2_.key)

        def rows(r0, t0):
            return V(zT.h[HB + r0:HB + r0 + 1024, t0:t0 + 128].rearrange("(h c) t -> c h t", c=128), zT.key)

        for d in range(2):
            P.memset(Tf[:, :, :], 0.0)
            P.memset(Tb[:, :, :], 0.0)
            hm = cst[:, (C_HMF if d == 0 else C_HMB):(C_HMF if d == 0 else C_HMB) + 128]
            order = range(NG) if d == 0 else range(NG - 1, -1, -1)
            corder = range(4) if d == 0 else range(3, -1, -1)
            lastoff = 31 if d == 0 else 0
            for g in order:
                t0 = g * 128
                P.ld(qf[:, :, :], rows(0, t0))
                P.ld(zf[:, :, :], rows(1024 * (1 + d), t0))
                P.ld(vf[:, :, :], rows(3072, t0))
                P.act(qf[:, :, :], qf[:, :, :], AF.Silu)
                P.act(t1[:, :, :], zf[:, :, :], AF.Sigmoid)
                P.tt(t1[:, :, :], t1[:, :, :], bc_t(oml[:, :], HH), ALU.mult)
                P.tt(t2[:, :, :], t1[:, :, :], bc_t(lbv[:, :], HH), ALU.add)
                P.ts(t2[:, :, :], t2[:, :, :], F_TINY, ALU.max)
                P.act(t2[:, :, :], t2[:, :, :], AF.Ln)
                P.stt(t1[:, :, :], t1[:, :, :], -1.0, bc_t(oml[:, :], HH), ALU.mult, ALU.add)
                for b in range(2):
                    ps = P.ps()
                    for j in range(4):
                        h = b * 4 + j
                        P.mm(ps[:, j * 128:(j + 1) * 128], t2[:, h, :], ident_f)
                    P.cp(lf_tm[:, b * 4:(b + 1) * 4, :].re("p a b -> p (a b)"), ps[:, :], eng=evac_eng())
                for b in range(2):
                    ps = P.ps()
                    for j in range(4):
                        h = b * 4 + j
                        P.mm(ps[:, j * 128:(j + 1) * 128], lf_tm[:, h, :], hm)
                    P.act(E1[:, b * 4:(b + 1) * 4, :].re("p a b -> p (a b)"), ps[:, :], AF.Exp)
                    P.act(E2[:, b * 4:(b + 1) * 4, :].re("p a b -> p (a b)"), ps[:, :], AF.Exp, scale=-1.0)
                P.tt(QT[:, :, :], qf[:, :, :], E1[:, :, :], ALU.mult)
                P.tt(t1[:, :, :], t1[:, :, :], E2[:, :, :], ALU.mult)
                P.cp(KT[:, :, :], t1[:, :, :], eng="act")
                gC = V(E1.h[:, :, :].rearrange("p h (c t) -> p h c t", t=32)[:, :, :, lastoff:lastoff + 1]
                       .broadcast_to([128, HH, 4, 32]), E1.key)
                P.tt(KH[:, :, :].re("p h (c t) -> p h c t", t=32), t1[:, :, :].re("p h (c t) -> p h c t", t=32), gC, ALU.mult)
                P.cp(VB[:, :, :], vf[:, :, :], eng="act")
                for (srcT, dstT) in ((VB, VT), (KH, KHT)):
                    for b in range(2):
                        ps = P.ps()
                        for j in range(4):
                            h = b * 4 + j
                            P.mm(ps[:, j * 128:(j + 1) * 128], srcT[:, h, :], ident_b)
                        P.cp(dstT[:, b * 4:(b + 1) * 4, :].re("p a b -> p (a b)"), ps[:, :], eng=evac_eng())
                cmv = cst[:, C_CM:C_CM + 4]
                P.tt(Vexp[:, :, :, :],
                     V(VT.h[:, :, :].unsqueeze(2).broadcast_to([128, HH, 4, 128]), VT.key),
                     V(cmv.ap.unsqueeze(1).unsqueeze(3).broadcast_to([128, HH, 4, 128]), cmv.key), ALU.mult)
                for h in range(HH):
                    ps = P.ps()
                    P.mm(ps[:, 0:128], KT[:, h, :], QT[:, h, :])
                    sc = Sc[h % 2]
                    P.tt(sc[:, :], ps[:, 0:128], hm, ALU.mult)
                    pq = P.ps()
                    P.mm(pq[:, :], KHT[:, h, :], Vexp[:, h, :, :].re("p c v -> p (c v)"))
                    py = P.ps()
                    for c4 in corder:
                        cs_ = slice(c4 * 32, (c4 + 1) * 32)
                        P.mm(py[:, cs_], VT[:, h, :], sc[:, cs_], start=True, stop=False)
                        P.mm(py[:, cs_], Tb[:, h, :], QT[:, h, cs_], start=False, stop=True)
                        tc_ = c4 * 32 + lastoff
                        P.stt(Tf[:, h, :], Tf[:, h, :], E1[:, h, tc_:tc_ + 1], pq[:, c4 * 128:(c4 + 1) * 128],
                              ALU.mult, ALU.add)
                        P.cp(Tb[:, h, :], Tf[:, h, :], eng="act")
                    P.cp(yv[:, h, :], py[:, 0:128], eng="act")
                dv = V(yhfT.h[:, t0:t0 + 128].rearrange("(h c) t -> c h t", c=128), yhfT.key)
                if d == 0:
                    P.ld(vk(dv, g), yv[:, :, :], q="pool")
                else:
                    P.ld(yf[:, :, :], dv)
                    P.ld(gf[:, :, :], rows(4096, t0))
                    P.tt(yv[:, :, :], yv[:, :, :], yf[:, :, :], ALU.add)
                    P.act(sqb[:, :, :], yv[:, :, :], AF.Square)
                    for b in range(2):
                        ps = P.ps()
                        P.mm(ps[:, :], ones_b, sqb[:, b * 4:(b + 1) * 4, :].re("p a b -> p (a b)"))
                        P.act(t2[:, b * 4:(b + 1) * 4, :].re("p a b -> p (a b)"), ps[:, :], AF.Sqrt, bias=cst[:, C_EPS:C_EPS + 1], scale=1.0 / 128)
                    P.recip(t2[:, :, :], t2[:, :, :])
                    P.stt(yv[:, :, :], yv[:, :, :], ng[:, 0:1], t2[:, :, :], ALU.mult, ALU.mult)
                    P.act(gf[:, :, :], gf[:, :, :], AF.Silu)
                    P.tt(yob[:, :, :], yv[:, :, :], gf[:, :, :], ALU.mult)
                    P.ld(vk(V(yT[1].h[:, t0:t0 + 128].rearrange("(h c) t -> c h t", c=128), yT[1].key), g), yob[:, :, :], q="pool")
            P.barrier()
        P.phase_end()

    MB = RWKV_W + HGRN_W
    qn_s = P.dram("qn_s", [8, 128, S], BF16)
    qr_s = P.dram("qr_s", [8, 64, S], BF16)
    kn_s = P.dram("kn_s", [8, 128, S], BF16)
    kr_s = P.dram("kr_s", [64, S], BF16)
    v_s = P.dram("v_s", [S, 1024], BF16)
    TWO_PI = 6.283185307179586
    PI = 3.141592653589793

    def phase_M1(l):
        wqf = P.sb("wqf", [128, 6, 1536], F32)
        P.ld(wqf[:, :, :], V(Wd["mla_w_uq"][l].ap.rearrange("(c p) m -> p c m", p=128), Wd["mla_w_uq"].key))
        wq = P.sb("wq", [128, 6, 1536], BF16)
        P.cp(wq[:, 0:3, :], wqf[:, 0:3, :])
        P.cp(wq[:, 3:6, :], wqf[:, 3:6, :], eng="act")
        wkf = P.sb("wkf", [128, 4, 2048], F32)
        P.ld(wkf[:, :, :], V(Wd["mla_w_ukv"][l].ap.rearrange("(c p) m -> p c m", p=128), Wd["mla_w_ukv"].key))
        wk = P.sb("wk", [128, 4, 2048], BF16)
        P.cp(wk[:, 0:2, :], wkf[:, 0:2, :])
        P.cp(wk[:, 2:4, :], wkf[:, 2:4, :], eng="act")
        qg = P.sb("qg", [128, 6], F32)
        load_vec(qg, Wd["mla_q_norm_g"][l], 6)
        kg = P.sb("kg", [128, 4], F32)
        load_vec(kg, Wd["mla_kv_norm_g"][l], 4)
        cq = P.sb("cq", [128, 6, 512], F32)
        ckv = P.sb("ckv", [128, 4, 512], F32)
        krf = P.sb("krf", [64, 512], F32)
        cqn = P.sb("cqn", [128, 6, 512], BF16)
        ckvn = P.sb("ckvn", [128, 4, 512], BF16)
        sq = P.sb("msq", [128, 6, 512], BF16)
        rstd = P.sb("mrstd", [128, 512], F32)
        posi = P.sb("posi", [64, 512], I32)
        ang = P.sb("ang", [64, 512], F32)
        am = P.sb("am", [64, 512], F32)
        cosv = P.sb("cosv", [64, 512], F32)
        sinv = P.sb("sinv", [64, 512], F32)
        xr = P.sb("xr", [64, 512], F32)
        r1 = P.sb("r1", [64, 512], F32)
        r2 = P.sb("r2", [64, 512], F32)
        ob = [P.sb("mob", [128, 512], BF16) for _ in range(4)]
        oi = [0]
        rot = cst[0:64, C_ROT:C_ROT + 64]
        invf = cst[0:64, C_IF:C_IF + 1]

        def rope(xsrc_ps, dst_bf):
            P.cp(xr[:, :], xsrc_ps, eng="act")
            ps = P.ps()
            P.mm(ps[0:64, :], rot, xr[:, :])
            P.tt(r1[:, :], xr[:, :], cosv[:, :], ALU.mult)
            P.tt(r2[:, :], ps[0:64, :], sinv[:, :], ALU.mult)
            P.tt(dst_bf, r1[:, :], r2[:, :], ALU.add)

        for n in range(NT):
            n0 = n * 512
            tsl = slice(n0, n0 + 512)
            P.ld(cq[:, :, :], V(zT.h[MB:MB + 768, tsl].rearrange("(c p) t -> p c t", p=128), zT.key))
            P.ld(ckv[:, :, :], V(zT.h[MB + 768:MB + 1280, tsl].rearrange("(c p) t -> p c t", p=128), zT.key))
            P.ld(krf[:, :], zT[MB + 1280:MB + 1344, tsl])
            rmsnorm_tile(cq, 6, 512, qg, lambda c: cqn[:, c, :], Q_LORA, sq, rstd)
            rmsnorm_tile(ckv, 4, 512, kg, lambda c: ckvn[:, c, :], KV_LORA, sq, rstd)
            P.ld(posi[:, :], V(pos_in.h[0:1, tsl].partition_broadcast(64), pos_in.key))
            P.cp(ang[:, :], posi[:, :])
            P.tt(ang[:, :], ang[:, :], invf.bc([64, 512]), ALU.mult)
            for (dstv, offs) in ((sinv, 0.0), (cosv, 0.25)):
                P.ts(am[:, :], ang[:, :], 1.0 / TWO_PI, ALU.mult, offs, ALU.add)
                P.cp(posi[:, :], am[:, :])
                P.cp(r1[:, :], posi[:, :])
                P.tt(am[:, :], am[:, :], r1[:, :], ALU.subtract)
                P.act(dstv[:, :], am[:, :], AF.Sin, scale=6.283185)
            o = ob[oi[0] % 4]; oi[0] += 1
            ps = P.ps()
            P.mm(ps[0:64, :], ident_f[0:64, 0:64], krf[:, :])
            rope(ps[0:64, :], o[0:64, :])
            P.ld(vk(kr_s[:, tsl], n), o[0:64, :], q="pool")
            for h in range(8):
                ps = P.ps()
                for c in range(6):
                    P.mm(ps[:, :], wq[:, c, 192 * h:192 * h + 128], cqn[:, c, :], start=(c == 0), stop=(c == 5))
                o = ob[oi[0] % 4]; oi[0] += 1
                P.cp(o[:, :], ps[:, :], eng=evac_eng())
                P.ld(vk(qn_s[h, :, tsl], (h, n)), o[:, :], q="pool")
                ps = P.ps()
                for c in range(6):
                    P.mm(ps[0:64, :], wq[:, c, 192 * h + 128:192 * h + 192], cqn[:, c, :], start=(c == 0), stop=(c == 5))
                o = ob[oi[0] % 4]; oi[0] += 1
                rope(ps[0:64, :], o[0:64, :])
                P.ld(vk(qr_s[h, :, tsl], (h, n)), o[0:64, :], q="pool")
                ps = P.ps()
                for c in range(4):
                    P.mm(ps[:, :], wk[:, c, 256 * h:256 * h + 128], ckvn[:, c, :], start=(c == 0), stop=(c == 3))
                o = ob[oi[0] % 4]; oi[0] += 1
                P.cp(o[:, :], ps[:, :], eng=evac_eng())
                P.ld(vk(kn_s[h, :, tsl], (h, n)), o[:, :], q="pool")
            for j in range(4):
                for hb_ in range(2):
                    ps = P.ps()
                    for c in range(4):
                        rv = V(wk.h[:, c, :].rearrange("p (h two v) -> p h two v", two=2, v=128)[:, hb_ * 4:(hb_ + 1) * 4, 1, :], wk.key)
                        P.mm(ps[:, :].re("p (h v) -> p h v", h=4), ckvn[:, c, j * 128:(j + 1) * 128], rv,
                             start=(c == 0), stop=(c == 3))
                    o = ob[oi[0] % 4]; oi[0] += 1
                    P.cp(o[:, :], ps[:, :], eng=evac_eng())
                    P.ld(vk(v_s[n0 + j * 128:n0 + (j + 1) * 128, hb_ * 512:(hb_ + 1) * 512], (n, j, hb_)), o[:, :], q="pool")
        P.phase_end()

    def phase_M2(l):
        SCALE = 192.0 ** -0.5
        kr = P.sb("kr", [64, S], BF16)
        P.ld(kr[:, :], kr_s[:, :])
        qn = P.sb("qn", [128, S], BF16)
        qr = P.sb("qr", [64, S], BF16)
        kn = P.sb("kn", [128, S], BF16)
        vh = P.sb("vh", [128, NG, 128], BF16)
        pt = [P.sb("pt", [128, 512], BF16) for _ in range(3)]
        rden = P.sb("rden", [128, 512], F32)
        oo = [P.sb("oo", [128, 512], BF16) for _ in range(2)]
        pti = 0
        P.ps_pool = [0, 1, 2, 3, 4, 5]
        po, pd = P.psb[6], P.psb[7]
        for h in range(8):
            P.ld(qn[:, :], qn_s[h, :, :])
            P.ld(qr[:, :], qr_s[h, :, :])
            P.ld(kn[:, :], kn_s[h, :, :])
            P.ld(vh[:, :, :], V(v_s.h[:, h * 128:(h + 1) * 128].rearrange("(g p) v -> p g v", p=128), v_s.key))
            for n in range(NT):
                tsl = slice(n * 512, (n + 1) * 512)
                for g in range(NG):
                    ks = slice(g * 128, (g + 1) * 128)
                    ps = P.ps()
                    P.mm(ps[:, :], kn[:, ks], qn[:, tsl], start=True, stop=False)
                    P.mm(ps[:, :], kr[:, ks], qr[:, tsl], start=False, stop=True)
                    p_ = pt[pti % 3]; pti += 1
                    P.act(p_[:, :], ps[:, :], AF.Exp, scale=SCALE)
                    P.mm(po[:, :], vh[:, g, :], p_[:, :], start=(g == 0), stop=(g == NG - 1))
                    P.mm(pd[:, :], ones_b, p_[:, :], start=(g == 0), stop=(g == NG - 1))
                P.op("dve", (lambda e, o=rden[:, :].ap, i=pd[:, :].ap: e.reciprocal(o, i)), reads=[pd.key], writes=[rden.key])
                o = oo[n % 2]
                P.tt(o[:, :], po[:, :], rden[:, :], ALU.mult)
                P.ld(vk(yT[2][h * 128:(h + 1) * 128, tsl], (h, n)), o[:, :], q="pool")
        P.ps_pool = list(range(8))
        P.phase_end()

    GB = RWKV_W + HGRN_W + MLA_W

    def phase_GF(l, last):
        g2 = P.sb("ln2g", [128, KC], F32)
        load_vec(g2, Wd["ln2_g"][l], KC)
        gfin = P.sb("fing", [128, KC], F32)
        load_vec(gfin, fin_g[:], KC)
        hb = P.sb("hb", [128, KC, 512], F32)
        hn2 = P.sb("hn2", [128, KC, 512], BF16)
        sq = P.sb("sq2", [128, KC, 512], BF16)
        rstd = P.sb("rstd2", [128, 512], F32)
        wbufs = [P.sb("wG", [128, 64, 128], BF16) for _ in range(2)]
        gz = [P.sb("gz", [128, 512], F32) for _ in range(3)]
        acc = P.sb("acc", [128, 512], F32)
        tmpf = [P.sb("tmpf", [128, 512], F32) for _ in range(2)]
        ptf = P.sb("ptf", [128, 4, PLE], F32)
        ptb = P.sb("ptb", [128, 2, 512], BF16)
        ot = P.sb("ot", [128, 512], F32)
        big0 = P.sb_off
        ys = [P.sb("ys%d" % n, [128, 8, 512], BF16) for n in range(3)]
        mixed = P.sb("mixed", [128, KC, 512], BF16)
        P.sb_off = big0
        h1 = P.sb("h1", [128, 64, 512], BF16)
        wi = [0]

        def wb():
            wi[0] += 1
            return wbufs[wi[0] % 2]

        for n in range(NT):
            tsl = slice(n * 512, (n + 1) * 512)
            P.ld(hb[:, :, :], V(hT.h[:, tsl].rearrange("(c p) t -> p c t", p=128), hT.key))
            for b in range(3):
                P.ld(ys[b][:, :, :], V(yT[b].h[:, tsl].rearrange("(c p) t -> p c t", p=128), yT[b].key))
            P.ld(ptf[:, :, :], V(p_in.h[l, tsl, :].rearrange("(j p) f -> p j f", p=128), p_in.key))
            for c in range(2):
                ps = P.ps()
                for j in range(4):
                    P.mm(ps[:, j * 128:(j + 1) * 128], ptf[:, j, c * 128:(c + 1) * 128], ident_f)
                P.cp(ptb[:, c, :], ps[:, :], eng=evac_eng())
            for m in range(KC):
                for b in range(3):
                    w = wb()
                    P.ld(w[:, 0:8, :], WS["w_b%d" % b][m, :, :, :])
                    P.ld(gz[b][:, :], zT[GB + b * 2048 + m * 128:GB + b * 2048 + (m + 1) * 128, tsl])
                    ps = P.ps()
                    for c in range(8):
                        P.mm(ps[:, :], w[:, c, :], ys[b][:, c, :], start=(c == 0), stop=(c == 7))
                    P.act(gz[b][:, :], gz[b][:, :], AF.Sigmoid)
                    if b == 0:
                        P.tt(acc[:, :], ps[:, :], gz[b][:, :], ALU.mult)
                    else:
                        t = tmpf[b % 2]
                        P.tt(t[:, :], ps[:, :], gz[b][:, :], ALU.mult)
                        if b == 1:
                            P.tt(acc[:, :], acc[:, :], t[:, :], ALU.add)
                        else:
                            P.tt(mixed[:, m, :], acc[:, :], t[:, :], ALU.add)
            for m in range(KC):
                w = wb()
                P.ld(w[:, 0:KC, :], WS["w_o"][m, :, :, :])
                ps = P.ps()
                for c in range(KC):
                    P.mm(ps[:, :], w[:, c, :], mixed[:, c, :], start=(c == 0), stop=(c == KC - 1))
                P.tt(hb[:, m, :], hb[:, m, :], ps[:, :], ALU.add)
            P.barrier()
            rmsnorm_tile(hb, KC, 512, g2, lambda c: hn2[:, c, :], D_MODEL, sq, rstd)
            for m in range(64):
                w = wb()
                P.ld(w[:, 0:KC, :], WS["w_mlp1"][m, :, :, :])
                ps = P.ps()
                for c in range(KC):
                    P.mm(ps[:, :], w[:, c, :], hn2[:, c, :], start=(c == 0), stop=(c == KC - 1))
                t = tmpf[m % 2]
                P.act(t[:, :], ps[:, :], AF.Relu)
                P.tt(h1[:, m, :], t[:, :], t[:, :], ALU.mult)
            for m in range(KC):
                w = wb()
                P.ld(w[:, :, :], WS["w_mlp2"][m, :, :, :])
                ps = P.ps()
                for c in range(64):
                    P.mm(ps[:, :], w[:, c, :], h1[:, c, :], start=(c == 0), stop=(c == 63))
                P.tt(hb[:, m, :], hb[:, m, :], ps[:, :], ALU.add)
            for c in range(KC):
                P.cp(hn2[:, c, :], hb[:, c, :], eng=evac_eng())
            for m in range(KC):
                w = wb()
                P.ld(w[:, 0:KC, :], WS["w_pg"][m, :, :, :])
                w2_ = wb()
                P.ld(w2_[:, 0:2, :], WS["w_pe"][m, :, :, :])
                ps = P.ps()
                for c in range(KC):
                    P.mm(ps[:, :], w[:, c, :], hn2[:, c, :], start=(c == 0), stop=(c == KC - 1))
                pe_ = P.ps()
                for c in range(2):
                    P.mm(pe_[:, :], w2_[:, c, :], ptb[:, c, :], start=(c == 0), stop=(c == 1))
                t = tmpf[m % 2]
                P.act(t[:, :], ps[:, :], AF.Sigmoid)
                P.tt(t[:, :], t[:, :], pe_[:, :], ALU.mult)
                P.tt(hb[:, m, :], hb[:, m, :], t[:, :], ALU.add)
            if not last:
                P.ld(vk(V(hT.h[:, tsl].rearrange("(c p) t -> p c t", p=128), hT.key), ("o", n)), hb[:, :, :], q="pool")
            else:
                ps = P.ps()
                for c in range(KC):
                    P.act(sq[:, c, :], hb[:, c, :], AF.Square)
                for c in range(KC):
                    P.mm(ps[:, :], ones_b, sq[:, c, :], start=(c == 0), stop=(c == KC - 1))
                P.act(rstd[:, :], ps[:, :], AF.Sqrt, bias=cst[:, C_EPS:C_EPS + 1], scale=1.0 / D_MODEL)
                P.recip(rstd[:, :], rstd[:, :])
                for c in range(KC):
                    P.stt(hb[:, c, :], hb[:, c, :], gfin[:, c:c + 1], rstd[:, :], ALU.mult, ALU.mult)
                for j in range(4):
                    for c4 in range(KC // 4):
                        ps = P.ps()
                        for q in range(4):
                            c = c4 * 4 + q
                            P.mm(ps[:, q * 128:(q + 1) * 128], hb[:, c, j * 128:(j + 1) * 128], ident_f)
                        P.cp(ot[:, :], ps[:, :], eng=evac_eng())
                        P.ld(vk(out_d[n * 512 + j * 128:n * 512 + (j + 1) * 128, c4 * 512:(c4 + 1) * 512], (n, j, c4)),
                             ot[:, :], q="pool")
            P.barrier()
        P.phase_end()

    phase_in()
    for l in range(L):
        if stop == "in":
            break
        precast_layer(l)
        phase_A(l)
        if stop == "A":
            break
        phase_R(l)
        if stop in ("R", "Rs", "Re", "Rt", "Rg", "Ri", "Ru", "Rx", "Rx1"):
            break
        phase_H(l)
        if stop == "H":
            break
        phase_M1(l)
        if stop == "M1":
            break
        phase_M2(l)
        if stop == "M2":
            break
        phase_GF(l, l == L - 1)
    P.emit()
    es.close()
    return nc, P


C_ID = 0
C_ONE = 128
C_RM1F = 256
C_RM1B = 768
C_RM2F = 1280
C_RM2B = 1408
C_TRIF = 1536
C_TRIB = 1664
C_HMF = 1792
C_HMB = 1920
C_CM = 2048
C_IF = 2052
C_PI = 2053
C_EPS = 2054
C_GEPS = 2055
C_ROT = 2056
CONST_W = 2128


def make_consts():
    c = np.zeros((128, CONST_W), np.float32)
    r = np.arange(128)[:, None]
    q = np.arange(128)[None, :]
    c[:, C_ID:C_ID + 128] = (r == q)
    c[:, C_ONE:C_ONE + 128] = 1.0
    su, iu = (r < q).astype(np.float32), (r <= q).astype(np.float32)
    sl, il = (r > q).astype(np.float32), (r >= q).astype(np.float32)
    c[:, C_RM1F:C_RM1F + 512] = np.concatenate([-su, -iu, su, iu], axis=1)
    c[:, C_RM1B:C_RM1B + 512] = np.concatenate([-sl, -il, sl, il], axis=1)
    c[:, C_RM2F:C_RM2F + 128] = -sl
    c[:, C_RM2B:C_RM2B + 128] = -su
    c[:, C_TRIF:C_TRIF + 128] = iu
    c[:, C_TRIB:C_TRIB + 128] = il
    same = ((r // 32) == (q // 32)).astype(np.float32)
    c[:, C_HMF:C_HMF + 128] = iu * same
    c[:, C_HMB:C_HMB + 128] = il * same
    c[:, C_CM:C_CM + 4] = ((np.arange(128)[:, None] // 32) == np.arange(4)[None, :])
    inv_freq = (1.0 / (np.float32(10000.0) ** (np.arange(0, 64, 2, dtype=np.float32) / np.float32(64)))).astype(np.float32)
    c[:, C_IF] = inv_freq[np.arange(128) % 32]
    c[:, C_PI] = np.float32(np.pi)
    c[:, C_EPS] = NORM_EPS
    c[:, C_GEPS] = GN_EPS
    R = np.zeros((64, 64), np.float32)
    for m in range(32):
        R[m, m + 32] = -1.0
        R[m + 32, m] = 1.0
    c[0:64, C_ROT:C_ROT + 64] = R.T
    return c


_CACHE = {}


def run(inputs, S, L, ncores):
    key = (S, L)
    if key not in _CACHE:
        _CACHE[key] = build(S, L)
    nc, P = _CACHE[key]
    consts = make_consts()
    in_maps = []
    for b in range(ncores):
        m = {"x": np.ascontiguousarray(inputs["x"][b]),
             "p": np.ascontiguousarray(inputs["p"][:, b]),
             "positions": np.ascontiguousarray(inputs["positions"][b:b + 1]).astype(np.int32),
             "final_g": np.ascontiguousarray(inputs["final_g"]),
             "consts": consts}
        for nm in W_SHAPES:
            m[nm] = np.ascontiguousarray(inputs[nm])
        in_maps.append(m)
    res = run_bass_kernel_spmd(nc, in_maps, core_ids=list(range(ncores)))
    return res


def kernel(**inputs):
    inputs = {k: np.asarray(v) for k, v in inputs.items()}
    B, S, _ = inputs["x"].shape
    L = inputs["w_in"].shape[0]
    res = run(inputs, S, L, B)
    out = np.stack([res.results[b]["out"] for b in range(B)], axis=0)
    return out.astype(np.float32)
```

```python
import numpy as np
from contextlib import ExitStack
import concourse.bass as bass
import concourse.mybir as mybir
from concourse.bass_utils import run_bass_kernel_spmd

F32 = mybir.dt.float32
BF16 = mybir.dt.bfloat16
I32 = mybir.dt.int32
AF = mybir.ActivationFunctionType
ALU = mybir.AluOpType
AX = mybir.AxisListType

D_MODEL = 2048
BW = 1024
LORA = 64
LORA_G = 160
RWKV_W = 3 * BW + 4 * LORA + LORA_G
HGRN_W = 5 * BW
Q_LORA = 768
KV_LORA = 512
ROPE = 64
MLA_W = Q_LORA + KV_LORA + ROPE
GATE_W = 3 * D_MODEL
IN_W = RWKV_W + HGRN_W + MLA_W + GATE_W
D_FF = 8192
PLE = 256
DECAY_SCALE = 0.606531
GN_EPS = 64e-5
NORM_EPS = 1e-6
F_TINY = 1e-30

DMA_K = 6


class Prog:
    CE = ("pe", "act", "dve", "pool")
    QS = ("sp", "pool")

    def __init__(self, nc, es):
        self.nc = nc
        self.es = es
        self.ops = {e: [] for e in ("pe", "act", "dve", "pool", "sp")}
        self.cnt = {e: 0 for e in self.CE}
        self.waited = {e: {} for e in self.ops}
        self.dma_n = {q: 0 for q in self.QS}
        self.bufs = {}
        self.sems = {}
        for e in self.CE:
            self.sems[e] = es.enter_context(nc.semaphore("s_" + e))
        for q in self.QS:
            for k in range(DMA_K):
                self.sems[(q, k)] = es.enter_context(nc.semaphore("d_%s%d" % (q, k)))
        self.n_ops = 0

    def _collect(self, reads, writes):
        deps = {}
        def add(ev):
            if ev is None:
                return
            sk, v = ev
            if deps.get(sk, 0) < v:
                deps[sk] = v
        for b in reads:
            st = self.bufs.get(b)
            if st is not None:
                add(st[0])
        for b in writes:
            st = self.bufs.get(b)
            if st is not None:
                add(st[0])
                for sk, v in st[1].items():
                    add((sk, v))
        return deps

    def _emit_waits(self, eng, deps):
        w = self.waited[eng]
        for sk, v in deps.items():
            if sk == eng and eng == "pe":
                continue
            if w.get(sk, 0) < v:
                w[sk] = v
                self.ops[eng].append(("wait", sk, v))

    def _update(self, ev, reads, writes):
        sk, v = ev
        for b in reads:
            st = self.bufs.get(b)
            if st is None:
                st = self.bufs[b] = [None, {}]
            if st[1].get(sk, 0) < v:
                st[1][sk] = v
        for b in writes:
            self.bufs[b] = [ev, {}]

    def op(self, eng, fn, reads=(), writes=()):
        deps = self._collect(reads, writes)
        self._emit_waits(eng, deps)
        self.cnt[eng] += 1
        ev = (eng, self.cnt[eng])
        self.ops[eng].append(("op", fn))
        self._update(ev, reads, writes)
        self.n_ops += 1

    def dma(self, q, out, in_, reads=(), writes=()):
        deps = self._collect(reads, writes)
        i = self.dma_n[q]
        self.dma_n[q] += 1
        slot = i % DMA_K
        val = 16 * (i // DMA_K + 1)
        if val > 16:
            deps[(q, slot)] = max(deps.get((q, slot), 0), val - 16)
        self._emit_waits(q, deps)
        ev = ((q, slot), val)
        self.ops[q].append(("dma", out, in_, (q, slot)))
        self._update(ev, reads, writes)
        self.n_ops += 1

    def barrier(self):
        deps = {}
        for e in self.CE:
            if self.cnt[e]:
                deps[e] = self.cnt[e]
        for q in self.QS:
            n = self.dma_n[q]
            for k in range(DMA_K):
                if n > k:
                    last = ((n - 1 - k) // DMA_K) * DMA_K + k
                    deps[(q, k)] = 16 * (last // DMA_K + 1)
        for e in self.ops:
            d = {sk: v for sk, v in deps.items() if not (sk == e)}
            self._emit_waits(e, d)

    def emit(self):
        nc = self.nc
        self.barrier()
        sems = self.sems
        with nc.Block() as blk:
            def replay(eng_name, e):
                for item in self.ops[eng_name]:
                    if item[0] == "wait":
                        e.wait_ge(sems[item[1]], item[2])
                    elif item[0] == "op":
                        item[1](e).then_inc(sems[eng_name], 1)
                    else:
                        e.dma_start(out=item[1], in_=item[2], allow_slow_non_contiguous=True).then_inc(sems[item[3]], 16)

            @blk.tensor
            def _(e):
                replay("pe", e)

            @blk.scalar
            def _(e):
                replay("act", e)

            @blk.vector
            def _(e):
                replay("dve", e)

            @blk.gpsimd
            def _(e):
                replay("pool", e)

            @blk.sync
            def _(e):
                replay("sp", e)


class V:
    __slots__ = ("ap", "key")

    def __init__(self, ap, key):
        self.ap = ap
        self.key = key

    def bc(self, shape):
        return V(self.ap.broadcast_to(list(shape)), self.key)

    def __getitem__(self, idx):
        return V(self.ap[idx], self.key)

    def re(self, s, **kw):
        return V(self.ap.rearrange(s, **kw), self.key)


class T:
    def __init__(self, h, key):
        self.h = h
        self.key = key

    def __getitem__(self, idx):
        return V(self.h[idx], self.key)

    def k(self, sub):
        return T(self.h, (self.key, sub))


def _dsize(dt):
    return 2 if dt == BF16 else 4


class K(Prog):
    def __init__(self, nc, es):
        super().__init__(nc, es)
        self.sb_off = 16640
        self.sb_base = 16640
        self.uid = 0
        self.psb = [T(es.enter_context(nc.psum_tensor("psb%d" % i, [128, 512], F32)), "psb%d" % i)
                    for i in range(8)]
        self.ps_i = 0
        self.ps_pool = list(range(8))

    def ps(self):
        t = self.psb[self.ps_pool[self.ps_i % len(self.ps_pool)]]
        self.ps_i += 1
        return t

    def sb(self, name, shape, dt):
        self.uid += 1
        n = 1
        for s in shape[1:]:
            n *= s
        nbytes = (n * _dsize(dt) + 63) // 64 * 64
        off = self.sb_off
        self.sb_off += nbytes
        assert self.sb_off <= 229000, ("SBUF overflow", name, self.sb_off)
        nm = "%s_%d" % (name, self.uid)
        h = self.nc.alloc_sbuf_tensor_at(nm, list(shape), dt, offset=off)
        return T(h, nm)

    def persist(self):
        self.sb_base = self.sb_off

    def phase_end(self):
        self.barrier()
        self.sb_off = self.sb_base

    def dram(self, name, shape, dt, kind="Internal"):
        h = self.nc.dram_tensor(name, list(shape), dt, kind=kind)
        return T(h.ap(), name)

    def mm(self, out, lhsT, rhs, start=True, stop=True):
        o, l, r = out.ap, lhsT.ap, rhs.ap
        self.op("pe", lambda e: e.matmul(o, l, r, start=start, stop=stop),
                reads=[lhsT.key, rhs.key], writes=[out.key])

    def act(self, out, in_, func, bias=None, scale=None, eng="act"):
        o, i = out.ap, in_.ap
        kw = {}
        rd = [in_.key]
        if bias is not None:
            if isinstance(bias, V):
                kw["bias"] = bias.ap
                rd.append(bias.key)
            else:
                kw["bias"] = bias
        if scale is not None:
            if isinstance(scale, V):
                kw["scale"] = scale.ap
                rd.append(scale.key)
            else:
                kw["scale"] = scale
        self.op("act", lambda e: e.activation(out=o, in_=i, func=func, **kw),
                reads=rd, writes=[out.key])

    def tt(self, out, in0, in1, op, eng="dve"):
        o, a, b = out.ap, in0.ap, in1.ap
        self.op(eng, lambda e: e.tensor_tensor(o, a, b, op),
                reads=[in0.key, in1.key], writes=[out.key])

    def ts(self, out, in0, s1, op0, s2=None, op1=None, eng="dve"):
        o, a = out.ap, in0.ap
        rd = [in0.key]
        if isinstance(s1, V):
            rd.append(s1.key)
            s1 = s1.ap
        if isinstance(s2, V):
            rd.append(s2.key)
            s2 = s2.ap
        if op1 is None:
            self.op(eng, lambda e: e.tensor_scalar(o, a, s1, None, op0), reads=rd, writes=[out.key])
        else:
            self.op(eng, lambda e: e.tensor_scalar(o, a, s1, s2, op0, op1), reads=rd, writes=[out.key])

    def stt(self, out, in0, scalar, in1, op0, op1, eng="dve"):
        o, a, b = out.ap, in0.ap, in1.ap
        rd = [in0.key, in1.key]
        if isinstance(scalar, V):
            rd.append(scalar.key)
            scalar = scalar.ap
        self.op(eng, lambda e: e.scalar_tensor_tensor(o, a, scalar, b, op0, op1),
                reads=rd, writes=[out.key])

    def cp(self, out, in_, eng="dve"):
        o, i = out.ap, in_.ap
        if eng == "act":
            self.op("act", lambda e: e.activation(out=o, in_=i, func=AF.Copy),
                    reads=[in_.key], writes=[out.key])
        else:
            self.op(eng, lambda e: e.tensor_copy(o, i), reads=[in_.key], writes=[out.key])

    def recip(self, out, in_):
        o, i = out.ap, in_.ap
        self.op("dve", lambda e: e.reciprocal(o, i), reads=[in_.key], writes=[out.key])

    def memset(self, out, val, eng="dve"):
        o = out.ap
        self.op(eng, lambda e: e.memset(o, val), writes=[out.key])

    def ld(self, out, in_, q="sp"):
        self.dma(q, out.ap, in_.ap, reads=[in_.key], writes=[out.key])


def vk(v, sub):
    return V(v.ap, (v.key, sub))


W_SHAPES = {
    "ln1_g": (D_MODEL,), "w_in": (D_MODEL, IN_W), "rwkv_mu": (2, RWKV_W), "rwkv_w0": (2, BW),
    "rwkv_w2": (2, LORA, BW), "rwkv_a0": (2, BW), "rwkv_a2": (2, LORA, BW), "rwkv_g2": (LORA_G, BW),
    "rwkv_kk": (BW,), "rwkv_ka": (BW,), "rwkv_rk": (16, 64), "rwkv_gn_w": (BW,), "rwkv_gn_b": (BW,),
    "hgrn_lb": (BW,), "hgrn_norm_g": (128,), "mla_q_norm_g": (Q_LORA,), "mla_kv_norm_g": (KV_LORA,),
    "mla_w_uq": (Q_LORA, 8 * 192), "mla_w_ukv": (KV_LORA, 8 * 256), "w_branch": (3, BW, D_MODEL),
    "w_o": (D_MODEL, D_MODEL), "ln2_g": (D_MODEL,), "w_mlp1": (D_MODEL, D_FF), "w_mlp2": (D_FF, D_MODEL),
    "w_pe": (PLE, D_MODEL), "w_pg": (D_MODEL, D_MODEL),
}


def build(S, L, dbg=()):
    import os as _os
    stop = _os.environ.get("MK_STOP", "")
    nc = bass.Bass("TRN2", target_bir_lowering=False)
    es = ExitStack()
    P = K(nc, es)
    NT = S // 512
    NG = S // 128
    KC = D_MODEL // 128

    x_in = P.dram("x", [S, D_MODEL], F32, "ExternalInput")
    p_in = P.dram("p", [L, S, PLE], F32, "ExternalInput")
    pos_in = P.dram("positions", [1, S], I32, "ExternalInput")
    Wd = {}
    for nm, shp in W_SHAPES.items():
        Wd[nm] = P.dram(nm, [L] + list(shp), F32, "ExternalInput")
    fin_g = P.dram("final_g", [D_MODEL], F32, "ExternalInput")
    cst_in = P.dram("consts", [128, CONST_W], F32, "ExternalInput")
    out_d = P.dram("out", [S, D_MODEL], F32, "ExternalOutput")
    dbg_out = {}

    hT = P.dram("hT", [D_MODEL, S], F32)
    zT = P.dram("zT", [IN_W, S], F32)
    yT = [P.dram("yT%d" % n, [BW, S], BF16) for n in range(3)]

    def wscr(name, Kdim, M):
        nm_ = (M + 127) // 128
        return P.dram("ws_" + name, [nm_, 128, Kdim // 128, 128], BF16)

    WS = {
        "w_in": wscr("w_in", D_MODEL, IN_W), "w_b0": wscr("w_b0", BW, D_MODEL),
        "w_b1": wscr("w_b1", BW, D_MODEL), "w_b2": wscr("w_b2", BW, D_MODEL),
        "w_o": wscr("w_o", D_MODEL, D_MODEL), "w_mlp1": wscr("w_mlp1", D_MODEL, D_FF),
        "w_mlp2": wscr("w_mlp2", D_FF, D_MODEL), "w_pe": wscr("w_pe", PLE, D_MODEL),
        "w_pg": wscr("w_pg", D_MODEL, D_MODEL),
    }

    cst = P.sb("cst", [128, CONST_W], F32)
    P.ld(cst[:, :], cst_in[:, :])
    cstb = P.sb("cstb", [128, CONST_W], BF16)
    P.cp(cstb[:, :], cst[:, :])
    ident_f = cst[:, C_ID:C_ID + 128]
    ident_b = cstb[:, C_ID:C_ID + 128]
    ones_b = cstb[:, C_ONE:C_ONE + 128]
    P.persist()
    rr = [0]

    def evac_eng():
        rr[0] += 1
        return "act" if rr[0] % 2 else "dve"

    def precast(src, Kdim, M, dst):
        kc_all = Kdim // 128
        nm_ = (M + 127) // 128
        stg, stb = pcb["f"], pcb["b"]
        it = 0
        for mi in range(nm_):
            m0 = mi * 128
            msz = min(128, M - m0)
            for k0 in range(0, kc_all, 16):
                kc = min(16, kc_all - k0)
                sf, sbb = stg[it % 3], stb[it % 3]
                it += 1
                srcv = V(src.ap[k0 * 128:(k0 + kc) * 128, m0:m0 + msz].rearrange("(c p) m -> p c m", p=128), src.key)
                P.ld(sf[:, 0:kc, 0:msz], srcv)
                P.cp(sbb[:, 0:kc, 0:msz], sf[:, 0:kc, 0:msz], eng=evac_eng())
                P.ld(vk(dst[mi, :, k0:k0 + kc, 0:msz], (mi, k0)), sbb[:, 0:kc, 0:msz], q="pool")

    pcb = {}

    def precast_layer(l):
        pcb["f"] = [P.sb("pc_f", [128, 16, 128], F32) for _ in range(3)]
        pcb["b"] = [P.sb("pc_b", [128, 16, 128], BF16) for _ in range(3)]
        precast(Wd["w_in"][l], D_MODEL, IN_W, WS["w_in"])
        for n in range(3):
            precast(Wd["w_branch"][l, n], BW, D_MODEL, WS["w_b%d" % n])
        precast(Wd["w_o"][l], D_MODEL, D_MODEL, WS["w_o"])
        precast(Wd["w_mlp1"][l], D_MODEL, D_FF, WS["w_mlp1"])
        precast(Wd["w_mlp2"][l], D_FF, D_MODEL, WS["w_mlp2"])
        precast(Wd["w_pe"][l], PLE, D_MODEL, WS["w_pe"])
        precast(Wd["w_pg"][l], D_MODEL, D_MODEL, WS["w_pg"])
        P.phase_end()

    def linear(xT, kc, wt, mtiles, ntiles, evac, wbufs):
        for j, (mi, msz) in enumerate(mtiles):
            wb = wbufs[j % len(wbufs)]
            P.ld(wb[:, 0:kc, :], wt[mi, :, :, :])
            for (n0, nsz) in ntiles:
                ps = P.ps()
                for c in range(kc):
                    P.mm(ps[0:msz, 0:nsz], wb[:, c, 0:msz], xT[:, c, n0:n0 + nsz],
                         start=(c == 0), stop=(c == kc - 1))
                evac(mi, msz, n0, nsz, ps)

    def rmsnorm_tile(src, kc, nsz, g_sb, dst, dim, sq_scr, rstd_scr):
        ps = P.ps()
        for c in range(kc):
            P.act(sq_scr[:, c, 0:nsz], src[:, c, 0:nsz], AF.Square)
        for c in range(kc):
            P.mm(ps[:, 0:nsz], ones_b, sq_scr[:, c, 0:nsz], start=(c == 0), stop=(c == kc - 1))
        P.act(rstd_scr[:, 0:nsz], ps[:, 0:nsz], AF.Sqrt, bias=cst[:, C_EPS:C_EPS + 1], scale=1.0 / dim)
        P.recip(rstd_scr[:, 0:nsz], rstd_scr[:, 0:nsz])
        for c in range(kc):
            P.stt(dst(c), src[:, c, 0:nsz], g_sb[:, c:c + 1], rstd_scr[:, 0:nsz], ALU.mult, ALU.mult,
                  eng="dve")

    def load_vec(dst, src_v, kc):
        P.ld(dst[:, 0:kc], V(src_v.ap.rearrange("(c p) -> p c", p=128), src_v.key))

    def phase_in():
        xs = [P.sb("xin", [128, D_MODEL], F32) for _ in range(2)]
        ho = [P.sb("hout", [128, 512], F32) for _ in range(4)]
        it = 0
        for g in range(NG):
            xt = xs[g % 2]
            P.ld(xt[:, :], x_in[g * 128:(g + 1) * 128, :])
            for c4 in range(KC // 4):
                ps = P.ps()
                for j in range(4):
                    c = c4 * 4 + j
                    P.mm(ps[:, j * 128:(j + 1) * 128], xt[:, c * 128:(c + 1) * 128], ident_f)
                o = ho[it % 4]
                it += 1
                P.cp(o[:, :], ps[:, :], eng=evac_eng())
                dstv = V(hT.h[c4 * 512:(c4 + 1) * 512, g * 128:(g + 1) * 128].rearrange("(j p) t -> p j t", p=128),
                         (hT.key, g, c4))
                P.ld(dstv, o[:, :].re("p (j t) -> p j t", j=4), q="pool")
        P.phase_end()

    def phase_A(l):
        g_sb = P.sb("ln1g", [128, KC], F32)
        load_vec(g_sb, Wd["ln1_g"][l], KC)
        ST = min(S, 2048)
        hn = P.sb("hn", [128, KC, ST], BF16)
        hb = P.sb("hb", [128, KC, 512], F32)
        sq = P.sb("sq", [128, KC, 512], BF16)
        rstd = P.sb("rstd", [128, 512], F32)
        wbufs = [P.sb("wA", [128, KC, 128], BF16) for _ in range(2)]
        stg = [P.sb("zst", [128, 512], F32) for _ in range(4)]
        cnt = [0]
        for s0 in range(0, S, ST):
            for n0 in range(0, ST, 512):
                P.ld(hb[:, :, :], V(hT.h[:, s0 + n0:s0 + n0 + 512].rearrange("(c p) t -> p c t", p=128), hT.key))
                rmsnorm_tile(hb, KC, 512, g_sb, lambda c, n0=n0: hn[:, c, n0:n0 + 512], D_MODEL, sq, rstd)

            def evac(mi, msz, n0, nsz, ps, s0=s0):
                o = stg[cnt[0] % 4]
                cnt[0] += 1
                P.cp(o[0:msz, 0:nsz], ps[0:msz, 0:nsz], eng=evac_eng())
                P.ld(vk(zT[mi * 128:mi * 128 + msz, s0 + n0:s0 + n0 + nsz], (mi, s0 + n0)), o[0:msz, 0:nsz], q="pool")

            mt = [(mi, min(128, IN_W - mi * 128)) for mi in range((IN_W + 127) // 128)]
            linear(hn, KC, WS["w_in"], mt, [(n0, 512) for n0 in range(0, ST, 512)], evac, wbufs)
        P.phase_end()

    yfT = P.dram("yfT", [BW, S], F32)
    H = 16

    def phase_R(l):
        def hv(name, src_v):
            t = P.sb(name, [64, H], F32)
            P.ld(t[:, :], V(src_v.ap.rearrange("(h c) -> c h", c=64), src_v.key))
            return t
        kkp = hv("kkp", Wd["rwkv_kk"][l])
        kap = hv("kap", Wd["rwkv_ka"][l])
        gnw = hv("gnw", Wd["rwkv_gn_w"][l])
        gnb = hv("gnb", Wd["rwkv_gn_b"][l])
        rkp = P.sb("rkp", [64, H], F32)
        P.ld(rkp[:, :], V(Wd["rwkv_rk"][l].ap.rearrange("h c -> c h"), Wd["rwkv_rk"].key))
        omka = P.sb("omka", [64, H], F32)
        P.ts(omka[:, :], kap[:, :], -1.0, ALU.mult, 1.0, ALU.add)
        tmka = P.sb("tmka", [64, H], F32)
        P.ts(tmka[:, :], kap[:, :], -2.0, ALU.mult, 2.0, ALU.add)
        mu = Wd["rwkv_mu"]
        def mut(name, parts, nb, lo, hi, which):
            t = P.sb(name, [parts, nb], F32)
            P.ld(t[:, :], V(mu.h[l, which, lo:hi].rearrange("(j c) -> c j", c=parts), mu.key))
            return t
        m0_rkv = mut("m0rkv", 64, 48, 0, 3072, 0)
        m1_rkv = mut("m1rkv", 64, 48, 0, 3072, 1)
        m0_lo = mut("m0lo", 64, 4, 3072, 3328, 0)
        m1_lo = mut("m1lo", 64, 4, 3072, 3328, 1)
        m0_g = mut("m0g", 80, 2, 3328, 3488, 0)
        m1_g = mut("m1g", 80, 2, 3328, 3488, 1)

        def c0of(name, a, b, parts, nb):
            t = P.sb(name, [parts, nb], F32)
            P.tt(t[:, :], a[:, :], b[:, :], ALU.add)
            P.ts(t[:, :], t[:, :], -1.0, ALU.mult, 1.0, ALU.add)
            return t
        c0_rkv = c0of("c0rkv", m0_rkv, m1_rkv, 64, 48)
        c0_lo = c0of("c0lo", m0_lo, m1_lo, 64, 4)
        c0_g = c0of("c0g", m0_g, m1_g, 80, 2)

        w2a = [P.sb("w2a%d" % d, [66, BW], BF16) for d in range(2)]
        a2a = [P.sb("a2a%d" % d, [66, BW], BF16) for d in range(2)]
        g2b = P.sb("g2b", [80, 2, BW], BF16)
        off0 = P.sb_off
        st = P.sb("augst", [64, BW], F32)
        br = P.sb("augbr", [1, BW], F32)
        hif = P.sb("aughif", [1, BW], F32)
        hi = P.sb("aughi", [1, BW], BF16)
        lo = P.sb("auglo", [1, BW], BF16)

        def aug(t, w_v, b_v):
            P.ld(st[:, :], w_v)
            P.cp(t[0:64, :], st[:, :])
            P.ld(br[:, :], V(b_v.ap.rearrange("(o n) -> o n", o=1), b_v.key))
            P.cp(hi[:, :], br[:, :])
            P.cp(hif[:, :], hi[:, :])
            P.tt(hif[:, :], br[:, :], hif[:, :], ALU.subtract)
            P.cp(lo[:, :], hif[:, :])
            P.ld(t[64:65, :], hi[:, :])
            P.ld(t[65:66, :], lo[:, :])
        for d in range(2):
            aug(w2a[d], Wd["rwkv_w2"][l, d], Wd["rwkv_w0"][l, d])
            aug(a2a[d], Wd["rwkv_a2"][l, d], Wd["rwkv_a0"][l, d])
        g2f = P.sb("g2f", [80, 2, BW], F32)
        P.ld(g2f[:, :, :], V(Wd["rwkv_g2"][l].ap.rearrange("(j p) n -> p j n", p=80), Wd["rwkv_g2"].key))
        P.cp(g2b[:, :, :], g2f[:, :, :])
        P.barrier()
        P.sb_off = off0

        zin = P.sb("zin", [64, 16, 130], F32)
        zlo = P.sb("zlo", [64, 4, 130], F32)
        zg = P.sb("zg", [80, 2, 130], F32)
        u = P.sb("u", [64, 48, 128], F32)
        ulo = P.sb("ulo", [64, 4, 128], F32)
        tlo = P.sb("tlo", [64, 4, 128], F32)
        ug = P.sb("ug", [80, 2, 128], F32)
        tg = P.sb("tg", [80, 2, 128], F32)
        wda = P.sb("wda", [66, 128], BF16)
        ada = [P.sb("ada%d" % d, [66, 128], BF16) for d in range(2)]
        sgd = P.sb("sgd", [80, 2, 128], BF16)
        P.memset(wda[64:66, :], 1.0)
        for d in range(2):
            P.memset(ada[d][64:66, :], 1.0)
        kk = P.sb("kk", [64, H, 128], F32)
        sqb = P.sb("sqb", [64, H, 128], BF16)
        s_tm = P.sb("s_tm", [128, BW], F32)
        E1 = P.sb("E1", [64, H, 128], F32)
        E2 = P.sb("E2", [64, H, 128], F32)
        av = [P.sb("a%d" % d, [64, H, 128], F32) for d in range(2)]
        kd = P.sb("kd", [64, H, 128], F32)
        be = P.sb("be", [64, H, 128], F32)
        tmp = be
        KR = P.sb("KR", [64, H, 256], BF16)
        BT = P.sb("BT", [64, H, 128], BF16)
        KT = P.sb("KT", [64, H, 128], BF16)
        NBH = P.sb("NBH", [64, H, 128], BF16)
        KH = P.sb("KH", [64, H, 128], BF16)
        VB = P.sb("VB", [64, H, 128], BF16)
        TM = P.sb("TM", [128, H, 320], BF16)
        G1 = [P.sb("G1", [128, 512], BF16) for _ in range(8)]
        G2 = [P.sb("G2", [128, 128], BF16) for _ in range(8)]
        LV = [[P.sb("LV", [128, 384], BF16) for _ in range(2)] for _ in range(8)]
        UM = [P.sb("UM", [128, 128], BF16) for _ in range(8)]
        RpT = [P.sb("RpT", [64, 128], BF16) for _ in range(8)]
        PT = [P.sb("PT", [64, 64], BF16) for _ in range(8)]
        dG = [P.sb("dG", [64, 64], F32) for _ in range(8)]
        WT = [P.sb("WT", [128, 128], BF16) for _ in range(8)]
        Tst = P.sb("Tst", [64, H, 64], BF16)
        yv = kk
        yf = E1
        yab = BT

        def bc_t(t2, nb):
            return V(t2.ap.unsqueeze(2).broadcast_to([t2.ap.shape[0], nb, 128]), t2.key)

        def shift(dst, src, tmpb, c0, m0, m1, nb, do=0, co=0):
            dv_ = dst[:, do:do + nb, :]
            tv_ = tmpb[:, 0:nb, :]
            P.tt(dv_, src[:, 0:nb, 1:129], bc_t(c0[:, co:co + nb], nb), ALU.mult)
            P.tt(tv_, src[:, 0:nb, 0:128], bc_t(m0[:, co:co + nb], nb), ALU.mult)
            P.tt(dv_, dv_, tv_, ALU.add)
            P.tt(tv_, src[:, 0:nb, 2:130], bc_t(m1[:, co:co + nb], nb), ALU.mult)
            P.tt(dv_, dv_, tv_, ALU.add)

        def load_halo(dst, r0, r1, parts, t0):
            lo = max(t0 - 1, 0)
            hi = min(t0 + 129, S)
            if t0 == 0:
                P.memset(dst[:, :, 0:1], 0.0)
            if t0 + 129 > S:
                P.memset(dst[:, :, 129:130], 0.0)
            P.ld(dst[:, :, lo - (t0 - 1):hi - (t0 - 1)],
                 V(zT.h[r0:r1, lo:hi].rearrange("(j c) t -> c j t", c=parts), zT.key))

        def rwkv_post(l, g, t0):
            P.cp(yab[:, :, :], yv[:, :, :], eng="act")
            for b4 in range(4):
                sl = slice(b4 * 4, (b4 + 1) * 4)
                ps = P.ps()
                P.mm(ps[0:64, :], ones_b[0:64, 0:64], yab[:, sl, :].re("p a b -> p (a b)"))
                P.stt(be[:, sl, :].re("p a b -> p (a b)"), ps[0:64, :], -1.0 / 64, yv[:, sl, :].re("p a b -> p (a b)"),
                      ALU.mult, ALU.add)
            P.act(sqb[:, :, :], be[:, :, :], AF.Square)
            for b4 in range(4):
                sl = slice(b4 * 4, (b4 + 1) * 4)
                ps = P.ps()
                P.mm(ps[0:64, :], ones_b[0:64, 0:64], sqb[:, sl, :].re("p a b -> p (a b)"))
                P.act(kd[:, sl, :].re("p a b -> p (a b)"), ps[0:64, :], AF.Sqrt, bias=cst[0:64, C_GEPS:C_GEPS + 1], scale=1.0 / 64)
            P.recip(kd[:, :, :], kd[:, :, :])
            P.tt(be[:, :, :], be[:, :, :], kd[:, :, :], ALU.mult)
            P.tt(be[:, :, :], be[:, :, :], bc_t(gnw[:, :], H), ALU.mult)
            P.tt(be[:, :, :], be[:, :, :], bc_t(gnb[:, :], H), ALU.add)
            P.tt(E2[:, :, :], av[0][:, :, :], av[1][:, :, :], ALU.add)
            P.tt(E2[:, :, :], E2[:, :, :], bc_t(kap[:, :], H), ALU.mult)
            P.tt(E2[:, :, :], E2[:, :, :], bc_t(tmka[:, :], H), ALU.add)
            P.tt(E2[:, :, :], E2[:, :, :], u[:, 16:32, :], ALU.mult)
            P.tt(E2[:, :, :], E2[:, :, :], u[:, 0:16, :], ALU.mult)
            P.tt(sqb[:, :, :], E2[:, :, :], bc_t(rkp[:, :], H), ALU.mult)
            for b4 in range(4):
                sl = slice(b4 * 4, (b4 + 1) * 4)
                ps = P.ps()
                P.mm(ps[0:64, :], ones_b[0:64, 0:64], sqb[:, sl, :].re("p a b -> p (a b)"))
                P.tt(kd[:, sl, :].re("p a b -> p (a b)"), ps[0:64, :], u[:, 32 + b4 * 4:32 + (b4 + 1) * 4, :].re("p a b -> p (a b)"),
                     ALU.mult)
            P.tt(be[:, :, :], be[:, :, :], kd[:, :, :], ALU.add)
            shift(ug, zg, tg, c0_g, m0_g, m1_g, 2)
            P.act(sgd[:, :, :], ug[:, :, :], AF.Sigmoid)
            for b4 in range(4):
                ps = P.ps()
                for j in range(4):
                    h = b4 * 4 + j
                    for q in range(2):
                        P.mm(ps[0:64, j * 128:(j + 1) * 128], g2b[:, q, h * 64:(h + 1) * 64], sgd[:, q, :],
                             start=(q == 0), stop=(q == 1))
                sl = slice(b4 * 4, (b4 + 1) * 4)
                P.tt(yab[:, sl, :].re("p a b -> p (a b)"), ps[0:64, :], be[:, sl, :].re("p a b -> p (a b)"), ALU.mult)
            P.ld(vk(V(yT[0].h[:, t0:t0 + 128].rearrange("(h c) t -> c h t", c=64), yT[0].key), g), yab[:, :, :], q="pool")

        if stop == "Rs":
            P.phase_end()
            return
        for d in range(2):
            P.memset(Tst[:, :, :], 0.0)
            mk1 = cst[:, (C_RM1F if d == 0 else C_RM1B):(C_RM1F if d == 0 else C_RM1B) + 512]
            mk2 = cst[:, (C_RM2F if d == 0 else C_RM2B):(C_RM2F if d == 0 else C_RM2B) + 128]
            tri = cst[:, (C_TRIF if d == 0 else C_TRIB):(C_TRIF if d == 0 else C_TRIB) + 128]
            last = 127 if d == 0 else 0
            order = range(NG) if d == 0 else range(NG - 1, -1, -1)
            for g in order:
                t0 = g * 128
                for part in range(3):
                    load_halo(zin, part * 1024, (part + 1) * 1024, 64, t0)
                    shift(u, zin, tmp, c0_rkv, m0_rkv, m1_rkv, 16, do=part * 16, co=part * 16)
                load_halo(zlo, 3072, 3328, 64, t0)
                load_halo(zg, 3328, 3488, 80, t0)
                shift(ulo, zlo, tlo, c0_lo, m0_lo, m1_lo, 4)
                r_ = lambda sl=slice(None): u[:, 0:16, sl]
                P.tt(kk[:, :, :], u[:, 16:32, :], bc_t(kkp[:, :], H), ALU.mult)
                P.act(sqb[:, :, :], kk[:, :, :], AF.Square)
                for b4 in range(4):
                    ps = P.ps()
                    P.mm(ps[0:64, :], ones_b[0:64, 0:64], sqb[:, b4 * 4:(b4 + 1) * 4, :].re("p a b -> p (a b)"))
                    P.ts(tmp[:, b4 * 4:(b4 + 1) * 4, :].re("p a b -> p (a b)"), ps[0:64, :], 1e-24, ALU.max)
                P.act(tmp[:, :, :], tmp[:, :, :], AF.Sqrt)
                P.recip(tmp[:, :, :], tmp[:, :, :])
                P.tt(kk[:, :, :], kk[:, :, :], tmp[:, 0:16, :], ALU.mult)
                P.act(wda[0:64, :], ulo[:, d, :], AF.Tanh)
                for dd in (range(2) if d == 1 else [d]):
                    P.cp(ada[dd][0:64, :], ulo[:, 2 + dd, :])
                for hf in range(2):
                    ps = P.ps()
                    P.mm(ps[:, :], wda[:, :], w2a[d][:, hf * 512:(hf + 1) * 512])
                    P.act(s_tm[:, hf * 512:(hf + 1) * 512], ps[:, :], AF.Sigmoid)
                for dd in (range(2) if d == 1 else [d]):
                    for b4 in range(4):
                        ps = P.ps()
                        for j in range(4):
                            h = b4 * 4 + j
                            P.mm(ps[0:64, j * 128:(j + 1) * 128], a2a[dd][:, h * 64:(h + 1) * 64], ada[dd][:, :])
                        P.act(av[dd][:, b4 * 4:(b4 + 1) * 4, :].re("p a b -> p (a b)"), ps[0:64, :], AF.Sigmoid)
                for b4 in range(4):
                    ps = P.ps()
                    for j in range(4):
                        h = b4 * 4 + j
                        P.mm(ps[0:64, j * 128:(j + 1) * 128], s_tm[:, h * 64:(h + 1) * 64], tri)
                    P.act(E1[:, b4 * 4:(b4 + 1) * 4, :].re("p a b -> p (a b)"), ps[0:64, :], AF.Exp, scale=-DECAY_SCALE)
                    P.act(E2[:, b4 * 4:(b4 + 1) * 4, :].re("p a b -> p (a b)"), ps[0:64, :], AF.Exp, scale=DECAY_SCALE)
                a = av[d]
                P.tt(kd[:, :, :], a[:, :, :], bc_t(kap[:, :], H), ALU.mult)
                P.tt(kd[:, :, :], kd[:, :, :], bc_t(omka[:, :], H), ALU.add)
                P.tt(kd[:, :, :], kd[:, :, :], u[:, 16:32, :], ALU.mult)
                P.tt(be[:, :, :], kk[:, :, :], a[:, :, :], ALU.mult)
                if d == 0:
                    P.tt(KR[:, :, 1:128], kk[:, :, 1:128], E1[:, :, 0:127], ALU.mult)
                    P.cp(KR[:, :, 0:1], kk[:, :, 0:1])
                else:
                    P.tt(KR[:, :, 0:127], kk[:, :, 0:127], E1[:, :, 1:128], ALU.mult)
                    P.cp(KR[:, :, 127:128], kk[:, :, 127:128])
                P.tt(KR[:, :, 128:256], u[:, 0:16, :], E1[:, :, :], ALU.mult)
                P.tt(be[:, :, :], be[:, :, :], E2[:, :, :], ALU.mult)
                P.cp(BT[:, :, :], be[:, :, :], eng="act")
                P.tt(kd[:, :, :], kd[:, :, :], E2[:, :, :], ALU.mult)
                P.cp(KT[:, :, :], kd[:, :, :], eng="act")
                gC = V(E1.h[:, :, last:last + 1].broadcast_to([64, H, 128]), E1.key)
                P.stt(NBH[:, :, :], be[:, :, :], -1.0, gC, ALU.mult, ALU.mult)
                P.tt(KH[:, :, :], kd[:, :, :], gC, ALU.mult)
                P.cp(VB[:, :, :], u[:, 32:48, :], eng="act")
                if stop == "Re":
                    continue
                for (srcT, off) in ((KR, 64), (NBH, 128), (KH, 192), (VB, 256)):
                    for b8 in range(2):
                        ps = P.ps()
                        for j in range(8):
                            h = b8 * 8 + j
                            P.mm(ps[:, j * 64:(j + 1) * 64], srcT[:, h, 0:128], ident_b[0:64, 0:64])
                        P.cp(TM[:, b8 * 8:(b8 + 1) * 8, off:off + 64],
                             ps[:, :].re("p (a b) -> p a b", a=8), eng=evac_eng())
                if stop == "Rt":
                    continue
                for b8 in range(2):
                    for q4 in range(2):
                        hs = [b8 * 8 + q4 * 4 + j for j in range(4)]
                        psA = [P.ps() for _ in range(4)]
                        psB = P.ps()
                        for j, h in enumerate(hs):
                            P.mm(psA[j][:, 0:256], BT[:, h, :], KR[:, h, :])
                            P.mm(psA[j][:, 256:512], KT[:, h, :], KR[:, h, :])
                            P.mm(psB[:, j * 128:(j + 1) * 128], KR[:, h, 0:128], BT[:, h, :])
                        for j, h in enumerate(hs):
                            jj = q4 * 4 + j
                            P.tt(G1[jj][:, :], psA[j][:, :], mk1, ALU.mult)
                            P.tt(G2[jj][:, :], psB[:, j * 128:(j + 1) * 128], mk2, ALU.mult, eng="dve")
                    cur = [None] * 8
                    for lev in range(1, 8):
                        pl = [P.ps() for _ in range(8)]
                        for jj in range(8):
                            if lev == 1:
                                Y, YT, Z = G1[jj][:, 0:128], G2[jj][:, :], ident_b
                            else:
                                c_ = cur[jj]
                                Y, YT, Z = c_[:, 0:128], c_[:, 128:256], c_[:, 256:384]
                            if lev < 7:
                                P.mm(pl[jj][:, 0:128], YT, Y)
                                P.mm(pl[jj][:, 128:256], Y, YT)
                            P.mm(pl[jj][:, 256:384], YT, Z, start=True, stop=False)
                            P.mm(pl[jj][:, 256:384], ident_b, Z, start=False, stop=True)
                        for jj in range(8):
                            nxt = LV[jj][lev % 2]
                            if lev < 7:
                                P.cp(nxt[:, :], pl[jj][:, 0:384], eng=evac_eng())
                            else:
                                P.cp(WT[jj][:, :], pl[jj][:, 256:384], eng=evac_eng())
                            cur[jj] = nxt
                    for q4 in range(2):
                        b4 = b8 * 2 + q4
                        hs = [b4 * 4 + j for j in range(4)]
                        J = [q4 * 4 + j for j in range(4)]
                        px = P.ps()
                        for j, h in enumerate(hs):
                            P.mm(px[:, j * 64:(j + 1) * 64], G1[J[j]][:, 256:384], TM[:, h, 256:320])
                        for j, h in enumerate(hs):
                            P.cp(TM[:, h, 0:64], px[:, j * 64:(j + 1) * 64], eng=evac_eng())
                        pu = P.ps()
                        for j, h in enumerate(hs):
                            P.mm(pu[:, j * 128:(j + 1) * 128], WT[J[j]][:, :], TM[:, h, 0:128])
                        for j, h in enumerate(hs):
                            P.cp(UM[J[j]][:, :], pu[:, j * 128:(j + 1) * 128], eng="dve")
                        pr = P.ps()
                        pp = P.ps()
                        for j, h in enumerate(hs):
                            P.mm(pr[0:64, j * 128:(j + 1) * 128], UM[J[j]][:, 64:128], G1[J[j]][:, 128:256], start=True, stop=False)
                            P.mm(pr[0:64, j * 128:(j + 1) * 128], ident_b[0:64, 0:64], KR[:, h, 128:256], start=False, stop=True)
                            P.mm(pp[0:64, j * 64:(j + 1) * 64], UM[J[j]][:, 64:128], TM[:, h, 128:192])
                        for j, h in enumerate(hs):
                            P.cp(RpT[J[j]][:, :], pr[0:64, j * 128:(j + 1) * 128], eng="act")
                            P.stt(dG[J[j]][:, :], ident_f[0:64, 0:64], E1[:, h, last:last + 1], ident_f[0:64, 0:64], ALU.mult, ALU.mult)
                            P.tt(PT[J[j]][:, :], pp[0:64, j * 64:(j + 1) * 64], dG[J[j]][:, :], ALU.add)
                        py = P.ps()
                        pt = P.ps()
                        for j, h in enumerate(hs):
                            yo = py[0:64, j * 128:(j + 1) * 128]
                            P.mm(yo, UM[J[j]][:, 0:64], G1[J[j]][:, 128:256], start=True, stop=False)
                            P.mm(yo, TM[:, h, 256:320], G1[J[j]][:, 384:512], start=False, stop=False)
                            P.mm(yo, Tst[:, h, :], RpT[J[j]][:, :], start=False, stop=True)
                            to = pt[0:64, j * 64:(j + 1) * 64]
                            P.mm(to, TM[:, h, 128:192], UM[J[j]][:, 0:64], start=True, stop=False)
                            P.mm(to, TM[:, h, 192:256], TM[:, h, 256:320], start=False, stop=False)
                            P.mm(to, PT[J[j]][:, :], Tst[:, h, :], start=False, stop=True)
                        P.cp(yv[:, b4 * 4:(b4 + 1) * 4, :].re("p a b -> p (a b)"), py[0:64, :], eng="act")
                        P.cp(Tst[:, b4 * 4:(b4 + 1) * 4, :].re("p a b -> p (a b)"), pt[0:64, 0:256], eng="dve")
                if stop in ("Rg", "Ri", "Ru", "Rx", "Rx1"):
                    continue
                if d == 0:
                    P.ld(vk(V(yfT.h[:, t0:t0 + 128].rearrange("(h c) t -> c h t", c=64), yfT.key), g), yv[:, :, :], q="pool")
                else:
                    P.ld(yf[:, :, :], V(yfT.h[:, t0:t0 + 128].rearrange("(h c) t -> c h t", c=64), yfT.key))
                    P.tt(yv[:, :, :], yv[:, :, :], yf[:, :, :], ALU.add)
                    rwkv_post(l, g, t0)
            P.barrier()
        P.phase_end()

    yhfT = P.dram("yhfT", [BW, S], F32)
    HB = RWKV_W

    def phase_H(l):
        HH = 8
        lball = P.sb("lball", [128, HH, L], F32)
        for i in range(L):
            P.ld(lball[:, :, i], V(Wd["hgrn_lb"].h[i].rearrange("(h c) -> c h", c=128), Wd["hgrn_lb"].key))
        P.act(lball[:, :, :], lball[:, :, :], AF.Exp)
        den = P.sb("lbden", [128, HH], F32)
        P.cp(den[:, :], lball[:, :, 0])
        for i in range(1, L):
            P.tt(den[:, :], den[:, :], lball[:, :, i], ALU.add)
        P.op("dve", (lambda e, o=den[:, :].ap: e.reciprocal(o, o)), reads=[den.key], writes=[den.key])
        lbv = P.sb("lbv", [128, HH], F32)
        P.memset(lbv[:, :], 0.0)
        for i in range(1, l + 1):
            P.tt(lbv[:, :], lbv[:, :], lball[:, :, i], ALU.add)
        P.tt(lbv[:, :], lbv[:, :], den[:, :], ALU.mult)
        oml = P.sb("oml", [128, HH], F32)
        P.ts(oml[:, :], lbv[:, :], -1.0, ALU.mult, 1.0, ALU.add)
        ng = P.sb("hng", [128, 1], F32)
        P.ld(ng[:, :], V(Wd["hgrn_norm_g"][l].ap.rearrange("(c o) -> c o", o=1), Wd["hgrn_norm_g"].key))

        def T3(name, dt):
            return P.sb(name, [128, HH, 128], dt)
        qf, zf, vf, gf = T3("qf", F32), T3("zf", F32), T3("vf", F32), T3("gf", F32)
        t1, t2 = T3("t1", F32), T3("t2", F32)
        lf_tm = T3("lf_tm", F32)
        E1, E2 = T3("hE1", F32), T3("hE2", F32)
        QT, KT, KH, VB = T3("hQT", BF16), T3("hKT", BF16), T3("hKH", BF16), T3("hVB", BF16)
        VT, KHT = T3("hVT", BF16), T3("hKHT", BF16)
        Vexp = P.sb("Vexp", [128, HH, 4, 128], BF16)
        Sc4 = [P.sb("Sc4", [128, 4, 128], BF16) for _ in range(2)]
        Tf = T3("hTf", F32)
        Tb = T3("hTb", BF16)
        yv = T3("hyv", F32)
        yf = T3("hyf", F32)
        sqb = T3("hsq", BF16)
        yob = T3("hyob", BF16)

        def bc_t(t2_, nb):
            return V(t2_.ap.unsqueeze(2).broadcast_to([128, nb, 128]), t2_.key)

        def rows(r0, t0):
            return V(zT.h[HB + r0:HB + r0 + 1024, t0:t0 + 128].rearrange("(h c) t -> c h t", c=128), zT.key)

        for d in range(2):
            P.memset(Tf[:, :, :], 0.0)
            P.memset(Tb[:, :, :], 0.0)
            hm = cst[:, (C_HMF if d == 0 else C_HMB):(C_HMF if d == 0 else C_HMB) + 128]
            order = range(NG) if d == 0 else range(NG - 1, -1, -1)
            corder = range(4) if d == 0 else range(3, -1, -1)
            lastoff = 31 if d == 0 else 0
            for g in order:
                t0 = g * 128
                P.ld(qf[:, :, :], rows(0, t0))
                P.ld(zf[:, :, :], rows(1024 * (1 + d), t0))
                P.ld(vf[:, :, :], rows(3072, t0))
                P.act(qf[:, :, :], qf[:, :, :], AF.Silu)
                P.act(t1[:, :, :], zf[:, :, :], AF.Sigmoid)
                P.tt(t1[:, :, :], t1[:, :, :], bc_t(oml[:, :], HH), ALU.mult)
                P.tt(t2[:, :, :], t1[:, :, :], bc_t(lbv[:, :], HH), ALU.add)
                P.ts(t2[:, :, :], t2[:, :, :], F_TINY, ALU.max)
                P.act(t2[:, :, :], t2[:, :, :], AF.Ln)
                P.stt(t1[:, :, :], t1[:, :, :], -1.0, bc_t(oml[:, :], HH), ALU.mult, ALU.add)
                for b in range(2):
                    ps = P.ps()
                    for j in range(4):
                        h = b * 4 + j
                        P.mm(ps[:, j * 128:(j + 1) * 128], t2[:, h, :], ident_f)
                    P.cp(lf_tm[:, b * 4:(b + 1) * 4, :].re("p a b -> p (a b)"), ps[:, :], eng=evac_eng())
                for b in range(2):
                    ps = P.ps()
                    for j in range(4):
                        h = b * 4 + j
                        P.mm(ps[:, j * 128:(j + 1) * 128], lf_tm[:, h, :], hm)
                    P.act(E1[:, b * 4:(b + 1) * 4, :].re("p a b -> p (a b)"), ps[:, :], AF.Exp)
                    P.act(E2[:, b * 4:(b + 1) * 4, :].re("p a b -> p (a b)"), ps[:, :], AF.Exp, scale=-1.0)
                P.tt(QT[:, :, :], qf[:, :, :], E1[:, :, :], ALU.mult)
                P.tt(t1[:, :, :], t1[:, :, :], E2[:, :, :], ALU.mult)
                P.cp(KT[:, :, :], t1[:, :, :], eng="act")
                gC = V(E1.h[:, :, :].rearrange("p h (c t) -> p h c t", t=32)[:, :, :, lastoff:lastoff + 1]
                       .broadcast_to([128, HH, 4, 32]), E1.key)
                P.tt(KH[:, :, :].re("p h (c t) -> p h c t", t=32), t1[:, :, :].re("p h (c t) -> p h c t", t=32), gC, ALU.mult)
                P.cp(VB[:, :, :], vf[:, :, :], eng="act")
                for (srcT, dstT) in ((VB, VT), (KH, KHT)):
                    for b in range(2):
                        ps = P.ps()
                        for j in range(4):
                            h = b * 4 + j
                            P.mm(ps[:, j * 128:(j + 1) * 128], srcT[:, h, :], ident_b)
                        P.cp(dstT[:, b * 4:(b + 1) * 4, :].re("p a b -> p (a b)"), ps[:, :], eng=evac_eng())
                cmv = cst[:, C_CM:C_CM + 4]
                P.tt(Vexp[:, :, :, :],
                     V(VT.h[:, :, :].unsqueeze(2).broadcast_to([128, HH, 4, 128]), VT.key),
                     V(cmv.ap.unsqueeze(1).unsqueeze(3).broadcast_to([128, HH, 4, 128]), cmv.key), ALU.mult)
                for half in range(2):
                    hs = [half * 4 + j for j in range(4)]
                    psc = P.ps()
                    for j, h in enumerate(hs):
                        P.mm(psc[:, j * 128:(j + 1) * 128], KT[:, h, :], QT[:, h, :])
                    sc4 = Sc4[half]
                    P.tt(sc4[:, :, :], psc[:, :].re("p (a b) -> p a b", a=4),
                         V(hm.ap.unsqueeze(1).broadcast_to([128, 4, 128]), hm.key), ALU.mult)
                    pqs = []
                    for j, h in enumerate(hs):
                        pq = P.ps()
                        P.mm(pq[:, :], KHT[:, h, :], Vexp[:, h, :, :].re("p c v -> p (c v)"))
                        pqs.append(pq)
                    py = P.ps()
                    for c4 in corder:
                        cs_ = slice(c4 * 32, (c4 + 1) * 32)
                        tc_ = c4 * 32 + lastoff
                        for j, h in enumerate(hs):
                            po_ = py[:, j * 128 + c4 * 32:j * 128 + (c4 + 1) * 32]
                            P.mm(po_, VT[:, h, :], sc4[:, j, cs_], start=True, stop=False)
                            P.mm(po_, Tb[:, h, :], QT[:, h, cs_], start=False, stop=True)
                        for j, h in enumerate(hs):
                            P.stt(Tf[:, h, :], Tf[:, h, :], E1[:, h, tc_:tc_ + 1], pqs[j][:, c4 * 128:(c4 + 1) * 128],
                                  ALU.mult, ALU.add)
                            P.cp(Tb[:, h, :], Tf[:, h, :], eng="act")
                    P.cp(yv[:, half * 4:(half + 1) * 4, :].re("p a b -> p (a b)"), py[:, :], eng="act")
                dv = V(yhfT.h[:, t0:t0 + 128].rearrange("(h c) t -> c h t", c=128), yhfT.key)
                if d == 0:
                    P.ld(vk(dv, g), yv[:, :, :], q="pool")
                else:
                    P.ld(yf[:, :, :], dv)
                    P.ld(gf[:, :, :], rows(4096, t0))
                    P.tt(yv[:, :, :], yv[:, :, :], yf[:, :, :], ALU.add)
                    P.act(sqb[:, :, :], yv[:, :, :], AF.Square)
                    for b in range(2):
                        ps = P.ps()
                        P.mm(ps[:, :], ones_b, sqb[:, b * 4:(b + 1) * 4, :].re("p a b -> p (a b)"))
                        P.act(t2[:, b * 4:(b + 1) * 4, :].re("p a b -> p (a b)"), ps[:, :], AF.Sqrt, bias=cst[:, C_EPS:C_EPS + 1], scale=1.0 / 128)
                    P.recip(t2[:, :, :], t2[:, :, :])
                    P.stt(yv[:, :, :], yv[:, :, :], ng[:, 0:1], t2[:, :, :], ALU.mult, ALU.mult)
                    P.act(gf[:, :, :], gf[:, :, :], AF.Silu)
                    P.tt(yob[:, :, :], yv[:, :, :], gf[:, :, :], ALU.mult)
                    P.ld(vk(V(yT[1].h[:, t0:t0 + 128].rearrange("(h c) t -> c h t", c=128), yT[1].key), g), yob[:, :, :], q="pool")
            P.barrier()
        P.phase_end()

    MB = RWKV_W + HGRN_W
    qn_s = P.dram("qn_s", [8, 128, S], BF16)
    qr_s = P.dram("qr_s", [8, 64, S], BF16)
    kn_s = P.dram("kn_s", [8, 128, S], BF16)
    kr_s = P.dram("kr_s", [64, S], BF16)
    v_s = P.dram("v_s", [S, 1024], BF16)
    TWO_PI = 6.283185307179586
    PI = 3.141592653589793

    def phase_M1(l):
        wqf = P.sb("wqf", [128, 6, 1536], F32)
        P.ld(wqf[:, :, :], V(Wd["mla_w_uq"][l].ap.rearrange("(c p) m -> p c m", p=128), Wd["mla_w_uq"].key))
        wq = P.sb("wq", [128, 6, 1536], BF16)
        P.cp(wq[:, 0:3, :], wqf[:, 0:3, :])
        P.cp(wq[:, 3:6, :], wqf[:, 3:6, :], eng="act")
        wkf = P.sb("wkf", [128, 4, 2048], F32)
        P.ld(wkf[:, :, :], V(Wd["mla_w_ukv"][l].ap.rearrange("(c p) m -> p c m", p=128), Wd["mla_w_ukv"].key))
        wk = P.sb("wk", [128, 4, 2048], BF16)
        P.cp(wk[:, 0:2, :], wkf[:, 0:2, :])
        P.cp(wk[:, 2:4, :], wkf[:, 2:4, :], eng="act")
        qg = P.sb("qg", [128, 6], F32)
        load_vec(qg, Wd["mla_q_norm_g"][l], 6)
        kg = P.sb("kg", [128, 4], F32)
        load_vec(kg, Wd["mla_kv_norm_g"][l], 4)
        cq = P.sb("cq", [128, 6, 512], F32)
        ckv = P.sb("ckv", [128, 4, 512], F32)
        krf = P.sb("krf", [64, 512], F32)
        cqn = P.sb("cqn", [128, 6, 512], BF16)
        ckvn = P.sb("ckvn", [128, 4, 512], BF16)
        sq = P.sb("msq", [128, 6, 512], BF16)
        rstd = P.sb("mrstd", [128, 512], F32)
        posi = P.sb("posi", [64, 512], I32)
        ang = P.sb("ang", [64, 512], F32)
        am = P.sb("am", [64, 512], F32)
        cosv = P.sb("cosv", [64, 512], F32)
        sinv = P.sb("sinv", [64, 512], F32)
        xr = P.sb("xr", [64, 512], F32)
        r1 = P.sb("r1", [64, 512], F32)
        r2 = P.sb("r2", [64, 512], F32)
        ob = [P.sb("mob", [128, 512], BF16) for _ in range(4)]
        oi = [0]
        rot = cst[0:64, C_ROT:C_ROT + 64]
        invf = cst[0:64, C_IF:C_IF + 1]

        def rope(xsrc_ps, dst_bf):
            P.cp(xr[:, :], xsrc_ps, eng="act")
            ps = P.ps()
            P.mm(ps[0:64, :], rot, xr[:, :])
            P.tt(r1[:, :], xr[:, :], cosv[:, :], ALU.mult)
            P.tt(r2[:, :], ps[0:64, :], sinv[:, :], ALU.mult)
            P.tt(dst_bf, r1[:, :], r2[:, :], ALU.add)

        for n in range(NT):
            n0 = n * 512
            tsl = slice(n0, n0 + 512)
            P.ld(cq[:, :, :], V(zT.h[MB:MB + 768, tsl].rearrange("(c p) t -> p c t", p=128), zT.key))
            P.ld(ckv[:, :, :], V(zT.h[MB + 768:MB + 1280, tsl].rearrange("(c p) t -> p c t", p=128), zT.key))
            P.ld(krf[:, :], zT[MB + 1280:MB + 1344, tsl])
            rmsnorm_tile(cq, 6, 512, qg, lambda c: cqn[:, c, :], Q_LORA, sq, rstd)
            rmsnorm_tile(ckv, 4, 512, kg, lambda c: ckvn[:, c, :], KV_LORA, sq, rstd)
            P.ld(posi[:, :], V(pos_in.h[0:1, tsl].partition_broadcast(64), pos_in.key))
            P.cp(ang[:, :], posi[:, :])
            P.tt(ang[:, :], ang[:, :], invf.bc([64, 512]), ALU.mult)
            for (dstv, offs) in ((sinv, 0.0), (cosv, 0.25)):
                P.ts(am[:, :], ang[:, :], 1.0 / TWO_PI, ALU.mult, offs, ALU.add)
                P.cp(posi[:, :], am[:, :])
                P.cp(r1[:, :], posi[:, :])
                P.tt(am[:, :], am[:, :], r1[:, :], ALU.subtract)
                P.act(dstv[:, :], am[:, :], AF.Sin, scale=6.283185)
            o = ob[oi[0] % 4]; oi[0] += 1
            ps = P.ps()
            P.mm(ps[0:64, :], ident_f[0:64, 0:64], krf[:, :])
            rope(ps[0:64, :], o[0:64, :])
            P.ld(vk(kr_s[:, tsl], n), o[0:64, :], q="pool")
            for h in range(8):
                ps = P.ps()
                for c in range(6):
                    P.mm(ps[:, :], wq[:, c, 192 * h:192 * h + 128], cqn[:, c, :], start=(c == 0), stop=(c == 5))
                o = ob[oi[0] % 4]; oi[0] += 1
                P.cp(o[:, :], ps[:, :], eng=evac_eng())
                P.ld(vk(qn_s[h, :, tsl], (h, n)), o[:, :], q="pool")
                ps = P.ps()
                for c in range(6):
                    P.mm(ps[0:64, :], wq[:, c, 192 * h + 128:192 * h + 192], cqn[:, c, :], start=(c == 0), stop=(c == 5))
                o = ob[oi[0] % 4]; oi[0] += 1
                rope(ps[0:64, :], o[0:64, :])
                P.ld(vk(qr_s[h, :, tsl], (h, n)), o[0:64, :], q="pool")
                ps = P.ps()
                for c in range(4):
                    P.mm(ps[:, :], wk[:, c, 256 * h:256 * h + 128], ckvn[:, c, :], start=(c == 0), stop=(c == 3))
                o = ob[oi[0] % 4]; oi[0] += 1
                P.cp(o[:, :], ps[:, :], eng=evac_eng())
                P.ld(vk(kn_s[h, :, tsl], (h, n)), o[:, :], q="pool")
            for j in range(4):
                for hb_ in range(2):
                    ps = P.ps()
                    for c in range(4):
                        rv = V(wk.h[:, c, :].rearrange("p (h two v) -> p h two v", two=2, v=128)[:, hb_ * 4:(hb_ + 1) * 4, 1, :], wk.key)
                        P.mm(ps[:, :].re("p (h v) -> p h v", h=4), ckvn[:, c, j * 128:(j + 1) * 128], rv,
                             start=(c == 0), stop=(c == 3))
                    o = ob[oi[0] % 4]; oi[0] += 1
                    P.cp(o[:, :], ps[:, :], eng=evac_eng())
                    P.ld(vk(v_s[n0 + j * 128:n0 + (j + 1) * 128, hb_ * 512:(hb_ + 1) * 512], (n, j, hb_)), o[:, :], q="pool")
        P.phase_end()

    def phase_M2(l):
        SCALE = 192.0 ** -0.5
        kr = P.sb("kr", [64, S], BF16)
        P.ld(kr[:, :], kr_s[:, :])
        qn = P.sb("qn", [128, S], BF16)
        qr = P.sb("qr", [64, S], BF16)
        kn = P.sb("kn", [128, S], BF16)
        vh = P.sb("vh", [128, NG, 128], BF16)
        pt = [P.sb("pt", [128, 512], BF16) for _ in range(3)]
        rden = P.sb("rden", [128, 512], F32)
        oo = [P.sb("oo", [128, 512], BF16) for _ in range(2)]
        pti = 0
        P.ps_pool = [0, 1, 2, 3, 4, 5]
        po, pd = P.psb[6], P.psb[7]
        for h in range(8):
            P.ld(qn[:, :], qn_s[h, :, :])
            P.ld(qr[:, :], qr_s[h, :, :])
            P.ld(kn[:, :], kn_s[h, :, :])
            P.ld(vh[:, :, :], V(v_s.h[:, h * 128:(h + 1) * 128].rearrange("(g p) v -> p g v", p=128), v_s.key))
            for n in range(NT):
                tsl = slice(n * 512, (n + 1) * 512)
                def scores(g_, tsl=tsl):
                    ks = slice(g_ * 128, (g_ + 1) * 128)
                    ps_ = P.ps()
                    P.mm(ps_[:, :], kn[:, ks], qn[:, tsl], start=True, stop=False)
                    P.mm(ps_[:, :], kr[:, ks], qr[:, tsl], start=False, stop=True)
                    return ps_
                ps_next = scores(0)
                for g in range(NG):
                    ps = ps_next
                    if g + 1 < NG:
                        ps_next = scores(g + 1)
                    p_ = pt[pti % 3]; pti += 1
                    P.act(p_[:, :], ps[:, :], AF.Exp, scale=SCALE)
                    P.mm(po[:, :], vh[:, g, :], p_[:, :], start=(g == 0), stop=(g == NG - 1))
                    P.mm(pd[:, :], ones_b, p_[:, :], start=(g == 0), stop=(g == NG - 1))
                P.op("dve", (lambda e, o=rden[:, :].ap, i=pd[:, :].ap: e.reciprocal(o, i)), reads=[pd.key], writes=[rden.key])
                o = oo[n % 2]
                P.tt(o[:, :], po[:, :], rden[:, :], ALU.mult)
                P.ld(vk(yT[2][h * 128:(h + 1) * 128, tsl], (h, n)), o[:, :], q="pool")
        P.ps_pool = list(range(8))
        P.phase_end()

    GB = RWKV_W + HGRN_W + MLA_W

    def phase_GF(l, last):
        g2 = P.sb("ln2g", [128, KC], F32)
        load_vec(g2, Wd["ln2_g"][l], KC)
        gfin = P.sb("fing", [128, KC], F32)
        load_vec(gfin, fin_g[:], KC)
        hb = P.sb("hb", [128, KC, 512], F32)
        hn2 = P.sb("hn2", [128, KC, 512], BF16)
        sq = P.sb("sq2", [128, KC, 512], BF16)
        rstd = P.sb("rstd2", [128, 512], F32)
        wbufs = [P.sb("wG", [128, 64, 128], BF16) for _ in range(2)]
        gz = [P.sb("gz", [128, 512], F32) for _ in range(3)]
        acc = P.sb("acc", [128, 512], F32)
        tmpf = [P.sb("tmpf", [128, 512], F32) for _ in range(2)]
        ptf = P.sb("ptf", [128, 4, PLE], F32)
        ptb = P.sb("ptb", [128, 2, 512], BF16)
        ot = P.sb("ot", [128, 512], F32)
        big0 = P.sb_off
        ys = [P.sb("ys%d" % n, [128, 8, 512], BF16) for n in range(3)]
        mixed = P.sb("mixed", [128, KC, 512], BF16)
        P.sb_off = big0
        h1 = P.sb("h1", [128, 64, 512], BF16)
        wi = [0]

        def wb():
            wi[0] += 1
            return wbufs[wi[0] % 2]

        for n in range(NT):
            tsl = slice(n * 512, (n + 1) * 512)
            P.ld(hb[:, :, :], V(hT.h[:, tsl].rearrange("(c p) t -> p c t", p=128), hT.key))
            for b in range(3):
                P.ld(ys[b][:, :, :], V(yT[b].h[:, tsl].rearrange("(c p) t -> p c t", p=128), yT[b].key))
            P.ld(ptf[:, :, :], V(p_in.h[l, tsl, :].rearrange("(j p) f -> p j f", p=128), p_in.key))
            for c in range(2):
                ps = P.ps()
                for j in range(4):
                    P.mm(ps[:, j * 128:(j + 1) * 128], ptf[:, j, c * 128:(c + 1) * 128], ident_f)
                P.cp(ptb[:, c, :], ps[:, :], eng=evac_eng())
            for m in range(KC):
                for b in range(3):
                    w = wb()
                    P.ld(w[:, 0:8, :], WS["w_b%d" % b][m, :, :, :])
                    P.ld(gz[b][:, :], zT[GB + b * 2048 + m * 128:GB + b * 2048 + (m + 1) * 128, tsl])
                    ps = P.ps()
                    for c in range(8):
                        P.mm(ps[:, :], w[:, c, :], ys[b][:, c, :], start=(c == 0), stop=(c == 7))
                    P.act(gz[b][:, :], gz[b][:, :], AF.Sigmoid)
                    if b == 0:
                        P.tt(acc[:, :], ps[:, :], gz[b][:, :], ALU.mult)
                    else:
                        t = tmpf[b % 2]
                        P.tt(t[:, :], ps[:, :], gz[b][:, :], ALU.mult)
                        if b == 1:
                            P.tt(acc[:, :], acc[:, :], t[:, :], ALU.add)
                        else:
                            P.tt(mixed[:, m, :], acc[:, :], t[:, :], ALU.add)
            for m in range(KC):
                w = wb()
                P.ld(w[:, 0:KC, :], WS["w_o"][m, :, :, :])
                ps = P.ps()
                for c in range(KC):
                    P.mm(ps[:, :], w[:, c, :], mixed[:, c, :], start=(c == 0), stop=(c == KC - 1))
                P.tt(hb[:, m, :], hb[:, m, :], ps[:, :], ALU.add)
            P.barrier()
            rmsnorm_tile(hb, KC, 512, g2, lambda c: hn2[:, c, :], D_MODEL, sq, rstd)
            for m in range(64):
                w = wb()
                P.ld(w[:, 0:KC, :], WS["w_mlp1"][m, :, :, :])
                ps = P.ps()
                for c in range(KC):
                    P.mm(ps[:, :], w[:, c, :], hn2[:, c, :], start=(c == 0), stop=(c == KC - 1))
                t = tmpf[m % 2]
                P.act(t[:, :], ps[:, :], AF.Relu)
                P.tt(h1[:, m, :], t[:, :], t[:, :], ALU.mult)
            for m in range(KC):
                w = wb()
                P.ld(w[:, :, :], WS["w_mlp2"][m, :, :, :])
                ps = P.ps()
                for c in range(64):
                    P.mm(ps[:, :], w[:, c, :], h1[:, c, :], start=(c == 0), stop=(c == 63))
                P.tt(hb[:, m, :], hb[:, m, :], ps[:, :], ALU.add)
            for c in range(KC):
                P.cp(hn2[:, c, :], hb[:, c, :], eng=evac_eng())
            for m in range(KC):
                w = wb()
                P.ld(w[:, 0:KC, :], WS["w_pg"][m, :, :, :])
                w2_ = wb()
                P.ld(w2_[:, 0:2, :], WS["w_pe"][m, :, :, :])
                ps = P.ps()
                for c in range(KC):
                    P.mm(ps[:, :], w[:, c, :], hn2[:, c, :], start=(c == 0), stop=(c == KC - 1))
                pe_ = P.ps()
                for c in range(2):
                    P.mm(pe_[:, :], w2_[:, c, :], ptb[:, c, :], start=(c == 0), stop=(c == 1))
                t = tmpf[m % 2]
                P.act(t[:, :], ps[:, :], AF.Sigmoid)
                P.tt(t[:, :], t[:, :], pe_[:, :], ALU.mult)
                P.tt(hb[:, m, :], hb[:, m, :], t[:, :], ALU.add)
            if not last:
                P.ld(vk(V(hT.h[:, tsl].rearrange("(c p) t -> p c t", p=128), hT.key), ("o", n)), hb[:, :, :], q="pool")
            else:
                ps = P.ps()
                for c in range(KC):
                    P.act(sq[:, c, :], hb[:, c, :], AF.Square)
                for c in range(KC):
                    P.mm(ps[:, :], ones_b, sq[:, c, :], start=(c == 0), stop=(c == KC - 1))
                P.act(rstd[:, :], ps[:, :], AF.Sqrt, bias=cst[:, C_EPS:C_EPS + 1], scale=1.0 / D_MODEL)
                P.recip(rstd[:, :], rstd[:, :])
                for c in range(KC):
                    P.stt(hb[:, c, :], hb[:, c, :], gfin[:, c:c + 1], rstd[:, :], ALU.mult, ALU.mult)
                for j in range(4):
                    for c4 in range(KC // 4):
                        ps = P.ps()
                        for q in range(4):
                            c = c4 * 4 + q
                            P.mm(ps[:, q * 128:(q + 1) * 128], hb[:, c, j * 128:(j + 1) * 128], ident_f)
                        P.cp(ot[:, :], ps[:, :], eng=evac_eng())
                        P.ld(vk(out_d[n * 512 + j * 128:n * 512 + (j + 1) * 128, c4 * 512:(c4 + 1) * 512], (n, j, c4)),
                             ot[:, :], q="pool")
            P.barrier()
        P.phase_end()

    phase_in()
    for l in range(L):
        if stop == "in":
            break
        precast_layer(l)
        phase_A(l)
        if stop == "A":
            break
        phase_R(l)
        if stop in ("R", "Rs", "Re", "Rt", "Rg", "Ri", "Ru", "Rx", "Rx1"):
            break
        phase_H(l)
        if stop == "H":
            break
        phase_M1(l)
        if stop == "M1":
            break
        phase_M2(l)
        if stop == "M2":
            break
        phase_GF(l, l == L - 1)
    P.emit()
    es.close()
    return nc, P


C_ID = 0
C_ONE = 128
C_RM1F = 256
C_RM1B = 768
C_RM2F = 1280
C_RM2B = 1408
C_TRIF = 1536
C_TRIB = 1664
C_HMF = 1792
C_HMB = 1920
C_CM = 2048
C_IF = 2052
C_PI = 2053
C_EPS = 2054
C_GEPS = 2055
C_ROT = 2056
CONST_W = 2128


def make_consts():
    c = np.zeros((128, CONST_W), np.float32)
    r = np.arange(128)[:, None]
    q = np.arange(128)[None, :]
    c[:, C_ID:C_ID + 128] = (r == q)
    c[:, C_ONE:C_ONE + 128] = 1.0
    su, iu = (r < q).astype(np.float32), (r <= q).astype(np.float32)
    sl, il = (r > q).astype(np.float32), (r >= q).astype(np.float32)
    c[:, C_RM1F:C_RM1F + 512] = np.concatenate([-su, -iu, su, iu], axis=1)
    c[:, C_RM1B:C_RM1B + 512] = np.concatenate([-sl, -il, sl, il], axis=1)
    c[:, C_RM2F:C_RM2F + 128] = -sl
    c[:, C_RM2B:C_RM2B + 128] = -su
    c[:, C_TRIF:C_TRIF + 128] = iu
    c[:, C_TRIB:C_TRIB + 128] = il
    same = ((r // 32) == (q // 32)).astype(np.float32)
    c[:, C_HMF:C_HMF + 128] = iu * same
    c[:, C_HMB:C_HMB + 128] = il * same
    c[:, C_CM:C_CM + 4] = ((np.arange(128)[:, None] // 32) == np.arange(4)[None, :])
    inv_freq = (1.0 / (np.float32(10000.0) ** (np.arange(0, 64, 2, dtype=np.float32) / np.float32(64)))).astype(np.float32)
    c[:, C_IF] = inv_freq[np.arange(128) % 32]
    c[:, C_PI] = np.float32(np.pi)
    c[:, C_EPS] = NORM_EPS
    c[:, C_GEPS] = GN_EPS
    R = np.zeros((64, 64), np.float32)
    for m in range(32):
        R[m, m + 32] = -1.0
        R[m + 32, m] = 1.0
    c[0:64, C_ROT:C_ROT + 64] = R.T
    return c


_CACHE = {}


def run(inputs, S, L, ncores):
    key = (S, L)
    if key not in _CACHE:
        _CACHE[key] = build(S, L)
    nc, P = _CACHE[key]
    consts = make_consts()
    in_maps = []
    for b in range(ncores):
        m = {"x": np.ascontiguousarray(inputs["x"][b]),
             "p": np.ascontiguousarray(inputs["p"][:, b]),
             "positions": np.ascontiguousarray(inputs["positions"][b:b + 1]).astype(np.int32),
             "final_g": np.ascontiguousarray(inputs["final_g"]),
             "consts": consts}
        for nm in W_SHAPES:
            m[nm] = np.ascontiguousarray(inputs[nm])
        in_maps.append(m)
    res = run_bass_kernel_spmd(nc, in_maps, core_ids=list(range(ncores)))
    return res


def kernel(**inputs):
    inputs = {k: np.asarray(v) for k, v in inputs.items()}
    B, S, _ = inputs["x"].shape
    L = inputs["w_in"].shape[0]
    res = run(inputs, S, L, B)
    out = np.stack([res.results[b]["out"] for b in range(B)], axis=0)
    return out.astype(np.float32)
```

```python
import numpy as np
from contextlib import ExitStack
import concourse.bass as bass
import concourse.mybir as mybir
from concourse.bass_utils import run_bass_kernel_spmd

F32 = mybir.dt.float32
BF16 = mybir.dt.bfloat16
I32 = mybir.dt.int32
AF = mybir.ActivationFunctionType
ALU = mybir.AluOpType
AX = mybir.AxisListType

D_MODEL = 2048
BW = 1024
LORA = 64
LORA_G = 160
RWKV_W = 3 * BW + 4 * LORA + LORA_G
HGRN_W = 5 * BW
Q_LORA = 768
KV_LORA = 512
ROPE = 64
MLA_W = Q_LORA + KV_LORA + ROPE
GATE_W = 3 * D_MODEL
IN_W = RWKV_W + HGRN_W + MLA_W + GATE_W
D_FF = 8192
PLE = 256
DECAY_SCALE = 0.606531
GN_EPS = 64e-5
NORM_EPS = 1e-6
F_TINY = 1e-30

DMA_K = 6


class Prog:
    CE = ("pe", "act", "dve", "pool")
    QS = ("sp", "pool")

    def __init__(self, nc, es):
        self.nc = nc
        self.es = es
        self.ops = {e: [] for e in ("pe", "act", "dve", "pool", "sp")}
        self.cnt = {e: 0 for e in self.CE}
        self.waited = {e: {} for e in self.ops}
        self.dma_n = {q: 0 for q in self.QS}
        self.bufs = {}
        self.sems = {}
        for e in self.CE:
            self.sems[e] = es.enter_context(nc.semaphore("s_" + e))
        for q in self.QS:
            for k in range(DMA_K):
                self.sems[(q, k)] = es.enter_context(nc.semaphore("d_%s%d" % (q, k)))
        self.n_ops = 0

    def _collect(self, reads, writes):
        deps = {}
        def add(ev):
            if ev is None:
                return
            sk, v = ev
            if deps.get(sk, 0) < v:
                deps[sk] = v
        for b in reads:
            st = self.bufs.get(b)
            if st is not None:
                add(st[0])
        for b in writes:
            st = self.bufs.get(b)
            if st is not None:
                add(st[0])
                for sk, v in st[1].items():
                    add((sk, v))
        return deps

    def _emit_waits(self, eng, deps):
        w = self.waited[eng]
        for sk, v in deps.items():
            if sk == eng and eng == "pe":
                continue
            if w.get(sk, 0) < v:
                w[sk] = v
                self.ops[eng].append(("wait", sk, v))

    def _update(self, ev, reads, writes):
        sk, v = ev
        for b in reads:
            st = self.bufs.get(b)
            if st is None:
                st = self.bufs[b] = [None, {}]
            if st[1].get(sk, 0) < v:
                st[1][sk] = v
        for b in writes:
            self.bufs[b] = [ev, {}]

    def op(self, eng, fn, reads=(), writes=()):
        deps = self._collect(reads, writes)
        self._emit_waits(eng, deps)
        self.cnt[eng] += 1
        ev = (eng, self.cnt[eng])
        self.ops[eng].append(("op", fn))
        self._update(ev, reads, writes)
        self.n_ops += 1

    def dma(self, q, out, in_, reads=(), writes=()):
        deps = self._collect(reads, writes)
        i = self.dma_n[q]
        self.dma_n[q] += 1
        slot = i % DMA_K
        val = 16 * (i // DMA_K + 1)
        if val > 16:
            deps[(q, slot)] = max(deps.get((q, slot), 0), val - 16)
        self._emit_waits(q, deps)
        ev = ((q, slot), val)
        self.ops[q].append(("dma", out, in_, (q, slot)))
        self._update(ev, reads, writes)
        self.n_ops += 1

    def barrier(self):
        deps = {}
        for e in self.CE:
            if self.cnt[e]:
                deps[e] = self.cnt[e]
        for q in self.QS:
            n = self.dma_n[q]
            for k in range(DMA_K):
                if n > k:
                    last = ((n - 1 - k) // DMA_K) * DMA_K + k
                    deps[(q, k)] = 16 * (last // DMA_K + 1)
        for e in self.ops:
            d = {sk: v for sk, v in deps.items() if not (sk == e)}
            self._emit_waits(e, d)

    def emit(self):
        nc = self.nc
        self.barrier()
        sems = self.sems
        with nc.Block() as blk:
            def replay(eng_name, e):
                for item in self.ops[eng_name]:
                    if item[0] == "wait":
                        e.wait_ge(sems[item[1]], item[2])
                    elif item[0] == "op":
                        item[1](e).then_inc(sems[eng_name], 1)
                    else:
                        e.dma_start(out=item[1], in_=item[2], allow_slow_non_contiguous=True).then_inc(sems[item[3]], 16)

            @blk.tensor
            def _(e):
                replay("pe", e)

            @blk.scalar
            def _(e):
                replay("act", e)

            @blk.vector
            def _(e):
                replay("dve", e)

            @blk.gpsimd
            def _(e):
                replay("pool", e)

            @blk.sync
            def _(e):
                replay("sp", e)


class V:
    __slots__ = ("ap", "key")

    def __init__(self, ap, key):
        self.ap = ap
        self.key = key

    def bc(self, shape):
        return V(self.ap.broadcast_to(list(shape)), self.key)

    def __getitem__(self, idx):
        return V(self.ap[idx], self.key)

    def re(self, s, **kw):
        return V(self.ap.rearrange(s, **kw), self.key)


class T:
    def __init__(self, h, key):
        self.h = h
        self.key = key

    def __getitem__(self, idx):
        return V(self.h[idx], self.key)

    def k(self, sub):
        return T(self.h, (self.key, sub))


def _dsize(dt):
    return 2 if dt == BF16 else 4


class K(Prog):
    def __init__(self, nc, es):
        super().__init__(nc, es)
        self.sb_off = 16640
        self.sb_base = 16640
        self.uid = 0
        self.psb = [T(es.enter_context(nc.psum_tensor("psb%d" % i, [128, 512], F32)), "psb%d" % i)
                    for i in range(8)]
        self.ps_i = 0
        self.ps_pool = list(range(8))

    def ps(self):
        t = self.psb[self.ps_pool[self.ps_i % len(self.ps_pool)]]
        self.ps_i += 1
        return t

    def sb(self, name, shape, dt):
        self.uid += 1
        n = 1
        for s in shape[1:]:
            n *= s
        nbytes = (n * _dsize(dt) + 63) // 64 * 64
        off = self.sb_off
        self.sb_off += nbytes
        assert self.sb_off <= 229000, ("SBUF overflow", name, self.sb_off)
        nm = "%s_%d" % (name, self.uid)
        h = self.nc.alloc_sbuf_tensor_at(nm, list(shape), dt, offset=off)
        return T(h, nm)

    def persist(self):
        self.sb_base = self.sb_off

    def phase_end(self):
        self.barrier()
        self.sb_off = self.sb_base

    def dram(self, name, shape, dt, kind="Internal"):
        h = self.nc.dram_tensor(name, list(shape), dt, kind=kind)
        return T(h.ap(), name)

    def mm(self, out, lhsT, rhs, start=True, stop=True):
        o, l, r = out.ap, lhsT.ap, rhs.ap
        self.op("pe", lambda e: e.matmul(o, l, r, start=start, stop=stop),
                reads=[lhsT.key, rhs.key], writes=[out.key])

    def act(self, out, in_, func, bias=None, scale=None, eng="act"):
        o, i = out.ap, in_.ap
        kw = {}
        rd = [in_.key]
        if bias is not None:
            if isinstance(bias, V):
                kw["bias"] = bias.ap
                rd.append(bias.key)
            else:
                kw["bias"] = bias
        if scale is not None:
            if isinstance(scale, V):
                kw["scale"] = scale.ap
                rd.append(scale.key)
            else:
                kw["scale"] = scale
        self.op("act", lambda e: e.activation(out=o, in_=i, func=func, **kw),
                reads=rd, writes=[out.key])

    def tt(self, out, in0, in1, op, eng="dve"):
        o, a, b = out.ap, in0.ap, in1.ap
        self.op(eng, lambda e: e.tensor_tensor(o, a, b, op),
                reads=[in0.key, in1.key], writes=[out.key])

    def ts(self, out, in0, s1, op0, s2=None, op1=None, eng="dve"):
        o, a = out.ap, in0.ap
        rd = [in0.key]
        if isinstance(s1, V):
            rd.append(s1.key)
            s1 = s1.ap
        if isinstance(s2, V):
            rd.append(s2.key)
            s2 = s2.ap
        if op1 is None:
            self.op(eng, lambda e: e.tensor_scalar(o, a, s1, None, op0), reads=rd, writes=[out.key])
        else:
            self.op(eng, lambda e: e.tensor_scalar(o, a, s1, s2, op0, op1), reads=rd, writes=[out.key])

    def stt(self, out, in0, scalar, in1, op0, op1, eng="dve"):
        o, a, b = out.ap, in0.ap, in1.ap
        rd = [in0.key, in1.key]
        if isinstance(scalar, V):
            rd.append(scalar.key)
            scalar = scalar.ap
        self.op(eng, lambda e: e.scalar_tensor_tensor(o, a, scalar, b, op0, op1),
                reads=rd, writes=[out.key])

    def cp(self, out, in_, eng="dve"):
        o, i = out.ap, in_.ap
        if eng == "act":
            self.op("act", lambda e: e.activation(out=o, in_=i, func=AF.Copy),
                    reads=[in_.key], writes=[out.key])
        else:
            self.op(eng, lambda e: e.tensor_copy(o, i), reads=[in_.key], writes=[out.key])

    def recip(self, out, in_):
        o, i = out.ap, in_.ap
        self.op("dve", lambda e: e.reciprocal(o, i), reads=[in_.key], writes=[out.key])

    def memset(self, out, val, eng="dve"):
        o = out.ap
        self.op(eng, lambda e: e.memset(o, val), writes=[out.key])

    def ld(self, out, in_, q="sp"):
        self.dma(q, out.ap, in_.ap, reads=[in_.key], writes=[out.key])


def vk(v, sub):
    return V(v.ap, (v.key, sub))


W_SHAPES = {
    "ln1_g": (D_MODEL,), "w_in": (D_MODEL, IN_W), "rwkv_mu": (2, RWKV_W), "rwkv_w0": (2, BW),
    "rwkv_w2": (2, LORA, BW), "rwkv_a0": (2, BW), "rwkv_a2": (2, LORA, BW), "rwkv_g2": (LORA_G, BW),
    "rwkv_kk": (BW,), "rwkv_ka": (BW,), "rwkv_rk": (16, 64), "rwkv_gn_w": (BW,), "rwkv_gn_b": (BW,),
    "hgrn_lb": (BW,), "hgrn_norm_g": (128,), "mla_q_norm_g": (Q_LORA,), "mla_kv_norm_g": (KV_LORA,),
    "mla_w_uq": (Q_LORA, 8 * 192), "mla_w_ukv": (KV_LORA, 8 * 256), "w_branch": (3, BW, D_MODEL),
    "w_o": (D_MODEL, D_MODEL), "ln2_g": (D_MODEL,), "w_mlp1": (D_MODEL, D_FF), "w_mlp2": (D_FF, D_MODEL),
    "w_pe": (PLE, D_MODEL), "w_pg": (D_MODEL, D_MODEL),
}


def build(S, L, dbg=()):
    import os as _os
    stop = _os.environ.get("MK_STOP", "")
    nc = bass.Bass("TRN2", target_bir_lowering=False)
    es = ExitStack()
    P = K(nc, es)
    NT = S // 512
    NG = S // 128
    KC = D_MODEL // 128

    x_in = P.dram("x", [S, D_MODEL], F32, "ExternalInput")
    p_in = P.dram("p", [L, S, PLE], F32, "ExternalInput")
    pos_in = P.dram("positions", [1, S], I32, "ExternalInput")
    Wd = {}
    for nm, shp in W_SHAPES.items():
        Wd[nm] = P.dram(nm, [L] + list(shp), F32, "ExternalInput")
    fin_g = P.dram("final_g", [D_MODEL], F32, "ExternalInput")
    cst_in = P.dram("consts", [128, CONST_W], F32, "ExternalInput")
    out_d = P.dram("out", [S, D_MODEL], F32, "ExternalOutput")
    dbg_out = {}

    hT = P.dram("hT", [D_MODEL, S], F32)
    zT = P.dram("zT", [IN_W, S], F32)
    yT = [P.dram("yT%d" % n, [BW, S], BF16) for n in range(3)]

    def wscr(name, Kdim, M):
        nm_ = (M + 127) // 128
        return P.dram("ws_" + name, [nm_, 128, Kdim // 128, 128], BF16)

    WS = {
        "w_in": wscr("w_in", D_MODEL, IN_W), "w_b0": wscr("w_b0", BW, D_MODEL),
        "w_b1": wscr("w_b1", BW, D_MODEL), "w_b2": wscr("w_b2", BW, D_MODEL),
        "w_o": wscr("w_o", D_MODEL, D_MODEL), "w_mlp1": wscr("w_mlp1", D_MODEL, D_FF),
        "w_mlp2": wscr("w_mlp2", D_FF, D_MODEL), "w_pe": wscr("w_pe", PLE, D_MODEL),
        "w_pg": wscr("w_pg", D_MODEL, D_MODEL),
    }

    cst = P.sb("cst", [128, CONST_W], F32)
    P.ld(cst[:, :], cst_in[:, :])
    cstb = P.sb("cstb", [128, CONST_W], BF16)
    P.cp(cstb[:, :], cst[:, :])
    ident_f = cst[:, C_ID:C_ID + 128]
    ident_b = cstb[:, C_ID:C_ID + 128]
    ones_b = cstb[:, C_ONE:C_ONE + 128]
    P.persist()
    rr = [0]

    def evac_eng():
        rr[0] += 1
        return "act" if rr[0] % 2 else "dve"

    def precast(src, Kdim, M, dst):
        kc_all = Kdim // 128
        nm_ = (M + 127) // 128
        stg, stb = pcb["f"], pcb["b"]
        it = 0
        for mi in range(nm_):
            m0 = mi * 128
            msz = min(128, M - m0)
            for k0 in range(0, kc_all, 16):
                kc = min(16, kc_all - k0)
                sf, sbb = stg[it % 3], stb[it % 3]
                it += 1
                srcv = V(src.ap[k0 * 128:(k0 + kc) * 128, m0:m0 + msz].rearrange("(c p) m -> p c m", p=128), src.key)
                P.ld(sf[:, 0:kc, 0:msz], srcv)
                P.cp(sbb[:, 0:kc, 0:msz], sf[:, 0:kc, 0:msz], eng=evac_eng())
                P.ld(vk(dst[mi, :, k0:k0 + kc, 0:msz], (mi, k0)), sbb[:, 0:kc, 0:msz], q="pool")

    pcb = {}

    def precast_layer(l):
        pcb["f"] = [P.sb("pc_f", [128, 16, 128], F32) for _ in range(3)]
        pcb["b"] = [P.sb("pc_b", [128, 16, 128], BF16) for _ in range(3)]
        precast(Wd["w_in"][l], D_MODEL, IN_W, WS["w_in"])
        for n in range(3):
            precast(Wd["w_branch"][l, n], BW, D_MODEL, WS["w_b%d" % n])
        precast(Wd["w_o"][l], D_MODEL, D_MODEL, WS["w_o"])
        precast(Wd["w_mlp1"][l], D_MODEL, D_FF, WS["w_mlp1"])
        precast(Wd["w_mlp2"][l], D_FF, D_MODEL, WS["w_mlp2"])
        precast(Wd["w_pe"][l], PLE, D_MODEL, WS["w_pe"])
        precast(Wd["w_pg"][l], D_MODEL, D_MODEL, WS["w_pg"])
        P.phase_end()

    def linear(xT, kc, wt, mtiles, ntiles, evac, wbufs):
        for j, (mi, msz) in enumerate(mtiles):
            wb = wbufs[j % len(wbufs)]
            P.ld(wb[:, 0:kc, :], wt[mi, :, :, :])
            for (n0, nsz) in ntiles:
                ps = P.ps()
                for c in range(kc):
                    P.mm(ps[0:msz, 0:nsz], wb[:, c, 0:msz], xT[:, c, n0:n0 + nsz],
                         start=(c == 0), stop=(c == kc - 1))
                evac(mi, msz, n0, nsz, ps)

    def rmsnorm_tile(src, kc, nsz, g_sb, dst, dim, sq_scr, rstd_scr):
        ps = P.ps()
        for c in range(kc):
            P.act(sq_scr[:, c, 0:nsz], src[:, c, 0:nsz], AF.Square)
        for c in range(kc):
            P.mm(ps[:, 0:nsz], ones_b, sq_scr[:, c, 0:nsz], start=(c == 0), stop=(c == kc - 1))
        P.act(rstd_scr[:, 0:nsz], ps[:, 0:nsz], AF.Sqrt, bias=cst[:, C_EPS:C_EPS + 1], scale=1.0 / dim)
        P.recip(rstd_scr[:, 0:nsz], rstd_scr[:, 0:nsz])
        for c in range(kc):
            P.stt(dst(c), src[:, c, 0:nsz], g_sb[:, c:c + 1], rstd_scr[:, 0:nsz], ALU.mult, ALU.mult,
                  eng="dve")

    def load_vec(dst, src_v, kc):
        P.ld(dst[:, 0:kc], V(src_v.ap.rearrange("(c p) -> p c", p=128), src_v.key))

    def phase_in():
        xs = [P.sb("xin", [128, D_MODEL], F32) for _ in range(2)]
        ho = [P.sb("hout", [128, 512], F32) for _ in range(4)]
        it = 0
        for g in range(NG):
            xt = xs[g % 2]
            P.ld(xt[:, :], x_in[g * 128:(g + 1) * 128, :])
            for c4 in range(KC // 4):
                ps = P.ps()
                for j in range(4):
                    c = c4 * 4 + j
                    P.mm(ps[:, j * 128:(j + 1) * 128], xt[:, c * 128:(c + 1) * 128], ident_f)
                o = ho[it % 4]
                it += 1
                P.cp(o[:, :], ps[:, :], eng=evac_eng())
                dstv = V(hT.h[c4 * 512:(c4 + 1) * 512, g * 128:(g + 1) * 128].rearrange("(j p) t -> p j t", p=128),
                         (hT.key, g, c4))
                P.ld(dstv, o[:, :].re("p (j t) -> p j t", j=4), q="pool")
        P.phase_end()

    def phase_A(l):
        g_sb = P.sb("ln1g", [128, KC], F32)
        load_vec(g_sb, Wd["ln1_g"][l], KC)
        ST = min(S, 2048)
        hn = P.sb("hn", [128, KC, ST], BF16)
        hb = P.sb("hb", [128, KC, 512], F32)
        sq = P.sb("sq", [128, KC, 512], BF16)
        rstd = P.sb("rstd", [128, 512], F32)
        wbufs = [P.sb("wA", [128, KC, 128], BF16) for _ in range(2)]
        stg = [P.sb("zst", [128, 512], F32) for _ in range(4)]
        cnt = [0]
        for s0 in range(0, S, ST):
            for n0 in range(0, ST, 512):
                P.ld(hb[:, :, :], V(hT.h[:, s0 + n0:s0 + n0 + 512].rearrange("(c p) t -> p c t", p=128), hT.key))
                rmsnorm_tile(hb, KC, 512, g_sb, lambda c, n0=n0: hn[:, c, n0:n0 + 512], D_MODEL, sq, rstd)

            def evac(mi, msz, n0, nsz, ps, s0=s0):
                o = stg[cnt[0] % 4]
                cnt[0] += 1
                P.cp(o[0:msz, 0:nsz], ps[0:msz, 0:nsz], eng=evac_eng())
                P.ld(vk(zT[mi * 128:mi * 128 + msz, s0 + n0:s0 + n0 + nsz], (mi, s0 + n0)), o[0:msz, 0:nsz], q="pool")

            mt = [(mi, min(128, IN_W - mi * 128)) for mi in range((IN_W + 127) // 128)]
            linear(hn, KC, WS["w_in"], mt, [(n0, 512) for n0 in range(0, ST, 512)], evac, wbufs)
        P.phase_end()

    yfT = P.dram("yfT", [BW, S], F32)
    H = 16

    def phase_R(l):
        def hv(name, src_v):
            t = P.sb(name, [64, H], F32)
            P.ld(t[:, :], V(src_v.ap.rearrange("(h c) -> c h", c=64), src_v.key))
            return t
        kkp = hv("kkp", Wd["rwkv_kk"][l])
        kap = hv("kap", Wd["rwkv_ka"][l])
        gnw = hv("gnw", Wd["rwkv_gn_w"][l])
        gnb = hv("gnb", Wd["rwkv_gn_b"][l])
        rkp = P.sb("rkp", [64, H], F32)
        P.ld(rkp[:, :], V(Wd["rwkv_rk"][l].ap.rearrange("h c -> c h"), Wd["rwkv_rk"].key))
        omka = P.sb("omka", [64, H], F32)
        P.ts(omka[:, :], kap[:, :], -1.0, ALU.mult, 1.0, ALU.add)
        tmka = P.sb("tmka", [64, H], F32)
        P.ts(tmka[:, :], kap[:, :], -2.0, ALU.mult, 2.0, ALU.add)
        mu = Wd["rwkv_mu"]
        def mut(name, parts, nb, lo, hi, which):
            t = P.sb(name, [parts, nb], F32)
            P.ld(t[:, :], V(mu.h[l, which, lo:hi].rearrange("(j c) -> c j", c=parts), mu.key))
            return t
        m0_rkv = mut("m0rkv", 64, 48, 0, 3072, 0)
        m1_rkv = mut("m1rkv", 64, 48, 0, 3072, 1)
        m0_lo = mut("m0lo", 64, 4, 3072, 3328, 0)
        m1_lo = mut("m1lo", 64, 4, 3072, 3328, 1)
        m0_g = mut("m0g", 80, 2, 3328, 3488, 0)
        m1_g = mut("m1g", 80, 2, 3328, 3488, 1)

        def c0of(name, a, b, parts, nb):
            t = P.sb(name, [parts, nb], F32)
            P.tt(t[:, :], a[:, :], b[:, :], ALU.add)
            P.ts(t[:, :], t[:, :], -1.0, ALU.mult, 1.0, ALU.add)
            return t
        c0_rkv = c0of("c0rkv", m0_rkv, m1_rkv, 64, 48)
        c0_lo = c0of("c0lo", m0_lo, m1_lo, 64, 4)
        c0_g = c0of("c0g", m0_g, m1_g, 80, 2)

        w2a = [P.sb("w2a%d" % d, [66, BW], BF16) for d in range(2)]
        a2a = [P.sb("a2a%d" % d, [66, BW], BF16) for d in range(2)]
        g2b = P.sb("g2b", [80, 2, BW], BF16)
        off0 = P.sb_off
        st = P.sb("augst", [64, BW], F32)
        br = P.sb("augbr", [1, BW], F32)
        hif = P.sb("aughif", [1, BW], F32)
        hi = P.sb("aughi", [1, BW], BF16)
        lo = P.sb("auglo", [1, BW], BF16)

        def aug(t, w_v, b_v):
            P.ld(st[:, :], w_v)
            P.cp(t[0:64, :], st[:, :])
            P.ld(br[:, :], V(b_v.ap.rearrange("(o n) -> o n", o=1), b_v.key))
            P.cp(hi[:, :], br[:, :])
            P.cp(hif[:, :], hi[:, :])
            P.tt(hif[:, :], br[:, :], hif[:, :], ALU.subtract)
            P.cp(lo[:, :], hif[:, :])
            P.ld(t[64:65, :], hi[:, :])
            P.ld(t[65:66, :], lo[:, :])
        for d in range(2):
            aug(w2a[d], Wd["rwkv_w2"][l, d], Wd["rwkv_w0"][l, d])
            aug(a2a[d], Wd["rwkv_a2"][l, d], Wd["rwkv_a0"][l, d])
        g2f = P.sb("g2f", [80, 2, BW], F32)
        P.ld(g2f[:, :, :], V(Wd["rwkv_g2"][l].ap.rearrange("(j p) n -> p j n", p=80), Wd["rwkv_g2"].key))
        P.cp(g2b[:, :, :], g2f[:, :, :])
        P.barrier()
        P.sb_off = off0

        zin = P.sb("zin", [64, 16, 130], F32)
        zlo = P.sb("zlo", [64, 4, 130], F32)
        zg = P.sb("zg", [80, 2, 130], F32)
        u = P.sb("u", [64, 48, 128], F32)
        ulo = P.sb("ulo", [64, 4, 128], F32)
        tlo = P.sb("tlo", [64, 4, 128], F32)
        ug = P.sb("ug", [80, 2, 128], F32)
        tg = P.sb("tg", [80, 2, 128], F32)
        wda = P.sb("wda", [66, 128], BF16)
        ada = [P.sb("ada%d" % d, [66, 128], BF16) for d in range(2)]
        sgd = P.sb("sgd", [80, 2, 128], BF16)
        P.memset(wda[64:66, :], 1.0)
        for d in range(2):
            P.memset(ada[d][64:66, :], 1.0)
        kk = P.sb("kk", [64, H, 128], F32)
        sqb = P.sb("sqb", [64, H, 128], BF16)
        s_tm = P.sb("s_tm", [128, BW], F32)
        E1 = P.sb("E1", [64, H, 128], F32)
        E2 = P.sb("E2", [64, H, 128], F32)
        av = [P.sb("a%d" % d, [64, H, 128], F32) for d in range(2)]
        kd = P.sb("kd", [64, H, 128], F32)
        be = P.sb("be", [64, H, 128], F32)
        tmp = be
        KR = P.sb("KR", [64, H, 256], BF16)
        BT = P.sb("BT", [64, H, 128], BF16)
        KT = P.sb("KT", [64, H, 128], BF16)
        NBH = P.sb("NBH", [64, H, 128], BF16)
        KH = P.sb("KH", [64, H, 128], BF16)
        VB = P.sb("VB", [64, H, 128], BF16)
        TM = P.sb("TM", [128, H, 320], BF16)
        G1 = [P.sb("G1", [128, 512], BF16) for _ in range(8)]
        G2 = [P.sb("G2", [128, 128], BF16) for _ in range(8)]
        LV = [[P.sb("LV", [128, 384], BF16) for _ in range(2)] for _ in range(8)]
        UM = [P.sb("UM", [128, 128], BF16) for _ in range(8)]
        RpT = [P.sb("RpT", [64, 128], BF16) for _ in range(8)]
        PT = [P.sb("PT", [64, 64], BF16) for _ in range(8)]
        dG = [P.sb("dG", [64, 64], F32) for _ in range(8)]
        WT = [P.sb("WT", [128, 128], BF16) for _ in range(8)]
        Tst = P.sb("Tst", [64, H, 64], BF16)
        yv = kk
        yf = E1
        yab = BT

        def bc_t(t2, nb):
            return V(t2.ap.unsqueeze(2).broadcast_to([t2.ap.shape[0], nb, 128]), t2.key)

        def shift(dst, src, tmpb, c0, m0, m1, nb, do=0, co=0):
            dv_ = dst[:, do:do + nb, :]
            tv_ = tmpb[:, 0:nb, :]
            P.tt(dv_, src[:, 0:nb, 1:129], bc_t(c0[:, co:co + nb], nb), ALU.mult)
            P.tt(tv_, src[:, 0:nb, 0:128], bc_t(m0[:, co:co + nb], nb), ALU.mult)
            P.tt(dv_, dv_, tv_, ALU.add)
            P.tt(tv_, src[:, 0:nb, 2:130], bc_t(m1[:, co:co + nb], nb), ALU.mult)
            P.tt(dv_, dv_, tv_, ALU.add)

        def load_halo(dst, r0, r1, parts, t0):
            lo = max(t0 - 1, 0)
            hi = min(t0 + 129, S)
            if t0 == 0:
                P.memset(dst[:, :, 0:1], 0.0)
            if t0 + 129 > S:
                P.memset(dst[:, :, 129:130], 0.0)
            P.ld(dst[:, :, lo - (t0 - 1):hi - (t0 - 1)],
                 V(zT.h[r0:r1, lo:hi].rearrange("(j c) t -> c j t", c=parts), zT.key))

        def rwkv_post(l, g, t0):
            P.cp(yab[:, :, :], yv[:, :, :], eng="act")
            for b4 in range(4):
                sl = slice(b4 * 4, (b4 + 1) * 4)
                ps = P.ps()
                P.mm(ps[0:64, :], ones_b[0:64, 0:64], yab[:, sl, :].re("p a b -> p (a b)"))
                P.stt(be[:, sl, :].re("p a b -> p (a b)"), ps[0:64, :], -1.0 / 64, yv[:, sl, :].re("p a b -> p (a b)"),
                      ALU.mult, ALU.add)
            P.act(sqb[:, :, :], be[:, :, :], AF.Square)
            for b4 in range(4):
                sl = slice(b4 * 4, (b4 + 1) * 4)
                ps = P.ps()
                P.mm(ps[0:64, :], ones_b[0:64, 0:64], sqb[:, sl, :].re("p a b -> p (a b)"))
                P.act(kd[:, sl, :].re("p a b -> p (a b)"), ps[0:64, :], AF.Ln, bias=cst[0:64, C_GEPS:C_GEPS + 1], scale=1.0 / 64)
            P.act(kd[:, :, :], kd[:, :, :], AF.Exp, scale=-0.5)
            P.tt(be[:, :, :], be[:, :, :], kd[:, :, :], ALU.mult)
            P.tt(be[:, :, :], be[:, :, :], bc_t(gnw[:, :], H), ALU.mult)
            P.tt(be[:, :, :], be[:, :, :], bc_t(gnb[:, :], H), ALU.add)
            P.tt(E2[:, :, :], av[0][:, :, :], av[1][:, :, :], ALU.add)
            P.tt(E2[:, :, :], E2[:, :, :], bc_t(kap[:, :], H), ALU.mult)
            P.tt(E2[:, :, :], E2[:, :, :], bc_t(tmka[:, :], H), ALU.add)
            P.tt(E2[:, :, :], E2[:, :, :], u[:, 16:32, :], ALU.mult)
            P.tt(E2[:, :, :], E2[:, :, :], u[:, 0:16, :], ALU.mult)
            P.tt(sqb[:, :, :], E2[:, :, :], bc_t(rkp[:, :], H), ALU.mult)
            for b4 in range(4):
                sl = slice(b4 * 4, (b4 + 1) * 4)
                ps = P.ps()
                P.mm(ps[0:64, :], ones_b[0:64, 0:64], sqb[:, sl, :].re("p a b -> p (a b)"))
                P.tt(kd[:, sl, :].re("p a b -> p (a b)"), ps[0:64, :], u[:, 32 + b4 * 4:32 + (b4 + 1) * 4, :].re("p a b -> p (a b)"),
                     ALU.mult)
            P.tt(be[:, :, :], be[:, :, :], kd[:, :, :], ALU.add)
            shift(ug, zg, tg, c0_g, m0_g, m1_g, 2)
            P.act(sgd[:, :, :], ug[:, :, :], AF.Sigmoid)
            for b4 in range(4):
                ps = P.ps()
                for j in range(4):
                    h = b4 * 4 + j
                    for q in range(2):
                        P.mm(ps[0:64, j * 128:(j + 1) * 128], g2b[:, q, h * 64:(h + 1) * 64], sgd[:, q, :],
                             start=(q == 0), stop=(q == 1))
                sl = slice(b4 * 4, (b4 + 1) * 4)
                P.tt(yab[:, sl, :].re("p a b -> p (a b)"), ps[0:64, :], be[:, sl, :].re("p a b -> p (a b)"), ALU.mult)
            P.ld(vk(V(yT[0].h[:, t0:t0 + 128].rearrange("(h c) t -> c h t", c=64), yT[0].key), g), yab[:, :, :], q="pool")

        if stop == "Rs":
            P.phase_end()
            return
        for d in range(2):
            P.memset(Tst[:, :, :], 0.0)
            mk1 = cst[:, (C_RM1F if d == 0 else C_RM1B):(C_RM1F if d == 0 else C_RM1B) + 512]
            mk2 = cst[:, (C_RM2F if d == 0 else C_RM2B):(C_RM2F if d == 0 else C_RM2B) + 128]
            tri = cst[:, (C_TRIF if d == 0 else C_TRIB):(C_TRIF if d == 0 else C_TRIB) + 128]
            last = 127 if d == 0 else 0
            order = range(NG) if d == 0 else range(NG - 1, -1, -1)
            for g in order:
                t0 = g * 128
                for part in range(3):
                    load_halo(zin, part * 1024, (part + 1) * 1024, 64, t0)
                    shift(u, zin, tmp, c0_rkv, m0_rkv, m1_rkv, 16, do=part * 16, co=part * 16)
                load_halo(zlo, 3072, 3328, 64, t0)
                load_halo(zg, 3328, 3488, 80, t0)
                shift(ulo, zlo, tlo, c0_lo, m0_lo, m1_lo, 4)
                r_ = lambda sl=slice(None): u[:, 0:16, sl]
                P.tt(kk[:, :, :], u[:, 16:32, :], bc_t(kkp[:, :], H), ALU.mult)
                P.act(sqb[:, :, :], kk[:, :, :], AF.Square)
                for b4 in range(4):
                    ps = P.ps()
                    P.mm(ps[0:64, :], ones_b[0:64, 0:64], sqb[:, b4 * 4:(b4 + 1) * 4, :].re("p a b -> p (a b)"))
                    P.ts(tmp[:, b4 * 4:(b4 + 1) * 4, :].re("p a b -> p (a b)"), ps[0:64, :], 1e-24, ALU.max)
                P.act(tmp[:, :, :], tmp[:, :, :], AF.Ln)
                P.act(tmp[:, :, :], tmp[:, :, :], AF.Exp, scale=-0.5)
                P.tt(kk[:, :, :], kk[:, :, :], tmp[:, 0:16, :], ALU.mult)
                P.act(wda[0:64, :], ulo[:, d, :], AF.Tanh)
                for dd in (range(2) if d == 1 else [d]):
                    P.cp(ada[dd][0:64, :], ulo[:, 2 + dd, :])
                for hf in range(2):
                    ps = P.ps()
                    P.mm(ps[:, :], wda[:, :], w2a[d][:, hf * 512:(hf + 1) * 512])
                    P.act(s_tm[:, hf * 512:(hf + 1) * 512], ps[:, :], AF.Sigmoid)
                for dd in (range(2) if d == 1 else [d]):
                    for b4 in range(4):
                        ps = P.ps()
                        for j in range(4):
                            h = b4 * 4 + j
                            P.mm(ps[0:64, j * 128:(j + 1) * 128], a2a[dd][:, h * 64:(h + 1) * 64], ada[dd][:, :])
                        P.act(av[dd][:, b4 * 4:(b4 + 1) * 4, :].re("p a b -> p (a b)"), ps[0:64, :], AF.Sigmoid)
                for b4 in range(4):
                    ps = P.ps()
                    for j in range(4):
                        h = b4 * 4 + j
                        P.mm(ps[0:64, j * 128:(j + 1) * 128], s_tm[:, h * 64:(h + 1) * 64], tri)
                    P.act(E1[:, b4 * 4:(b4 + 1) * 4, :].re("p a b -> p (a b)"), ps[0:64, :], AF.Exp, scale=-DECAY_SCALE)
                    P.act(E2[:, b4 * 4:(b4 + 1) * 4, :].re("p a b -> p (a b)"), ps[0:64, :], AF.Exp, scale=DECAY_SCALE)
                a = av[d]
                P.tt(kd[:, :, :], a[:, :, :], bc_t(kap[:, :], H), ALU.mult)
                P.tt(kd[:, :, :], kd[:, :, :], bc_t(omka[:, :], H), ALU.add)
                P.tt(kd[:, :, :], kd[:, :, :], u[:, 16:32, :], ALU.mult)
                P.tt(be[:, :, :], kk[:, :, :], a[:, :, :], ALU.mult)
                if d == 0:
                    P.tt(KR[:, :, 1:128], kk[:, :, 1:128], E1[:, :, 0:127], ALU.mult)
                    P.cp(KR[:, :, 0:1], kk[:, :, 0:1])
                else:
                    P.tt(KR[:, :, 0:127], kk[:, :, 0:127], E1[:, :, 1:128], ALU.mult)
                    P.cp(KR[:, :, 127:128], kk[:, :, 127:128])
                P.tt(KR[:, :, 128:256], u[:, 0:16, :], E1[:, :, :], ALU.mult)
                P.tt(be[:, :, :], be[:, :, :], E2[:, :, :], ALU.mult)
                P.cp(BT[:, :, :], be[:, :, :], eng="act")
                P.tt(kd[:, :, :], kd[:, :, :], E2[:, :, :], ALU.mult)
                P.cp(KT[:, :, :], kd[:, :, :], eng="act")
                gC = V(E1.h[:, :, last:last + 1].broadcast_to([64, H, 128]), E1.key)
                P.stt(NBH[:, :, :], be[:, :, :], -1.0, gC, ALU.mult, ALU.mult)
                P.tt(KH[:, :, :], kd[:, :, :], gC, ALU.mult)
                P.cp(VB[:, :, :], u[:, 32:48, :], eng="act")
                if stop == "Re":
                    continue
                for (srcT, off) in ((KR, 64), (NBH, 128), (KH, 192), (VB, 256)):
                    for b8 in range(2):
                        ps = P.ps()
                        for j in range(8):
                            h = b8 * 8 + j
                            P.mm(ps[:, j * 64:(j + 1) * 64], srcT[:, h, 0:128], ident_b[0:64, 0:64])
                        P.cp(TM[:, b8 * 8:(b8 + 1) * 8, off:off + 64],
                             ps[:, :].re("p (a b) -> p a b", a=8), eng=evac_eng())
                if stop == "Rt":
                    continue
                for b8 in range(2):
                    for q4 in range(2):
                        hs = [b8 * 8 + q4 * 4 + j for j in range(4)]
                        psA = [P.ps() for _ in range(4)]
                        psB = P.ps()
                        for j, h in enumerate(hs):
                            P.mm(psA[j][:, 0:256], BT[:, h, :], KR[:, h, :])
                            P.mm(psA[j][:, 256:512], KT[:, h, :], KR[:, h, :])
                            P.mm(psB[:, j * 128:(j + 1) * 128], KR[:, h, 0:128], BT[:, h, :])
                        for j, h in enumerate(hs):
                            jj = q4 * 4 + j
                            P.tt(G1[jj][:, :], psA[j][:, :], mk1, ALU.mult)
                            P.tt(G2[jj][:, :], psB[:, j * 128:(j + 1) * 128], mk2, ALU.mult, eng="dve")
                    cur = [None] * 8
                    for lev in range(1, 8):
                        pl = [P.ps() for _ in range(8)]
                        for jj in range(8):
                            if lev == 1:
                                Y, YT, Z = G1[jj][:, 0:128], G2[jj][:, :], ident_b
                            else:
                                c_ = cur[jj]
                                Y, YT, Z = c_[:, 0:128], c_[:, 128:256], c_[:, 256:384]
                            if lev < 7:
                                P.mm(pl[jj][:, 0:128], YT, Y)
                                P.mm(pl[jj][:, 128:256], Y, YT)
                            P.mm(pl[jj][:, 256:384], YT, Z, start=True, stop=False)
                            P.mm(pl[jj][:, 256:384], ident_b, Z, start=False, stop=True)
                        for jj in range(8):
                            nxt = LV[jj][lev % 2]
                            if lev < 7:
                                P.cp(nxt[:, :], pl[jj][:, 0:384], eng=evac_eng())
                            else:
                                P.cp(WT[jj][:, :], pl[jj][:, 256:384], eng=evac_eng())
                            cur[jj] = nxt
                    for q4 in range(2):
                        b4 = b8 * 2 + q4
                        hs = [b4 * 4 + j for j in range(4)]
                        J = [q4 * 4 + j for j in range(4)]
                        px = P.ps()
                        for j, h in enumerate(hs):
                            P.mm(px[:, j * 64:(j + 1) * 64], G1[J[j]][:, 256:384], TM[:, h, 256:320])
                        for j, h in enumerate(hs):
                            P.cp(TM[:, h, 0:64], px[:, j * 64:(j + 1) * 64], eng=evac_eng())
                        pu = P.ps()
                        for j, h in enumerate(hs):
                            P.mm(pu[:, j * 128:(j + 1) * 128], WT[J[j]][:, :], TM[:, h, 0:128])
                        for j, h in enumerate(hs):
                            P.cp(UM[J[j]][:, :], pu[:, j * 128:(j + 1) * 128], eng="dve")
                        pr = P.ps()
                        pp = P.ps()
                        for j, h in enumerate(hs):
                            P.mm(pr[0:64, j * 128:(j + 1) * 128], UM[J[j]][:, 64:128], G1[J[j]][:, 128:256], start=True, stop=False)
                            P.mm(pr[0:64, j * 128:(j + 1) * 128], ident_b[0:64, 0:64], KR[:, h, 128:256], start=False, stop=True)
                            P.mm(pp[0:64, j * 64:(j + 1) * 64], UM[J[j]][:, 64:128], TM[:, h, 128:192])
                        for j, h in enumerate(hs):
                            P.cp(RpT[J[j]][:, :], pr[0:64, j * 128:(j + 1) * 128], eng="act")
                            P.stt(dG[J[j]][:, :], ident_f[0:64, 0:64], E1[:, h, last:last + 1], ident_f[0:64, 0:64], ALU.mult, ALU.mult)
                            P.tt(PT[J[j]][:, :], pp[0:64, j * 64:(j + 1) * 64], dG[J[j]][:, :], ALU.add)
                        py = P.ps()
                        pt = P.ps()
                        for j, h in enumerate(hs):
                            yo = py[0:64, j * 128:(j + 1) * 128]
                            P.mm(yo, UM[J[j]][:, 0:64], G1[J[j]][:, 128:256], start=True, stop=False)
                            P.mm(yo, TM[:, h, 256:320], G1[J[j]][:, 384:512], start=False, stop=False)
                            P.mm(yo, Tst[:, h, :], RpT[J[j]][:, :], start=False, stop=True)
                            to = pt[0:64, j * 64:(j + 1) * 64]
                            P.mm(to, TM[:, h, 128:192], UM[J[j]][:, 0:64], start=True, stop=False)
                            P.mm(to, TM[:, h, 192:256], TM[:, h, 256:320], start=False, stop=False)
                            P.mm(to, PT[J[j]][:, :], Tst[:, h, :], start=False, stop=True)
                        P.cp(yv[:, b4 * 4:(b4 + 1) * 4, :].re("p a b -> p (a b)"), py[0:64, :], eng="act")
                        P.cp(Tst[:, b4 * 4:(b4 + 1) * 4, :].re("p a b -> p (a b)"), pt[0:64, 0:256], eng="dve")
                if stop in ("Rg", "Ri", "Ru", "Rx", "Rx1"):
                    continue
                if d == 0:
                    P.ld(vk(V(yfT.h[:, t0:t0 + 128].rearrange("(h c) t -> c h t", c=64), yfT.key), g), yv[:, :, :], q="pool")
                else:
                    P.ld(yf[:, :, :], V(yfT.h[:, t0:t0 + 128].rearrange("(h c) t -> c h t", c=64), yfT.key))
                    P.tt(yv[:, :, :], yv[:, :, :], yf[:, :, :], ALU.add)
                    rwkv_post(l, g, t0)
            P.barrier()
        P.phase_end()

    yhfT = P.dram("yhfT", [BW, S], F32)
    HB = RWKV_W

    def phase_H(l):
        HH = 8
        lball = P.sb("lball", [128, HH, L], F32)
        for i in range(L):
            P.ld(lball[:, :, i], V(Wd["hgrn_lb"].h[i].rearrange("(h c) -> c h", c=128), Wd["hgrn_lb"].key))
        P.act(lball[:, :, :], lball[:, :, :], AF.Exp)
        den = P.sb("lbden", [128, HH], F32)
        P.cp(den[:, :], lball[:, :, 0])
        for i in range(1, L):
            P.tt(den[:, :], den[:, :], lball[:, :, i], ALU.add)
        P.op("dve", (lambda e, o=den[:, :].ap: e.reciprocal(o, o)), reads=[den.key], writes=[den.key])
        lbv = P.sb("lbv", [128, HH], F32)
        P.memset(lbv[:, :], 0.0)
        for i in range(1, l + 1):
            P.tt(lbv[:, :], lbv[:, :], lball[:, :, i], ALU.add)
        P.tt(lbv[:, :], lbv[:, :], den[:, :], ALU.mult)
        oml = P.sb("oml", [128, HH], F32)
        P.ts(oml[:, :], lbv[:, :], -1.0, ALU.mult, 1.0, ALU.add)
        ng = P.sb("hng", [128, 1], F32)
        P.ld(ng[:, :], V(Wd["hgrn_norm_g"][l].ap.rearrange("(c o) -> c o", o=1), Wd["hgrn_norm_g"].key))

        def T3(name, dt):
            return P.sb(name, [128, HH, 128], dt)
        qf, zf, vf, gf = T3("qf", F32), T3("zf", F32), T3("vf", F32), T3("gf", F32)
        t1, t2 = T3("t1", F32), T3("t2", F32)
        lf_tm = T3("lf_tm", F32)
        E1, E2 = T3("hE1", F32), T3("hE2", F32)
        QT, KT, KH, VB = T3("hQT", BF16), T3("hKT", BF16), T3("hKH", BF16), T3("hVB", BF16)
        VT, KHT = T3("hVT", BF16), T3("hKHT", BF16)
        Vexp = P.sb("Vexp", [128, HH, 4, 128], BF16)
        Sc4 = [P.sb("Sc4", [128, 4, 128], BF16) for _ in range(2)]
        Tf = T3("hTf", F32)
        Tb = T3("hTb", BF16)
        yv = T3("hyv", F32)
        yf = T3("hyf", F32)
        sqb = T3("hsq", BF16)
        yob = T3("hyob", BF16)

        def bc_t(t2_, nb):
            return V(t2_.ap.unsqueeze(2).broadcast_to([128, nb, 128]), t2_.key)

        def rows(r0, t0):
            return V(zT.h[HB + r0:HB + r0 + 1024, t0:t0 + 128].rearrange("(h c) t -> c h t", c=128), zT.key)

        for d in range(2):
            P.memset(Tf[:, :, :], 0.0)
            P.memset(Tb[:, :, :], 0.0)
            hm = cst[:, (C_HMF if d == 0 else C_HMB):(C_HMF if d == 0 else C_HMB) + 128]
            order = range(NG) if d == 0 else range(NG - 1, -1, -1)
            corder = range(4) if d == 0 else range(3, -1, -1)
            lastoff = 31 if d == 0 else 0
            for g in order:
                t0 = g * 128
                P.ld(qf[:, :, :], rows(0, t0))
                P.ld(zf[:, :, :], rows(1024 * (1 + d), t0))
                P.ld(vf[:, :, :], rows(3072, t0))
                P.act(qf[:, :, :], qf[:, :, :], AF.Silu)
                P.act(t1[:, :, :], zf[:, :, :], AF.Sigmoid)
                P.tt(t1[:, :, :], t1[:, :, :], bc_t(oml[:, :], HH), ALU.mult)
                P.tt(t2[:, :, :], t1[:, :, :], bc_t(lbv[:, :], HH), ALU.add)
                P.ts(t2[:, :, :], t2[:, :, :], F_TINY, ALU.max)
                P.act(t2[:, :, :], t2[:, :, :], AF.Ln)
                P.stt(t1[:, :, :], t1[:, :, :], -1.0, bc_t(oml[:, :], HH), ALU.mult, ALU.add)
                for b in range(2):
                    ps = P.ps()
                    for j in range(4):
                        h = b * 4 + j
                        P.mm(ps[:, j * 128:(j + 1) * 128], t2[:, h, :], ident_f)
                    P.cp(lf_tm[:, b * 4:(b + 1) * 4, :].re("p a b -> p (a b)"), ps[:, :], eng=evac_eng())
                for b in range(2):
                    ps = P.ps()
                    for j in range(4):
                        h = b * 4 + j
                        P.mm(ps[:, j * 128:(j + 1) * 128], lf_tm[:, h, :], hm)
                    P.act(E1[:, b * 4:(b + 1) * 4, :].re("p a b -> p (a b)"), ps[:, :], AF.Exp)
                    P.act(E2[:, b * 4:(b + 1) * 4, :].re("p a b -> p (a b)"), ps[:, :], AF.Exp, scale=-1.0)
                P.tt(QT[:, :, :], qf[:, :, :], E1[:, :, :], ALU.mult)
                P.tt(t1[:, :, :], t1[:, :, :], E2[:, :, :], ALU.mult)
                P.cp(KT[:, :, :], t1[:, :, :], eng="act")
                gC = V(E1.h[:, :, :].rearrange("p h (c t) -> p h c t", t=32)[:, :, :, lastoff:lastoff + 1]
                       .broadcast_to([128, HH, 4, 32]), E1.key)
                P.tt(KH[:, :, :].re("p h (c t) -> p h c t", t=32), t1[:, :, :].re("p h (c t) -> p h c t", t=32), gC, ALU.mult)
                P.cp(VB[:, :, :], vf[:, :, :], eng="act")
                for (srcT, dstT) in ((VB, VT), (KH, KHT)):
                    for b in range(2):
                        ps = P.ps()
                        for j in range(4):
                            h = b * 4 + j
                            P.mm(ps[:, j * 128:(j + 1) * 128], srcT[:, h, :], ident_b)
                        P.cp(dstT[:, b * 4:(b + 1) * 4, :].re("p a b -> p (a b)"), ps[:, :], eng=evac_eng())
                cmv = cst[:, C_CM:C_CM + 4]
                P.tt(Vexp[:, :, :, :],
                     V(VT.h[:, :, :].unsqueeze(2).broadcast_to([128, HH, 4, 128]), VT.key),
                     V(cmv.ap.unsqueeze(1).unsqueeze(3).broadcast_to([128, HH, 4, 128]), cmv.key), ALU.mult)
                for half in range(2):
                    hs = [half * 4 + j for j in range(4)]
                    psc = P.ps()
                    for j, h in enumerate(hs):
                        P.mm(psc[:, j * 128:(j + 1) * 128], KT[:, h, :], QT[:, h, :])
                    sc4 = Sc4[half]
                    P.tt(sc4[:, :, :], psc[:, :].re("p (a b) -> p a b", a=4),
                         V(hm.ap.unsqueeze(1).broadcast_to([128, 4, 128]), hm.key), ALU.mult)
                    pqs = []
                    for j, h in enumerate(hs):
                        pq = P.ps()
                        P.mm(pq[:, :], KHT[:, h, :], Vexp[:, h, :, :].re("p c v -> p (c v)"))
                        pqs.append(pq)
                    py = P.ps()
                    for c4 in corder:
                        cs_ = slice(c4 * 32, (c4 + 1) * 32)
                        tc_ = c4 * 32 + lastoff
                        for j, h in enumerate(hs):
                            po_ = py[:, j * 128 + c4 * 32:j * 128 + (c4 + 1) * 32]
                            P.mm(po_, VT[:, h, :], sc4[:, j, cs_], start=True, stop=False)
                            P.mm(po_, Tb[:, h, :], QT[:, h, cs_], start=False, stop=True)
                        for j, h in enumerate(hs):
                            P.stt(Tf[:, h, :], Tf[:, h, :], E1[:, h, tc_:tc_ + 1], pqs[j][:, c4 * 128:(c4 + 1) * 128],
                                  ALU.mult, ALU.add)
                            P.cp(Tb[:, h, :], Tf[:, h, :], eng="act")
                    P.cp(yv[:, half * 4:(half + 1) * 4, :].re("p a b -> p (a b)"), py[:, :], eng="act")
                dv = V(yhfT.h[:, t0:t0 + 128].rearrange("(h c) t -> c h t", c=128), yhfT.key)
                if d == 0:
                    P.ld(vk(dv, g), yv[:, :, :], q="pool")
                else:
                    P.ld(yf[:, :, :], dv)
                    P.ld(gf[:, :, :], rows(4096, t0))
                    P.tt(yv[:, :, :], yv[:, :, :], yf[:, :, :], ALU.add)
                    P.act(sqb[:, :, :], yv[:, :, :], AF.Square)
                    for b in range(2):
                        ps = P.ps()
                        P.mm(ps[:, :], ones_b, sqb[:, b * 4:(b + 1) * 4, :].re("p a b -> p (a b)"))
                        P.act(t2[:, b * 4:(b + 1) * 4, :].re("p a b -> p (a b)"), ps[:, :], AF.Ln, bias=cst[:, C_EPS:C_EPS + 1], scale=1.0 / 128)
                    P.act(t2[:, :, :], t2[:, :, :], AF.Exp, scale=-0.5)
                    P.stt(yv[:, :, :], yv[:, :, :], ng[:, 0:1], t2[:, :, :], ALU.mult, ALU.mult)
                    P.act(gf[:, :, :], gf[:, :, :], AF.Silu)
                    P.tt(yob[:, :, :], yv[:, :, :], gf[:, :, :], ALU.mult)
                    P.ld(vk(V(yT[1].h[:, t0:t0 + 128].rearrange("(h c) t -> c h t", c=128), yT[1].key), g), yob[:, :, :], q="pool")
            P.barrier()
        P.phase_end()

    MB = RWKV_W + HGRN_W
    qn_s = P.dram("qn_s", [8, 128, S], BF16)
    qr_s = P.dram("qr_s", [8, 64, S], BF16)
    kn_s = P.dram("kn_s", [8, 128, S], BF16)
    kr_s = P.dram("kr_s", [64, S], BF16)
    v_s = P.dram("v_s", [S, 1024], BF16)
    TWO_PI = 6.283185307179586
    PI = 3.141592653589793

    def phase_M1(l):
        wqf = P.sb("wqf", [128, 6, 1536], F32)
        P.ld(wqf[:, :, :], V(Wd["mla_w_uq"][l].ap.rearrange("(c p) m -> p c m", p=128), Wd["mla_w_uq"].key))
        wq = P.sb("wq", [128, 6, 1536], BF16)
        P.cp(wq[:, 0:3, :], wqf[:, 0:3, :])
        P.cp(wq[:, 3:6, :], wqf[:, 3:6, :], eng="act")
        wkf = P.sb("wkf", [128, 4, 2048], F32)
        P.ld(wkf[:, :, :], V(Wd["mla_w_ukv"][l].ap.rearrange("(c p) m -> p c m", p=128), Wd["mla_w_ukv"].key))
        wk = P.sb("wk", [128, 4, 2048], BF16)
        P.cp(wk[:, 0:2, :], wkf[:, 0:2, :])
        P.cp(wk[:, 2:4, :], wkf[:, 2:4, :], eng="act")
        qg = P.sb("qg", [128, 6], F32)
        load_vec(qg, Wd["mla_q_norm_g"][l], 6)
        kg = P.sb("kg", [128, 4], F32)
        load_vec(kg, Wd["mla_kv_norm_g"][l], 4)
        cq = P.sb("cq", [128, 6, 512], F32)
        ckv = P.sb("ckv", [128, 4, 512], F32)
        krf = P.sb("krf", [64, 512], F32)
        cqn = P.sb("cqn", [128, 6, 512], BF16)
        ckvn = P.sb("ckvn", [128, 4, 512], BF16)
        sq = P.sb("msq", [128, 6, 512], BF16)
        rstd = P.sb("mrstd", [128, 512], F32)
        posi = P.sb("posi", [64, 512], I32)
        ang = P.sb("ang", [64, 512], F32)
        am = P.sb("am", [64, 512], F32)
        cosv = P.sb("cosv", [64, 512], F32)
        sinv = P.sb("sinv", [64, 512], F32)
        xr = P.sb("xr", [64, 512], F32)
        r1 = P.sb("r1", [64, 512], F32)
        r2 = P.sb("r2", [64, 512], F32)
        ob = [P.sb("mob", [128, 512], BF16) for _ in range(4)]
        oi = [0]
        rot = cst[0:64, C_ROT:C_ROT + 64]
        invf = cst[0:64, C_IF:C_IF + 1]

        def rope(xsrc_ps, dst_bf):
            P.cp(xr[:, :], xsrc_ps, eng="act")
            ps = P.ps()
            P.mm(ps[0:64, :], rot, xr[:, :])
            P.tt(r1[:, :], xr[:, :], cosv[:, :], ALU.mult)
            P.tt(r2[:, :], ps[0:64, :], sinv[:, :], ALU.mult)
            P.tt(dst_bf, r1[:, :], r2[:, :], ALU.add)

        for n in range(NT):
            n0 = n * 512
            tsl = slice(n0, n0 + 512)
            P.ld(cq[:, :, :], V(zT.h[MB:MB + 768, tsl].rearrange("(c p) t -> p c t", p=128), zT.key))
            P.ld(ckv[:, :, :], V(zT.h[MB + 768:MB + 1280, tsl].rearrange("(c p) t -> p c t", p=128), zT.key))
            P.ld(krf[:, :], zT[MB + 1280:MB + 1344, tsl])
            rmsnorm_tile(cq, 6, 512, qg, lambda c: cqn[:, c, :], Q_LORA, sq, rstd)
            rmsnorm_tile(ckv, 4, 512, kg, lambda c: ckvn[:, c, :], KV_LORA, sq, rstd)
            P.ld(posi[:, :], V(pos_in.h[0:1, tsl].partition_broadcast(64), pos_in.key))
            P.cp(ang[:, :], posi[:, :])
            P.tt(ang[:, :], ang[:, :], invf.bc([64, 512]), ALU.mult)
            for (dstv, offs) in ((sinv, 0.0), (cosv, 0.25)):
                P.ts(am[:, :], ang[:, :], 1.0 / TWO_PI, ALU.mult, offs, ALU.add)
                P.cp(posi[:, :], am[:, :])
                P.cp(r1[:, :], posi[:, :])
                P.tt(am[:, :], am[:, :], r1[:, :], ALU.subtract)
                P.act(dstv[:, :], am[:, :], AF.Sin, scale=6.283185)
            o = ob[oi[0] % 4]; oi[0] += 1
            ps = P.ps()
            P.mm(ps[0:64, :], ident_f[0:64, 0:64], krf[:, :])
            rope(ps[0:64, :], o[0:64, :])
            P.ld(vk(kr_s[:, tsl], n), o[0:64, :], q="pool")
            for h in range(8):
                ps = P.ps()
                for c in range(6):
                    P.mm(ps[:, :], wq[:, c, 192 * h:192 * h + 128], cqn[:, c, :], start=(c == 0), stop=(c == 5))
                o = ob[oi[0] % 4]; oi[0] += 1
                P.cp(o[:, :], ps[:, :], eng=evac_eng())
                P.ld(vk(qn_s[h, :, tsl], (h, n)), o[:, :], q="pool")
                ps = P.ps()
                for c in range(6):
                    P.mm(ps[0:64, :], wq[:, c, 192 * h + 128:192 * h + 192], cqn[:, c, :], start=(c == 0), stop=(c == 5))
                o = ob[oi[0] % 4]; oi[0] += 1
                rope(ps[0:64, :], o[0:64, :])
                P.ld(vk(qr_s[h, :, tsl], (h, n)), o[0:64, :], q="pool")
                ps = P.ps()
                for c in range(4):
                    P.mm(ps[:, :], wk[:, c, 256 * h:256 * h + 128], ckvn[:, c, :], start=(c == 0), stop=(c == 3))
                o = ob[oi[0] % 4]; oi[0] += 1
                P.cp(o[:, :], ps[:, :], eng=evac_eng())
                P.ld(vk(kn_s[h, :, tsl], (h, n)), o[:, :], q="pool")
            for j in range(4):
                for hb_ in range(2):
                    ps = P.ps()
                    for c in range(4):
                        rv = V(wk.h[:, c, :].rearrange("p (h two v) -> p h two v", two=2, v=128)[:, hb_ * 4:(hb_ + 1) * 4, 1, :], wk.key)
                        P.mm(ps[:, :].re("p (h v) -> p h v", h=4), ckvn[:, c, j * 128:(j + 1) * 128], rv,
                             start=(c == 0), stop=(c == 3))
                    o = ob[oi[0] % 4]; oi[0] += 1
                    P.cp(o[:, :], ps[:, :], eng=evac_eng())
                    P.ld(vk(v_s[n0 + j * 128:n0 + (j + 1) * 128, hb_ * 512:(hb_ + 1) * 512], (n, j, hb_)), o[:, :], q="pool")
        P.phase_end()

    def phase_M2(l):
        SCALE = 192.0 ** -0.5
        kr = P.sb("kr", [64, S], BF16)
        P.ld(kr[:, :], kr_s[:, :])
        qn = P.sb("qn", [128, S], BF16)
        qr = P.sb("qr", [64, S], BF16)
        kn = P.sb("kn", [128, S], BF16)
        vh = P.sb("vh", [128, NG, 128], BF16)
        pt = [P.sb("pt", [128, 512], BF16) for _ in range(3)]
        rden = P.sb("rden", [128, 512], F32)
        oo = [P.sb("oo", [128, 512], BF16) for _ in range(2)]
        pti = 0
        P.ps_pool = [0, 1, 2, 3, 4, 5]
        po, pd = P.psb[6], P.psb[7]
        for h in range(8):
            P.ld(qn[:, :], qn_s[h, :, :])
            P.ld(qr[:, :], qr_s[h, :, :])
            P.ld(kn[:, :], kn_s[h, :, :])
            P.ld(vh[:, :, :], V(v_s.h[:, h * 128:(h + 1) * 128].rearrange("(g p) v -> p g v", p=128), v_s.key))
            for n in range(NT):
                tsl = slice(n * 512, (n + 1) * 512)
                def scores(g_, tsl=tsl):
                    ks = slice(g_ * 128, (g_ + 1) * 128)
                    ps_ = P.ps()
                    P.mm(ps_[:, :], kn[:, ks], qn[:, tsl], start=True, stop=False)
                    P.mm(ps_[:, :], kr[:, ks], qr[:, tsl], start=False, stop=True)
                    return ps_
                ps_next = scores(0)
                for g in range(NG):
                    ps = ps_next
                    if g + 1 < NG:
                        ps_next = scores(g + 1)
                    p_ = pt[pti % 3]; pti += 1
                    P.act(p_[:, :], ps[:, :], AF.Exp, scale=SCALE)
                    P.mm(po[:, :], vh[:, g, :], p_[:, :], start=(g == 0), stop=(g == NG - 1))
                    P.mm(pd[:, :], ones_b, p_[:, :], start=(g == 0), stop=(g == NG - 1))
                P.op("dve", (lambda e, o=rden[:, :].ap, i=pd[:, :].ap: e.reciprocal(o, i)), reads=[pd.key], writes=[rden.key])
                o = oo[n % 2]
                P.tt(o[:, :], po[:, :], rden[:, :], ALU.mult)
                P.ld(vk(yT[2][h * 128:(h + 1) * 128, tsl], (h, n)), o[:, :], q="pool")
        P.ps_pool = list(range(8))
        P.phase_end()

    GB = RWKV_W + HGRN_W + MLA_W

    def phase_GF(l, last):
        g2 = P.sb("ln2g", [128, KC], F32)
        load_vec(g2, Wd["ln2_g"][l], KC)
        gfin = P.sb("fing", [128, KC], F32)
        load_vec(gfin, fin_g[:], KC)
        hb = P.sb("hb", [128, KC, 512], F32)
        hn2 = P.sb("hn2", [128, KC, 512], BF16)
        sq = P.sb("sq2", [128, KC, 512], BF16)
        rstd = P.sb("rstd2", [128, 512], F32)
        wbig = [P.sb("wGb", [128, 64, 128], BF16) for _ in range(2)]
        wbufs = [P.sb("wG", [128, KC, 128], BF16) for _ in range(3)]
        gz = [P.sb("gz", [128, 512], F32) for _ in range(3)]
        acc = P.sb("acc", [128, 512], F32)
        tmpf = [P.sb("tmpf", [128, 512], F32) for _ in range(2)]
        ptf = P.sb("ptf", [128, 4, PLE], F32)
        ptb = P.sb("ptb", [128, 2, 512], BF16)
        ot = P.sb("ot", [128, 512], F32)
        big0 = P.sb_off
        ys = [P.sb("ys%d" % n, [128, 8, 512], BF16) for n in range(3)]
        mixed = P.sb("mixed", [128, KC, 512], BF16)
        P.sb_off = big0
        h1 = P.sb("h1", [128, 64, 512], BF16)
        wi = [0]

        def wb():
            wi[0] += 1
            return wbufs[wi[0] % 3]

        def wq():
            return "sp" if wi[0] % 2 else "pool"

        for n in range(NT):
            tsl = slice(n * 512, (n + 1) * 512)
            P.ld(hb[:, :, :], V(hT.h[:, tsl].rearrange("(c p) t -> p c t", p=128), hT.key))
            for b in range(3):
                P.ld(ys[b][:, :, :], V(yT[b].h[:, tsl].rearrange("(c p) t -> p c t", p=128), yT[b].key))
            P.ld(ptf[:, :, :], V(p_in.h[l, tsl, :].rearrange("(j p) f -> p j f", p=128), p_in.key))
            for c in range(2):
                ps = P.ps()
                for j in range(4):
                    P.mm(ps[:, j * 128:(j + 1) * 128], ptf[:, j, c * 128:(c + 1) * 128], ident_f)
                P.cp(ptb[:, c, :], ps[:, :], eng=evac_eng())
            for m in range(KC):
                for b in range(3):
                    w = wb()
                    P.ld(w[:, 0:8, :], WS["w_b%d" % b][m, :, :, :], q=wq())
                    P.ld(gz[b][:, :], zT[GB + b * 2048 + m * 128:GB + b * 2048 + (m + 1) * 128, tsl])
                    ps = P.ps()
                    for c in range(8):
                        P.mm(ps[:, :], w[:, c, :], ys[b][:, c, :], start=(c == 0), stop=(c == 7))
                    P.act(gz[b][:, :], gz[b][:, :], AF.Sigmoid)
                    if b == 0:
                        P.tt(acc[:, :], ps[:, :], gz[b][:, :], ALU.mult)
                    else:
                        t = tmpf[b % 2]
                        P.tt(t[:, :], ps[:, :], gz[b][:, :], ALU.mult)
                        if b == 1:
                            P.tt(acc[:, :], acc[:, :], t[:, :], ALU.add)
                        else:
                            P.tt(mixed[:, m, :], acc[:, :], t[:, :], ALU.add)
            for m in range(KC):
                w = wb()
                P.ld(w[:, 0:KC, :], WS["w_o"][m, :, :, :], q=wq())
                ps = P.ps()
                for c in range(KC):
                    P.mm(ps[:, :], w[:, c, :], mixed[:, c, :], start=(c == 0), stop=(c == KC - 1))
                P.tt(hb[:, m, :], hb[:, m, :], ps[:, :], ALU.add)
            P.barrier()
            rmsnorm_tile(hb, KC, 512, g2, lambda c: hn2[:, c, :], D_MODEL, sq, rstd)
            for m in range(64):
                w = wb()
                P.ld(w[:, 0:KC, :], WS["w_mlp1"][m, :, :, :], q=wq())
                ps = P.ps()
                for c in range(KC):
                    P.mm(ps[:, :], w[:, c, :], hn2[:, c, :], start=(c == 0), stop=(c == KC - 1))
                t = tmpf[m % 2]
                P.act(t[:, :], ps[:, :], AF.Relu)
                P.tt(h1[:, m, :], t[:, :], t[:, :], ALU.mult)
            for m in range(KC):
                w = wbig[m % 2]
                P.ld(w[:, :, :], WS["w_mlp2"][m, :, :, :], q=("sp" if m % 2 else "pool"))
                ps = P.ps()
                for c in range(64):
                    P.mm(ps[:, :], w[:, c, :], h1[:, c, :], start=(c == 0), stop=(c == 63))
                P.tt(hb[:, m, :], hb[:, m, :], ps[:, :], ALU.add)
            for c in range(KC):
                P.cp(hn2[:, c, :], hb[:, c, :], eng=evac_eng())
            for m in range(KC):
                w = wb()
                P.ld(w[:, 0:KC, :], WS["w_pg"][m, :, :, :], q=wq())
                w2_ = wb()
                P.ld(w2_[:, 0:2, :], WS["w_pe"][m, :, :, :])
                ps = P.ps()
                for c in range(KC):
                    P.mm(ps[:, :], w[:, c, :], hn2[:, c, :], start=(c == 0), stop=(c == KC - 1))
                pe_ = P.ps()
                for c in range(2):
                    P.mm(pe_[:, :], w2_[:, c, :], ptb[:, c, :], start=(c == 0), stop=(c == 1))
                t = tmpf[m % 2]
                P.act(t[:, :], ps[:, :], AF.Sigmoid)
                P.tt(t[:, :], t[:, :], pe_[:, :], ALU.mult)
                P.tt(hb[:, m, :], hb[:, m, :], t[:, :], ALU.add)
            if not last:
                P.ld(vk(V(hT.h[:, tsl].rearrange("(c p) t -> p c t", p=128), hT.key), ("o", n)), hb[:, :, :], q="pool")
            else:
                ps = P.ps()
                for c in range(KC):
                    P.act(sq[:, c, :], hb[:, c, :], AF.Square)
                for c in range(KC):
                    P.mm(ps[:, :], ones_b, sq[:, c, :], start=(c == 0), stop=(c == KC - 1))
                P.act(rstd[:, :], ps[:, :], AF.Sqrt, bias=cst[:, C_EPS:C_EPS + 1], scale=1.0 / D_MODEL)
                P.recip(rstd[:, :], rstd[:, :])
                for c in range(KC):
                    P.stt(hb[:, c, :], hb[:, c, :], gfin[:, c:c + 1], rstd[:, :], ALU.mult, ALU.mult)
                for j in range(4):
                    for c4 in range(KC // 4):
                        ps = P.ps()
                        for q in range(4):
                            c = c4 * 4 + q
                            P.mm(ps[:, q * 128:(q + 1) * 128], hb[:, c, j * 128:(j + 1) * 128], ident_f)
                        P.cp(ot[:, :], ps[:, :], eng=evac_eng())
                        P.ld(vk(out_d[n * 512 + j * 128:n * 512 + (j + 1) * 128, c4 * 512:(c4 + 1) * 512], (n, j, c4)),
                             ot[:, :], q="pool")
            P.barrier()
        P.phase_end()

    phase_in()
    for l in range(L):
        if stop == "in":
            break
        precast_layer(l)
        phase_A(l)
        if stop == "A":
            break
        phase_R(l)
        if stop in ("R", "Rs", "Re", "Rt", "Rg", "Ri", "Ru", "Rx", "Rx1"):
            break
        phase_H(l)
        if stop == "H":
            break
        phase_M1(l)
        if stop == "M1":
            break
        phase_M2(l)
        if stop == "M2":
            break
        phase_GF(l, l == L - 1)
    P.emit()
    es.close()
    return nc, P


C_ID = 0
C_ONE = 128
C_RM1F = 256
C_RM1B = 768
C_RM2F = 1280
C_RM2B = 1408
C_TRIF = 1536
C_TRIB = 1664
C_HMF = 1792
C_HMB = 1920
C_CM = 2048
C_IF = 2052
C_PI = 2053
C_EPS = 2054
C_GEPS = 2055
C_ROT = 2056
CONST_W = 2128


def make_consts():
    c = np.zeros((128, CONST_W), np.float32)
    r = np.arange(128)[:, None]
    q = np.arange(128)[None, :]
    c[:, C_ID:C_ID + 128] = (r == q)
    c[:, C_ONE:C_ONE + 128] = 1.0
    su, iu = (r < q).astype(np.float32), (r <= q).astype(np.float32)
    sl, il = (r > q).astype(np.float32), (r >= q).astype(np.float32)
    c[:, C_RM1F:C_RM1F + 512] = np.concatenate([-su, -iu, su, iu], axis=1)
    c[:, C_RM1B:C_RM1B + 512] = np.concatenate([-sl, -il, sl, il], axis=1)
    c[:, C_RM2F:C_RM2F + 128] = -sl
    c[:, C_RM2B:C_RM2B + 128] = -su
    c[:, C_TRIF:C_TRIF + 128] = iu
    c[:, C_TRIB:C_TRIB + 128] = il
    same = ((r // 32) == (q // 32)).astype(np.float32)
    c[:, C_HMF:C_HMF + 128] = iu * same
    c[:, C_HMB:C_HMB + 128] = il * same
    c[:, C_CM:C_CM + 4] = ((np.arange(128)[:, None] // 32) == np.arange(4)[None, :])
    inv_freq = (1.0 / (np.float32(10000.0) ** (np.arange(0, 64, 2, dtype=np.float32) / np.float32(64)))).astype(np.float32)
    c[:, C_IF] = inv_freq[np.arange(128) % 32]
    c[:, C_PI] = np.float32(np.pi)
    c[:, C_EPS] = NORM_EPS
    c[:, C_GEPS] = GN_EPS
    R = np.zeros((64, 64), np.float32)
    for m in range(32):
        R[m, m + 32] = -1.0
        R[m + 32, m] = 1.0
    c[0:64, C_ROT:C_ROT + 64] = R.T
    return c


_CACHE = {}


def run(inputs, S, L, ncores):
    key = (S, L)
    if key not in _CACHE:
        _CACHE[key] = build(S, L)
    nc, P = _CACHE[key]
    consts = make_consts()
    in_maps = []
    for b in range(ncores):
        m = {"x": np.ascontiguousarray(inputs["x"][b]),
             "p": np.ascontiguousarray(inputs["p"][:, b]),
             "positions": np.ascontiguousarray(inputs["positions"][b:b + 1]).astype(np.int32),
             "final_g": np.ascontiguousarray(inputs["final_g"]),
             "consts": consts}
        for nm in W_SHAPES:
            m[nm] = np.ascontiguousarray(inputs[nm])
        in_maps.append(m)
    res = run_bass_kernel_spmd(nc, in_maps, core_ids=list(range(ncores)))
    return res


def kernel(**inputs):
    inputs = {k: np.asarray(v) for k, v in inputs.items()}
    B, S, _ = inputs["x"].shape
    L = inputs["w_in"].shape[0]
    res = run(inputs, S, L, B)
    out = np.stack([res.results[b]["out"] for b in range(B)], axis=0)
    return out.astype(np.float32)
```

```python
import numpy as np
from contextlib import ExitStack
import concourse.bass as bass
import concourse.mybir as mybir
from concourse.bass_utils import run_bass_kernel_spmd

F32 = mybir.dt.float32
BF16 = mybir.dt.bfloat16
I32 = mybir.dt.int32
AF = mybir.ActivationFunctionType
ALU = mybir.AluOpType
AX = mybir.AxisListType

D_MODEL = 2048
BW = 1024
LORA = 64
LORA_G = 160
RWKV_W = 3 * BW + 4 * LORA + LORA_G
HGRN_W = 5 * BW
Q_LORA = 768
KV_LORA = 512
ROPE = 64
MLA_W = Q_LORA + KV_LORA + ROPE
GATE_W = 3 * D_MODEL
IN_W = RWKV_W + HGRN_W + MLA_W + GATE_W
D_FF = 8192
PLE = 256
DECAY_SCALE = 0.606531
GN_EPS = 64e-5
NORM_EPS = 1e-6
F_TINY = 1e-30

DMA_K = 6


class Prog:
    CE = ("pe", "act", "dve", "pool")
    QS = ("sp", "pool")

    def __init__(self, nc, es):
        self.nc = nc
        self.es = es
        self.ops = {e: [] for e in ("pe", "act", "dve", "pool", "sp")}
        self.cnt = {e: 0 for e in self.CE}
        self.waited = {e: {} for e in self.ops}
        self.dma_n = {q: 0 for q in self.QS}
        self.bufs = {}
        self.sems = {}
        for e in self.CE:
            self.sems[e] = es.enter_context(nc.semaphore("s_" + e))
        for q in self.QS:
            for k in range(DMA_K):
                self.sems[(q, k)] = es.enter_context(nc.semaphore("d_%s%d" % (q, k)))
        self.n_ops = 0

    def _collect(self, reads, writes):
        deps = {}
        def add(ev):
            if ev is None:
                return
            sk, v = ev
            if deps.get(sk, 0) < v:
                deps[sk] = v
        for b in reads:
            st = self.bufs.get(b)
            if st is not None:
                add(st[0])
        for b in writes:
            st = self.bufs.get(b)
            if st is not None:
                add(st[0])
                for sk, v in st[1].items():
                    add((sk, v))
        return deps

    def _emit_waits(self, eng, deps):
        w = self.waited[eng]
        for sk, v in deps.items():
            if sk == eng and eng == "pe":
                continue
            if w.get(sk, 0) < v:
                w[sk] = v
                self.ops[eng].append(("wait", sk, v))

    def _update(self, ev, reads, writes):
        sk, v = ev
        for b in reads:
            st = self.bufs.get(b)
            if st is None:
                st = self.bufs[b] = [None, {}]
            if st[1].get(sk, 0) < v:
                st[1][sk] = v
        for b in writes:
            self.bufs[b] = [ev, {}]

    def op(self, eng, fn, reads=(), writes=()):
        deps = self._collect(reads, writes)
        self._emit_waits(eng, deps)
        self.cnt[eng] += 1
        ev = (eng, self.cnt[eng])
        self.ops[eng].append(("op", fn))
        self._update(ev, reads, writes)
        self.n_ops += 1

    def dma(self, q, out, in_, reads=(), writes=()):
        deps = self._collect(reads, writes)
        i = self.dma_n[q]
        self.dma_n[q] += 1
        slot = i % DMA_K
        val = 16 * (i // DMA_K + 1)
        if val > 16:
            deps[(q, slot)] = max(deps.get((q, slot), 0), val - 16)
        self._emit_waits(q, deps)
        ev = ((q, slot), val)
        self.ops[q].append(("dma", out, in_, (q, slot)))
        self._update(ev, reads, writes)
        self.n_ops += 1

    def barrier(self):
        deps = {}
        for e in self.CE:
            if self.cnt[e]:
                deps[e] = self.cnt[e]
        for q in self.QS:
            n = self.dma_n[q]
            for k in range(DMA_K):
                if n > k:
                    last = ((n - 1 - k) // DMA_K) * DMA_K + k
                    deps[(q, k)] = 16 * (last // DMA_K + 1)
        for e in self.ops:
            d = {sk: v for sk, v in deps.items() if not (sk == e)}
            self._emit_waits(e, d)

    def emit(self):
        nc = self.nc
        self.barrier()
        sems = self.sems
        with nc.Block() as blk:
            def replay(eng_name, e):
                for item in self.ops[eng_name]:
                    if item[0] == "wait":
                        e.wait_ge(sems[item[1]], item[2])
                    elif item[0] == "op":
                        item[1](e).then_inc(sems[eng_name], 1)
                    else:
                        e.dma_start(out=item[1], in_=item[2], allow_slow_non_contiguous=True).then_inc(sems[item[3]], 16)

            @blk.tensor
            def _(e):
                replay("pe", e)

            @blk.scalar
            def _(e):
                replay("act", e)

            @blk.vector
            def _(e):
                replay("dve", e)

            @blk.gpsimd
            def _(e):
                replay("pool", e)

            @blk.sync
            def _(e):
                replay("sp", e)


class V:
    __slots__ = ("ap", "key")

    def __init__(self, ap, key):
        self.ap = ap
        self.key = key

    def bc(self, shape):
        return V(self.ap.broadcast_to(list(shape)), self.key)

    def __getitem__(self, idx):
        return V(self.ap[idx], self.key)

    def re(self, s, **kw):
        return V(self.ap.rearrange(s, **kw), self.key)


class T:
    def __init__(self, h, key):
        self.h = h
        self.key = key

    def __getitem__(self, idx):
        return V(self.h[idx], self.key)

    def k(self, sub):
        return T(self.h, (self.key, sub))


def _dsize(dt):
    return 2 if dt == BF16 else 4


class K(Prog):
    def __init__(self, nc, es):
        super().__init__(nc, es)
        self.sb_off = 16640
        self.sb_base = 16640
        self.uid = 0
        self.psb = [T(es.enter_context(nc.psum_tensor("psb%d" % i, [128, 512], F32)), "psb%d" % i)
                    for i in range(8)]
        self.ps_i = 0
        self.ps_pool = list(range(8))

    def ps(self):
        t = self.psb[self.ps_pool[self.ps_i % len(self.ps_pool)]]
        self.ps_i += 1
        return t

    def sb(self, name, shape, dt):
        self.uid += 1
        n = 1
        for s in shape[1:]:
            n *= s
        nbytes = (n * _dsize(dt) + 63) // 64 * 64
        off = self.sb_off
        self.sb_off += nbytes
        assert self.sb_off <= 229000, ("SBUF overflow", name, self.sb_off)
        nm = "%s_%d" % (name, self.uid)
        h = self.nc.alloc_sbuf_tensor_at(nm, list(shape), dt, offset=off)
        return T(h, nm)

    def persist(self):
        self.sb_base = self.sb_off

    def phase_end(self):
        self.barrier()
        self.sb_off = self.sb_base

    def dram(self, name, shape, dt, kind="Internal"):
        h = self.nc.dram_tensor(name, list(shape), dt, kind=kind)
        return T(h.ap(), name)

    def mm(self, out, lhsT, rhs, start=True, stop=True):
        o, l, r = out.ap, lhsT.ap, rhs.ap
        self.op("pe", lambda e: e.matmul(o, l, r, start=start, stop=stop),
                reads=[lhsT.key, rhs.key], writes=[out.key])

    def act(self, out, in_, func, bias=None, scale=None, eng="act"):
        o, i = out.ap, in_.ap
        kw = {}
        rd = [in_.key]
        if bias is not None:
            if isinstance(bias, V):
                kw["bias"] = bias.ap
                rd.append(bias.key)
            else:
                kw["bias"] = bias
        if scale is not None:
            if isinstance(scale, V):
                kw["scale"] = scale.ap
                rd.append(scale.key)
            else:
                kw["scale"] = scale
        self.op("act", lambda e: e.activation(out=o, in_=i, func=func, **kw),
                reads=rd, writes=[out.key])

    def tt(self, out, in0, in1, op, eng="dve"):
        o, a, b = out.ap, in0.ap, in1.ap
        self.op(eng, lambda e: e.tensor_tensor(o, a, b, op),
                reads=[in0.key, in1.key], writes=[out.key])

    def ts(self, out, in0, s1, op0, s2=None, op1=None, eng="dve"):
        o, a = out.ap, in0.ap
        rd = [in0.key]
        if isinstance(s1, V):
            rd.append(s1.key)
            s1 = s1.ap
        if isinstance(s2, V):
            rd.append(s2.key)
            s2 = s2.ap
        if op1 is None:
            self.op(eng, lambda e: e.tensor_scalar(o, a, s1, None, op0), reads=rd, writes=[out.key])
        else:
            self.op(eng, lambda e: e.tensor_scalar(o, a, s1, s2, op0, op1), reads=rd, writes=[out.key])

    def stt(self, out, in0, scalar, in1, op0, op1, eng="dve"):
        o, a, b = out.ap, in0.ap, in1.ap
        rd = [in0.key, in1.key]
        if isinstance(scalar, V):
            rd.append(scalar.key)
            scalar = scalar.ap
        self.op(eng, lambda e: e.scalar_tensor_tensor(o, a, scalar, b, op0, op1),
                reads=rd, writes=[out.key])

    def cp(self, out, in_, eng="dve"):
        o, i = out.ap, in_.ap
        if eng == "act":
            self.op("act", lambda e: e.activation(out=o, in_=i, func=AF.Copy),
                    reads=[in_.key], writes=[out.key])
        else:
            self.op(eng, lambda e: e.tensor_copy(o, i), reads=[in_.key], writes=[out.key])

    def recip(self, out, in_):
        o, i = out.ap, in_.ap
        self.op("dve", lambda e: e.reciprocal(o, i), reads=[in_.key], writes=[out.key])

    def memset(self, out, val, eng="dve"):
        o = out.ap
        self.op(eng, lambda e: e.memset(o, val), writes=[out.key])

    def ld(self, out, in_, q="sp"):
        self.dma(q, out.ap, in_.ap, reads=[in_.key], writes=[out.key])


def vk(v, sub):
    return V(v.ap, (v.key, sub))


W_SHAPES = {
    "ln1_g": (D_MODEL,), "w_in": (D_MODEL, IN_W), "rwkv_mu": (2, RWKV_W), "rwkv_w0": (2, BW),
    "rwkv_w2": (2, LORA, BW), "rwkv_a0": (2, BW), "rwkv_a2": (2, LORA, BW), "rwkv_g2": (LORA_G, BW),
    "rwkv_kk": (BW,), "rwkv_ka": (BW,), "rwkv_rk": (16, 64), "rwkv_gn_w": (BW,), "rwkv_gn_b": (BW,),
    "hgrn_lb": (BW,), "hgrn_norm_g": (128,), "mla_q_norm_g": (Q_LORA,), "mla_kv_norm_g": (KV_LORA,),
    "mla_w_uq": (Q_LORA, 8 * 192), "mla_w_ukv": (KV_LORA, 8 * 256), "w_branch": (3, BW, D_MODEL),
    "w_o": (D_MODEL, D_MODEL), "ln2_g": (D_MODEL,), "w_mlp1": (D_MODEL, D_FF), "w_mlp2": (D_FF, D_MODEL),
    "w_pe": (PLE, D_MODEL), "w_pg": (D_MODEL, D_MODEL),
}


def build(S, L, dbg=()):
    import os as _os
    stop = _os.environ.get("MK_STOP", "")
    nc = bass.Bass("TRN2", target_bir_lowering=False)
    es = ExitStack()
    P = K(nc, es)
    NT = S // 512
    NG = S // 128
    KC = D_MODEL // 128

    x_in = P.dram("x", [S, D_MODEL], F32, "ExternalInput")
    p_in = P.dram("p", [L, S, PLE], F32, "ExternalInput")
    pos_in = P.dram("positions", [1, S], I32, "ExternalInput")
    Wd = {}
    for nm, shp in W_SHAPES.items():
        Wd[nm] = P.dram(nm, [L] + list(shp), F32, "ExternalInput")
    fin_g = P.dram("final_g", [D_MODEL], F32, "ExternalInput")
    cst_in = P.dram("consts", [128, CONST_W], F32, "ExternalInput")
    out_d = P.dram("out", [S, D_MODEL], F32, "ExternalOutput")
    dbg_out = {}

    hT = P.dram("hT", [D_MODEL, S], F32)
    zT = P.dram("zT", [IN_W, S], F32)
    yT = [P.dram("yT%d" % n, [BW, S], BF16) for n in range(3)]

    def wscr(name, Kdim, M):
        nm_ = (M + 127) // 128
        return P.dram("ws_" + name, [nm_, 128, Kdim // 128, 128], BF16)

    WS = {
        "w_in": wscr("w_in", D_MODEL, IN_W), "w_b0": wscr("w_b0", BW, D_MODEL),
        "w_b1": wscr("w_b1", BW, D_MODEL), "w_b2": wscr("w_b2", BW, D_MODEL),
        "w_o": wscr("w_o", D_MODEL, D_MODEL), "w_mlp1": wscr("w_mlp1", D_MODEL, D_FF),
        "w_mlp2": wscr("w_mlp2", D_FF, D_MODEL), "w_pe": wscr("w_pe", PLE, D_MODEL),
        "w_pg": wscr("w_pg", D_MODEL, D_MODEL),
    }

    cst = P.sb("cst", [128, CONST_W], F32)
    P.ld(cst[:, :], cst_in[:, :])
    cstb = P.sb("cstb", [128, CONST_W], BF16)
    P.cp(cstb[:, :], cst[:, :])
    ident_f = cst[:, C_ID:C_ID + 128]
    ident_b = cstb[:, C_ID:C_ID + 128]
    ones_b = cstb[:, C_ONE:C_ONE + 128]
    P.persist()
    rr = [0]

    def evac_eng():
        rr[0] += 1
        return "act" if rr[0] % 2 else "dve"

    def precast(src, Kdim, M, dst):
        kc_all = Kdim // 128
        nm_ = (M + 127) // 128
        stg, stb = pcb["f"], pcb["b"]
        it = 0
        for mi in range(nm_):
            m0 = mi * 128
            msz = min(128, M - m0)
            for k0 in range(0, kc_all, 16):
                kc = min(16, kc_all - k0)
                sf, sbb = stg[it % 3], stb[it % 3]
                it += 1
                srcv = V(src.ap[k0 * 128:(k0 + kc) * 128, m0:m0 + msz].rearrange("(c p) m -> p c m", p=128), src.key)
                P.ld(sf[:, 0:kc, 0:msz], srcv)
                P.cp(sbb[:, 0:kc, 0:msz], sf[:, 0:kc, 0:msz], eng=evac_eng())
                P.ld(vk(dst[mi, :, k0:k0 + kc, 0:msz], (mi, k0)), sbb[:, 0:kc, 0:msz], q="pool")

    pcb = {}

    def precast_layer(l):
        pcb["f"] = [P.sb("pc_f", [128, 16, 128], F32) for _ in range(3)]
        pcb["b"] = [P.sb("pc_b", [128, 16, 128], BF16) for _ in range(3)]
        precast(Wd["w_in"][l], D_MODEL, IN_W, WS["w_in"])
        for n in range(3):
            precast(Wd["w_branch"][l, n], BW, D_MODEL, WS["w_b%d" % n])
        precast(Wd["w_o"][l], D_MODEL, D_MODEL, WS["w_o"])
        precast(Wd["w_mlp1"][l], D_MODEL, D_FF, WS["w_mlp1"])
        precast(Wd["w_mlp2"][l], D_FF, D_MODEL, WS["w_mlp2"])
        precast(Wd["w_pe"][l], PLE, D_MODEL, WS["w_pe"])
        precast(Wd["w_pg"][l], D_MODEL, D_MODEL, WS["w_pg"])
        P.phase_end()

    def linear(xT, kc, wt, mtiles, ntiles, evac, wbufs):
        for j, (mi, msz) in enumerate(mtiles):
            wb = wbufs[j % len(wbufs)]
            P.ld(wb[:, 0:kc, :], wt[mi, :, :, :])
            for (n0, nsz) in ntiles:
                ps = P.ps()
                for c in range(kc):
                    P.mm(ps[0:msz, 0:nsz], wb[:, c, 0:msz], xT[:, c, n0:n0 + nsz],
                         start=(c == 0), stop=(c == kc - 1))
                evac(mi, msz, n0, nsz, ps)

    def rmsnorm_tile(src, kc, nsz, g_sb, dst, dim, sq_scr, rstd_scr):
        ps = P.ps()
        for c in range(kc):
            P.act(sq_scr[:, c, 0:nsz], src[:, c, 0:nsz], AF.Square)
        for c in range(kc):
            P.mm(ps[:, 0:nsz], ones_b, sq_scr[:, c, 0:nsz], start=(c == 0), stop=(c == kc - 1))
        P.act(rstd_scr[:, 0:nsz], ps[:, 0:nsz], AF.Sqrt, bias=cst[:, C_EPS:C_EPS + 1], scale=1.0 / dim)
        P.recip(rstd_scr[:, 0:nsz], rstd_scr[:, 0:nsz])
        for c in range(kc):
            P.stt(dst(c), src[:, c, 0:nsz], g_sb[:, c:c + 1], rstd_scr[:, 0:nsz], ALU.mult, ALU.mult,
                  eng="dve")

    def load_vec(dst, src_v, kc):
        P.ld(dst[:, 0:kc], V(src_v.ap.rearrange("(c p) -> p c", p=128), src_v.key))

    def phase_in():
        xs = [P.sb("xin", [128, D_MODEL], F32) for _ in range(2)]
        ho = [P.sb("hout", [128, 512], F32) for _ in range(4)]
        it = 0
        for g in range(NG):
            xt = xs[g % 2]
            P.ld(xt[:, :], x_in[g * 128:(g + 1) * 128, :])
            for c4 in range(KC // 4):
                ps = P.ps()
                for j in range(4):
                    c = c4 * 4 + j
                    P.mm(ps[:, j * 128:(j + 1) * 128], xt[:, c * 128:(c + 1) * 128], ident_f)
                o = ho[it % 4]
                it += 1
                P.cp(o[:, :], ps[:, :], eng=evac_eng())
                dstv = V(hT.h[c4 * 512:(c4 + 1) * 512, g * 128:(g + 1) * 128].rearrange("(j p) t -> p j t", p=128),
                         (hT.key, g, c4))
                P.ld(dstv, o[:, :].re("p (j t) -> p j t", j=4), q="pool")
        P.phase_end()

    def phase_A(l):
        g_sb = P.sb("ln1g", [128, KC], F32)
        load_vec(g_sb, Wd["ln1_g"][l], KC)
        ST = min(S, 2048)
        hn = P.sb("hn", [128, KC, ST], BF16)
        hb = P.sb("hb", [128, KC, 512], F32)
        sq = P.sb("sq", [128, KC, 512], BF16)
        rstd = P.sb("rstd", [128, 512], F32)
        wbufs = [P.sb("wA", [128, KC, 128], BF16) for _ in range(2)]
        stg = [P.sb("zst", [128, 512], F32) for _ in range(4)]
        cnt = [0]
        for s0 in range(0, S, ST):
            for n0 in range(0, ST, 512):
                P.ld(hb[:, :, :], V(hT.h[:, s0 + n0:s0 + n0 + 512].rearrange("(c p) t -> p c t", p=128), hT.key))
                rmsnorm_tile(hb, KC, 512, g_sb, lambda c, n0=n0: hn[:, c, n0:n0 + 512], D_MODEL, sq, rstd)

            def evac(mi, msz, n0, nsz, ps, s0=s0):
                o = stg[cnt[0] % 4]
                cnt[0] += 1
                P.cp(o[0:msz, 0:nsz], ps[0:msz, 0:nsz], eng=evac_eng())
                P.ld(vk(zT[mi * 128:mi * 128 + msz, s0 + n0:s0 + n0 + nsz], (mi, s0 + n0)), o[0:msz, 0:nsz], q="pool")

            mt = [(mi, min(128, IN_W - mi * 128)) for mi in range((IN_W + 127) // 128)]
            linear(hn, KC, WS["w_in"], mt, [(n0, 512) for n0 in range(0, ST, 512)], evac, wbufs)
        P.phase_end()

    yfT = P.dram("yfT", [BW, S], F32)
    H = 16

    def phase_R(l):
        def hv(name, src_v):
            t = P.sb(name, [64, H], F32)
            P.ld(t[:, :], V(src_v.ap.rearrange("(h c) -> c h", c=64), src_v.key))
            return t
        kkp = hv("kkp", Wd["rwkv_kk"][l])
        kap = hv("kap", Wd["rwkv_ka"][l])
        gnw = hv("gnw", Wd["rwkv_gn_w"][l])
        gnb = hv("gnb", Wd["rwkv_gn_b"][l])
        rkp = P.sb("rkp", [64, H], F32)
        P.ld(rkp[:, :], V(Wd["rwkv_rk"][l].ap.rearrange("h c -> c h"), Wd["rwkv_rk"].key))
        omka = P.sb("omka", [64, H], F32)
        P.ts(omka[:, :], kap[:, :], -1.0, ALU.mult, 1.0, ALU.add)
        tmka = P.sb("tmka", [64, H], F32)
        P.ts(tmka[:, :], kap[:, :], -2.0, ALU.mult, 2.0, ALU.add)
        mu = Wd["rwkv_mu"]
        def mut(name, parts, nb, lo, hi, which):
            t = P.sb(name, [parts, nb], F32)
            P.ld(t[:, :], V(mu.h[l, which, lo:hi].rearrange("(j c) -> c j", c=parts), mu.key))
            return t
        m0_rkv = mut("m0rkv", 64, 48, 0, 3072, 0)
        m1_rkv = mut("m1rkv", 64, 48, 0, 3072, 1)
        m0_lo = mut("m0lo", 64, 4, 3072, 3328, 0)
        m1_lo = mut("m1lo", 64, 4, 3072, 3328, 1)
        m0_g = mut("m0g", 80, 2, 3328, 3488, 0)
        m1_g = mut("m1g", 80, 2, 3328, 3488, 1)

        def c0of(name, a, b, parts, nb):
            t = P.sb(name, [parts, nb], F32)
            P.tt(t[:, :], a[:, :], b[:, :], ALU.add)
            P.ts(t[:, :], t[:, :], -1.0, ALU.mult, 1.0, ALU.add)
            return t
        c0_rkv = c0of("c0rkv", m0_rkv, m1_rkv, 64, 48)
        c0_lo = c0of("c0lo", m0_lo, m1_lo, 64, 4)
        c0_g = c0of("c0g", m0_g, m1_g, 80, 2)

        w2a = [P.sb("w2a%d" % d, [66, BW], BF16) for d in range(2)]
        a2a = [P.sb("a2a%d" % d, [66, BW], BF16) for d in range(2)]
        g2b = P.sb("g2b", [80, 2, BW], BF16)
        off0 = P.sb_off
        st = P.sb("augst", [64, BW], F32)
        br = P.sb("augbr", [1, BW], F32)
        hif = P.sb("aughif", [1, BW], F32)
        hi = P.sb("aughi", [1, BW], BF16)
        lo = P.sb("auglo", [1, BW], BF16)

        def aug(t, w_v, b_v):
            P.ld(st[:, :], w_v)
            P.cp(t[0:64, :], st[:, :])
            P.ld(br[:, :], V(b_v.ap.rearrange("(o n) -> o n", o=1), b_v.key))
            P.cp(hi[:, :], br[:, :])
            P.cp(hif[:, :], hi[:, :])
            P.tt(hif[:, :], br[:, :], hif[:, :], ALU.subtract)
            P.cp(lo[:, :], hif[:, :])
            P.ld(t[64:65, :], hi[:, :])
            P.ld(t[65:66, :], lo[:, :])
        for d in range(2):
            aug(w2a[d], Wd["rwkv_w2"][l, d], Wd["rwkv_w0"][l, d])
            aug(a2a[d], Wd["rwkv_a2"][l, d], Wd["rwkv_a0"][l, d])
        g2f = P.sb("g2f", [80, 2, BW], F32)
        P.ld(g2f[:, :, :], V(Wd["rwkv_g2"][l].ap.rearrange("(j p) n -> p j n", p=80), Wd["rwkv_g2"].key))
        P.cp(g2b[:, :, :], g2f[:, :, :])
        P.barrier()
        P.sb_off = off0

        zin = P.sb("zin", [64, 16, 130], F32)
        zlo = P.sb("zlo", [64, 4, 130], F32)
        zg = P.sb("zg", [80, 2, 130], F32)
        u = P.sb("u", [64, 48, 128], F32)
        ulo = P.sb("ulo", [64, 4, 128], F32)
        tlo = P.sb("tlo", [64, 4, 128], F32)
        ug = P.sb("ug", [80, 2, 128], F32)
        tg = P.sb("tg", [80, 2, 128], F32)
        wda = P.sb("wda", [66, 128], BF16)
        ada = [P.sb("ada%d" % d, [66, 128], BF16) for d in range(2)]
        sgd = P.sb("sgd", [80, 2, 128], BF16)
        P.memset(wda[64:66, :], 1.0)
        for d in range(2):
            P.memset(ada[d][64:66, :], 1.0)
        kk = P.sb("kk", [64, H, 128], F32)
        sqb = P.sb("sqb", [64, H, 128], BF16)
        s_tm = P.sb("s_tm", [128, BW], F32)
        E1 = P.sb("E1", [64, H, 128], F32)
        E2 = P.sb("E2", [64, H, 128], F32)
        av = [P.sb("a%d" % d, [64, H, 128], F32) for d in range(2)]
        kd = P.sb("kd", [64, H, 128], F32)
        be = P.sb("be", [64, H, 128], F32)
        tmp = be
        KR = P.sb("KR", [64, H, 256], BF16)
        BT = P.sb("BT", [64, H, 128], BF16)
        KT = P.sb("KT", [64, H, 128], BF16)
        NBH = P.sb("NBH", [64, H, 128], BF16)
        KH = P.sb("KH", [64, H, 128], BF16)
        VB = P.sb("VB", [64, H, 128], BF16)
        TM = P.sb("TM", [128, H, 320], BF16)
        G1 = [P.sb("G1", [128, 512], BF16) for _ in range(8)]
        G2 = [P.sb("G2", [128, 128], BF16) for _ in range(8)]
        LV = [[P.sb("LV", [128, 384], BF16) for _ in range(2)] for _ in range(8)]
        UM = [P.sb("UM", [128, 128], BF16) for _ in range(8)]
        RpT = [P.sb("RpT", [64, 128], BF16) for _ in range(8)]
        PT = [P.sb("PT", [64, 64], BF16) for _ in range(8)]
        dG = [P.sb("dG", [64, 64], F32) for _ in range(8)]
        WT = [P.sb("WT", [128, 128], BF16) for _ in range(8)]
        Tst = P.sb("Tst", [64, H, 64], BF16)
        yv = kk
        yf = E1
        yab = BT

        def bc_t(t2, nb):
            return V(t2.ap.unsqueeze(2).broadcast_to([t2.ap.shape[0], nb, 128]), t2.key)

        def shift(dst, src, tmpb, c0, m0, m1, nb, do=0, co=0):
            dv_ = dst[:, do:do + nb, :]
            tv_ = tmpb[:, 0:nb, :]
            P.tt(dv_, src[:, 0:nb, 1:129], bc_t(c0[:, co:co + nb], nb), ALU.mult)
            P.tt(tv_, src[:, 0:nb, 0:128], bc_t(m0[:, co:co + nb], nb), ALU.mult)
            P.tt(dv_, dv_, tv_, ALU.add)
            P.tt(tv_, src[:, 0:nb, 2:130], bc_t(m1[:, co:co + nb], nb), ALU.mult)
            P.tt(dv_, dv_, tv_, ALU.add)

        def load_halo(dst, r0, r1, parts, t0):
            lo = max(t0 - 1, 0)
            hi = min(t0 + 129, S)
            if t0 == 0:
                P.memset(dst[:, :, 0:1], 0.0)
            if t0 + 129 > S:
                P.memset(dst[:, :, 129:130], 0.0)
            P.ld(dst[:, :, lo - (t0 - 1):hi - (t0 - 1)],
                 V(zT.h[r0:r1, lo:hi].rearrange("(j c) t -> c j t", c=parts), zT.key))

        def rwkv_post(l, g, t0):
            P.cp(yab[:, :, :], yv[:, :, :], eng="act")
            for b4 in range(4):
                sl = slice(b4 * 4, (b4 + 1) * 4)
                ps = P.ps()
                P.mm(ps[0:64, :], ones_b[0:64, 0:64], yab[:, sl, :].re("p a b -> p (a b)"))
                P.stt(be[:, sl, :].re("p a b -> p (a b)"), ps[0:64, :], -1.0 / 64, yv[:, sl, :].re("p a b -> p (a b)"),
                      ALU.mult, ALU.add)
            P.act(sqb[:, :, :], be[:, :, :], AF.Square)
            for b4 in range(4):
                sl = slice(b4 * 4, (b4 + 1) * 4)
                ps = P.ps()
                P.mm(ps[0:64, :], ones_b[0:64, 0:64], sqb[:, sl, :].re("p a b -> p (a b)"))
                P.act(kd[:, sl, :].re("p a b -> p (a b)"), ps[0:64, :], AF.Ln, bias=cst[0:64, C_GEPS:C_GEPS + 1], scale=1.0 / 64)
            P.act(kd[:, :, :], kd[:, :, :], AF.Exp, scale=-0.5)
            P.tt(be[:, :, :], be[:, :, :], kd[:, :, :], ALU.mult)
            P.tt(be[:, :, :], be[:, :, :], bc_t(gnw[:, :], H), ALU.mult)
            P.tt(be[:, :, :], be[:, :, :], bc_t(gnb[:, :], H), ALU.add)
            P.tt(E2[:, :, :], av[0][:, :, :], av[1][:, :, :], ALU.add)
            P.tt(E2[:, :, :], E2[:, :, :], bc_t(kap[:, :], H), ALU.mult)
            P.tt(E2[:, :, :], E2[:, :, :], bc_t(tmka[:, :], H), ALU.add)
            P.tt(E2[:, :, :], E2[:, :, :], u[:, 16:32, :], ALU.mult)
            P.tt(E2[:, :, :], E2[:, :, :], u[:, 0:16, :], ALU.mult)
            P.tt(sqb[:, :, :], E2[:, :, :], bc_t(rkp[:, :], H), ALU.mult)
            for b4 in range(4):
                sl = slice(b4 * 4, (b4 + 1) * 4)
                ps = P.ps()
                P.mm(ps[0:64, :], ones_b[0:64, 0:64], sqb[:, sl, :].re("p a b -> p (a b)"))
                P.tt(kd[:, sl, :].re("p a b -> p (a b)"), ps[0:64, :], u[:, 32 + b4 * 4:32 + (b4 + 1) * 4, :].re("p a b -> p (a b)"),
                     ALU.mult)
            P.tt(be[:, :, :], be[:, :, :], kd[:, :, :], ALU.add)
            shift(ug, zg, tg, c0_g, m0_g, m1_g, 2)
            P.act(sgd[:, :, :], ug[:, :, :], AF.Sigmoid)
            for b4 in range(4):
                ps = P.ps()
                for j in range(4):
                    h = b4 * 4 + j
                    for q in range(2):
                        P.mm(ps[0:64, j * 128:(j + 1) * 128], g2b[:, q, h * 64:(h + 1) * 64], sgd[:, q, :],
                             start=(q == 0), stop=(q == 1))
                sl = slice(b4 * 4, (b4 + 1) * 4)
                P.tt(yab[:, sl, :].re("p a b -> p (a b)"), ps[0:64, :], be[:, sl, :].re("p a b -> p (a b)"), ALU.mult)
            P.ld(vk(V(yT[0].h[:, t0:t0 + 128].rearrange("(h c) t -> c h t", c=64), yT[0].key), g), yab[:, :, :], q="pool")

        if stop == "Rs":
            P.phase_end()
            return
        for d in range(2):
            P.memset(Tst[:, :, :], 0.0)
            mk1 = cst[:, (C_RM1F if d == 0 else C_RM1B):(C_RM1F if d == 0 else C_RM1B) + 512]
            mk2 = cst[:, (C_RM2F if d == 0 else C_RM2B):(C_RM2F if d == 0 else C_RM2B) + 128]
            tri = cst[:, (C_TRIF if d == 0 else C_TRIB):(C_TRIF if d == 0 else C_TRIB) + 128]
            last = 127 if d == 0 else 0
            order = range(NG) if d == 0 else range(NG - 1, -1, -1)
            for g in order:
                t0 = g * 128
                for part in range(3):
                    load_halo(zin, part * 1024, (part + 1) * 1024, 64, t0)
                    shift(u, zin, tmp, c0_rkv, m0_rkv, m1_rkv, 16, do=part * 16, co=part * 16)
                load_halo(zlo, 3072, 3328, 64, t0)
                load_halo(zg, 3328, 3488, 80, t0)
                shift(ulo, zlo, tlo, c0_lo, m0_lo, m1_lo, 4)
                r_ = lambda sl=slice(None): u[:, 0:16, sl]
                P.tt(kk[:, :, :], u[:, 16:32, :], bc_t(kkp[:, :], H), ALU.mult)
                P.act(sqb[:, :, :], kk[:, :, :], AF.Square)
                for b4 in range(4):
                    ps = P.ps()
                    P.mm(ps[0:64, :], ones_b[0:64, 0:64], sqb[:, b4 * 4:(b4 + 1) * 4, :].re("p a b -> p (a b)"))
                    P.ts(tmp[:, b4 * 4:(b4 + 1) * 4, :].re("p a b -> p (a b)"), ps[0:64, :], 1e-24, ALU.max)
                P.act(tmp[:, :, :], tmp[:, :, :], AF.Ln)
                P.act(tmp[:, :, :], tmp[:, :, :], AF.Exp, scale=-0.5)
                P.tt(kk[:, :, :], kk[:, :, :], tmp[:, 0:16, :], ALU.mult)
                P.act(wda[0:64, :], ulo[:, d, :], AF.Tanh)
                for dd in (range(2) if d == 1 else [d]):
                    P.cp(ada[dd][0:64, :], ulo[:, 2 + dd, :])
                for hf in range(2):
                    ps = P.ps()
                    P.mm(ps[:, :], wda[:, :], w2a[d][:, hf * 512:(hf + 1) * 512])
                    P.act(s_tm[:, hf * 512:(hf + 1) * 512], ps[:, :], AF.Sigmoid)
                for dd in (range(2) if d == 1 else [d]):
                    for b4 in range(4):
                        ps = P.ps()
                        for j in range(4):
                            h = b4 * 4 + j
                            P.mm(ps[0:64, j * 128:(j + 1) * 128], a2a[dd][:, h * 64:(h + 1) * 64], ada[dd][:, :])
                        P.act(av[dd][:, b4 * 4:(b4 + 1) * 4, :].re("p a b -> p (a b)"), ps[0:64, :], AF.Sigmoid)
                for b4 in range(4):
                    ps = P.ps()
                    for j in range(4):
                        h = b4 * 4 + j
                        P.mm(ps[0:64, j * 128:(j + 1) * 128], s_tm[:, h * 64:(h + 1) * 64], tri)
                    P.act(E1[:, b4 * 4:(b4 + 1) * 4, :].re("p a b -> p (a b)"), ps[0:64, :], AF.Exp, scale=-DECAY_SCALE)
                    P.act(E2[:, b4 * 4:(b4 + 1) * 4, :].re("p a b -> p (a b)"), ps[0:64, :], AF.Exp, scale=DECAY_SCALE)
                a = av[d]
                P.tt(kd[:, :, :], a[:, :, :], bc_t(kap[:, :], H), ALU.mult)
                P.tt(kd[:, :, :], kd[:, :, :], bc_t(omka[:, :], H), ALU.add)
                P.tt(kd[:, :, :], kd[:, :, :], u[:, 16:32, :], ALU.mult)
                P.tt(be[:, :, :], kk[:, :, :], a[:, :, :], ALU.mult)
                if d == 0:
                    P.tt(KR[:, :, 1:128], kk[:, :, 1:128], E1[:, :, 0:127], ALU.mult)
                    P.cp(KR[:, :, 0:1], kk[:, :, 0:1])
                else:
                    P.tt(KR[:, :, 0:127], kk[:, :, 0:127], E1[:, :, 1:128], ALU.mult)
                    P.cp(KR[:, :, 127:128], kk[:, :, 127:128])
                P.tt(KR[:, :, 128:256], u[:, 0:16, :], E1[:, :, :], ALU.mult)
                P.tt(be[:, :, :], be[:, :, :], E2[:, :, :], ALU.mult)
                P.cp(BT[:, :, :], be[:, :, :], eng="act")
                P.tt(kd[:, :, :], kd[:, :, :], E2[:, :, :], ALU.mult)
                P.cp(KT[:, :, :], kd[:, :, :], eng="act")
                gC = V(E1.h[:, :, last:last + 1].broadcast_to([64, H, 128]), E1.key)
                P.stt(NBH[:, :, :], be[:, :, :], -1.0, gC, ALU.mult, ALU.mult)
                P.tt(KH[:, :, :], kd[:, :, :], gC, ALU.mult)
                P.cp(VB[:, :, :], u[:, 32:48, :], eng="act")
                if stop == "Re":
                    continue
                for (srcT, off) in ((KR, 64), (NBH, 128), (KH, 192), (VB, 256)):
                    for b8 in range(2):
                        ps = P.ps()
                        for j in range(8):
                            h = b8 * 8 + j
                            P.mm(ps[:, j * 64:(j + 1) * 64], srcT[:, h, 0:128], ident_b[0:64, 0:64])
                        P.cp(TM[:, b8 * 8:(b8 + 1) * 8, off:off + 64],
                             ps[:, :].re("p (a b) -> p a b", a=8), eng=evac_eng())
                if stop == "Rt":
                    continue
                for b8 in range(2):
                    for q4 in range(2):
                        hs = [b8 * 8 + q4 * 4 + j for j in range(4)]
                        psA = [P.ps() for _ in range(4)]
                        psB = P.ps()
                        for j, h in enumerate(hs):
                            P.mm(psA[j][:, 0:256], BT[:, h, :], KR[:, h, :])
                            P.mm(psA[j][:, 256:512], KT[:, h, :], KR[:, h, :])
                            P.mm(psB[:, j * 128:(j + 1) * 128], KR[:, h, 0:128], BT[:, h, :])
                        for j, h in enumerate(hs):
                            jj = q4 * 4 + j
                            P.tt(G1[jj][:, :], psA[j][:, :], mk1, ALU.mult)
                            P.tt(G2[jj][:, :], psB[:, j * 128:(j + 1) * 128], mk2, ALU.mult, eng="dve")
                    cur = [None] * 8
                    for lev in range(1, 8):
                        pl = [P.ps() for _ in range(8)]
                        for jj in range(8):
                            if lev == 1:
                                Y, YT, Z = G1[jj][:, 0:128], G2[jj][:, :], ident_b
                            else:
                                c_ = cur[jj]
                                Y, YT, Z = c_[:, 0:128], c_[:, 128:256], c_[:, 256:384]
                            if lev < 7:
                                P.mm(pl[jj][:, 0:128], YT, Y)
                                P.mm(pl[jj][:, 128:256], Y, YT)
                            P.mm(pl[jj][:, 256:384], YT, Z, start=True, stop=False)
                            P.mm(pl[jj][:, 256:384], ident_b, Z, start=False, stop=True)
                        for jj in range(8):
                            nxt = LV[jj][lev % 2]
                            if lev < 7:
                                P.cp(nxt[:, :], pl[jj][:, 0:384], eng=evac_eng())
                            else:
                                P.cp(WT[jj][:, :], pl[jj][:, 256:384], eng=evac_eng())
                            cur[jj] = nxt
                    for q4 in range(2):
                        b4 = b8 * 2 + q4
                        hs = [b4 * 4 + j for j in range(4)]
                        J = [q4 * 4 + j for j in range(4)]
                        px = P.ps()
                        for j, h in enumerate(hs):
                            P.mm(px[:, j * 64:(j + 1) * 64], G1[J[j]][:, 256:384], TM[:, h, 256:320])
                        for j, h in enumerate(hs):
                            P.cp(TM[:, h, 0:64], px[:, j * 64:(j + 1) * 64], eng=evac_eng())
                        pu = P.ps()
                        for j, h in enumerate(hs):
                            P.mm(pu[:, j * 128:(j + 1) * 128], WT[J[j]][:, :], TM[:, h, 0:128])
                        for j, h in enumerate(hs):
                            P.cp(UM[J[j]][:, :], pu[:, j * 128:(j + 1) * 128], eng="dve")
                        pr = P.ps()
                        pp = P.ps()
                        for j, h in enumerate(hs):
                            P.mm(pr[0:64, j * 128:(j + 1) * 128], UM[J[j]][:, 64:128], G1[J[j]][:, 128:256], start=True, stop=False)
                            P.mm(pr[0:64, j * 128:(j + 1) * 128], ident_b[0:64, 0:64], KR[:, h, 128:256], start=False, stop=True)
                            P.mm(pp[0:64, j * 64:(j + 1) * 64], UM[J[j]][:, 64:128], TM[:, h, 128:192])
                        for j, h in enumerate(hs):
                            P.cp(RpT[J[j]][:, :], pr[0:64, j * 128:(j + 1) * 128], eng="act")
                            P.stt(dG[J[j]][:, :], ident_f[0:64, 0:64], E1[:, h, last:last + 1], ident_f[0:64, 0:64], ALU.mult, ALU.mult)
                            P.tt(PT[J[j]][:, :], pp[0:64, j * 64:(j + 1) * 64], dG[J[j]][:, :], ALU.add)
                        py = P.ps()
                        pt = P.ps()
                        for j, h in enumerate(hs):
                            yo = py[0:64, j * 128:(j + 1) * 128]
                            P.mm(yo, UM[J[j]][:, 0:64], G1[J[j]][:, 128:256], start=True, stop=False)
                            P.mm(yo, TM[:, h, 256:320], G1[J[j]][:, 384:512], start=False, stop=False)
                            P.mm(yo, Tst[:, h, :], RpT[J[j]][:, :], start=False, stop=True)
                            to = pt[0:64, j * 64:(j + 1) * 64]
                            P.mm(to, TM[:, h, 128:192], UM[J[j]][:, 0:64], start=True, stop=False)
                            P.mm(to, TM[:, h, 192:256], TM[:, h, 256:320], start=False, stop=False)
                            P.mm(to, PT[J[j]][:, :], Tst[:, h, :], start=False, stop=True)
                        P.cp(yv[:, b4 * 4:(b4 + 1) * 4, :].re("p a b -> p (a b)"), py[0:64, :], eng="act")
                        P.cp(Tst[:, b4 * 4:(b4 + 1) * 4, :].re("p a b -> p (a b)"), pt[0:64, 0:256], eng="dve")
                if stop in ("Rg", "Ri", "Ru", "Rx", "Rx1"):
                    continue
                if d == 0:
                    P.ld(vk(V(yfT.h[:, t0:t0 + 128].rearrange("(h c) t -> c h t", c=64), yfT.key), g), yv[:, :, :], q="pool")
                else:
                    P.ld(yf[:, :, :], V(yfT.h[:, t0:t0 + 128].rearrange("(h c) t -> c h t", c=64), yfT.key))
                    P.tt(yv[:, :, :], yv[:, :, :], yf[:, :, :], ALU.add)
                    rwkv_post(l, g, t0)
            P.barrier()
        P.phase_end()

    yhfT = P.dram("yhfT", [BW, S], F32)
    yhbT = P.dram("yhbT", [BW, S], F32)
    HB = RWKV_W

    def phase_H(l):
        HH = 8
        lball = P.sb("lball", [128, HH, L], F32)
        for i in range(L):
            P.ld(lball[:, :, i], V(Wd["hgrn_lb"].h[i].rearrange("(h c) -> c h", c=128), Wd["hgrn_lb"].key))
        P.act(lball[:, :, :], lball[:, :, :], AF.Exp)
        den = P.sb("lbden", [128, HH], F32)
        P.cp(den[:, :], lball[:, :, 0])
        for i in range(1, L):
            P.tt(den[:, :], den[:, :], lball[:, :, i], ALU.add)
        P.op("dve", (lambda e, o=den[:, :].ap: e.reciprocal(o, o)), reads=[den.key], writes=[den.key])
        lbv = P.sb("lbv", [128, HH], F32)
        P.memset(lbv[:, :], 0.0)
        for i in range(1, l + 1):
            P.tt(lbv[:, :], lbv[:, :], lball[:, :, i], ALU.add)
        P.tt(lbv[:, :], lbv[:, :], den[:, :], ALU.mult)
        oml = P.sb("oml", [128, HH], F32)
        P.ts(oml[:, :], lbv[:, :], -1.0, ALU.mult, 1.0, ALU.add)
        ng = P.sb("hng", [128, 1], F32)
        P.ld(ng[:, :], V(Wd["hgrn_norm_g"][l].ap.rearrange("(c o) -> c o", o=1), Wd["hgrn_norm_g"].key))

        def T3(name, dt):
            return P.sb(name, [128, HH, 128], dt)

        def bc_t(t2_, nb):
            return V(t2_.ap.unsqueeze(2).broadcast_to([128, nb, 128]), t2_.key)

        def rows(r0, t0):
            return V(zT.h[HB + r0:HB + r0 + 1024, t0:t0 + 128].rearrange("(h c) t -> c h t", c=128), zT.key)

        def mk_tiles(d):
            t = {}
            for nm in ("qf", "zf", "vf", "t1", "t2", "lf_tm", "E1", "E2", "Tf", "yv"):
                t[nm] = T3("h%s%d" % (nm, d), F32)
            for nm in ("QT", "KT", "KH", "VB", "VT", "KHT", "Tb"):
                t[nm] = T3("h%s%d" % (nm, d), BF16)
            t["Vexp"] = P.sb("Vexp%d" % d, [128, HH, 4, 128], BF16)
            t["Sc2"] = [P.sb("Sc2_%d" % d, [128, 2, 128], BF16) for _ in range(2)]
            t["psi"] = 0
            return t
        TL = [mk_tiles(0), mk_tiles(1)]
        yscr = [yhfT, yhbT]

        def grp(d, g):
            T_ = TL[d]
            qf, zf, vf, t1, t2, lf_tm, E1, E2, Tf, yv = (T_[k] for k in ("qf", "zf", "vf", "t1", "t2", "lf_tm", "E1", "E2", "Tf", "yv"))
            QT, KT, KH, VB, VT, KHT, Tb = (T_[k] for k in ("QT", "KT", "KH", "VB", "VT", "KHT", "Tb"))
            Vexp = T_["Vexp"]

            def ps_():
                t = P.psb[d * 4 + (T_["psi"] % 4)]
                T_["psi"] += 1
                return t
            hm = cst[:, (C_HMF if d == 0 else C_HMB):(C_HMF if d == 0 else C_HMB) + 128]
            corder = range(4) if d == 0 else range(3, -1, -1)
            lastoff = 31 if d == 0 else 0
            t0 = g * 128
            P.ld(qf[:, :, :], rows(0, t0))
            P.ld(zf[:, :, :], rows(1024 * (1 + d), t0))
            P.ld(vf[:, :, :], rows(3072, t0))
            yield
            P.act(qf[:, :, :], qf[:, :, :], AF.Silu)
            P.act(t1[:, :, :], zf[:, :, :], AF.Sigmoid)
            yield
            P.tt(t1[:, :, :], t1[:, :, :], bc_t(oml[:, :], HH), ALU.mult)
            P.tt(t2[:, :, :], t1[:, :, :], bc_t(lbv[:, :], HH), ALU.add)
            yield
            P.ts(t2[:, :, :], t2[:, :, :], F_TINY, ALU.max)
            P.act(t2[:, :, :], t2[:, :, :], AF.Ln)
            yield
            P.stt(t1[:, :, :], t1[:, :, :], -1.0, bc_t(oml[:, :], HH), ALU.mult, ALU.add)
            for b in range(2):
                ps = ps_()
                for j in range(4):
                    h = b * 4 + j
                    P.mm(ps[:, j * 128:(j + 1) * 128], t2[:, h, :], ident_f)
                yield
                P.cp(lf_tm[:, b * 4:(b + 1) * 4, :].re("p a b -> p (a b)"), ps[:, :], eng=evac_eng())
            for b in range(2):
                ps = ps_()
                for j in range(4):
                    h = b * 4 + j
                    P.mm(ps[:, j * 128:(j + 1) * 128], lf_tm[:, h, :], hm)
                yield
                P.act(E1[:, b * 4:(b + 1) * 4, :].re("p a b -> p (a b)"), ps[:, :], AF.Exp)
                P.act(E2[:, b * 4:(b + 1) * 4, :].re("p a b -> p (a b)"), ps[:, :], AF.Exp, scale=-1.0)
            yield
            P.tt(QT[:, :, :], qf[:, :, :], E1[:, :, :], ALU.mult)
            P.tt(t1[:, :, :], t1[:, :, :], E2[:, :, :], ALU.mult)
            yield
            P.cp(KT[:, :, :], t1[:, :, :], eng="act")
            gC = V(E1.h[:, :, :].rearrange("p h (c t) -> p h c t", t=32)[:, :, :, lastoff:lastoff + 1]
                   .broadcast_to([128, HH, 4, 32]), E1.key)
            P.tt(KH[:, :, :].re("p h (c t) -> p h c t", t=32), t1[:, :, :].re("p h (c t) -> p h c t", t=32), gC, ALU.mult)
            P.cp(VB[:, :, :], vf[:, :, :], eng="act")
            yield
            for (srcT, dstT) in ((VB, VT), (KH, KHT)):
                for b in range(2):
                    ps = ps_()
                    for j in range(4):
                        h = b * 4 + j
                        P.mm(ps[:, j * 128:(j + 1) * 128], srcT[:, h, :], ident_b)
                    yield
                    P.cp(dstT[:, b * 4:(b + 1) * 4, :].re("p a b -> p (a b)"), ps[:, :], eng=evac_eng())
            cmv = cst[:, C_CM:C_CM + 4]
            P.tt(Vexp[:, :, :, :],
                 V(VT.h[:, :, :].unsqueeze(2).broadcast_to([128, HH, 4, 128]), VT.key),
                 V(cmv.ap.unsqueeze(1).unsqueeze(3).broadcast_to([128, HH, 4, 128]), cmv.key), ALU.mult)
            yield
            for pr_ in range(4):
                hs = [pr_ * 2, pr_ * 2 + 1]
                psc = ps_()
                for j, h in enumerate(hs):
                    P.mm(psc[:, j * 128:(j + 1) * 128], KT[:, h, :], QT[:, h, :])
                pqs = []
                for j, h in enumerate(hs):
                    pq = ps_()
                    P.mm(pq[:, :], KHT[:, h, :], Vexp[:, h, :, :].re("p c v -> p (c v)"))
                    pqs.append(pq)
                yield
                sc2 = T_["Sc2"][pr_ % 2]
                P.tt(sc2[:, :, :], psc[:, 0:256].re("p (a b) -> p a b", a=2),
                     V(hm.ap.unsqueeze(1).broadcast_to([128, 2, 128]), hm.key), ALU.mult)
                py = ps_()
                for c4 in corder:
                    cs_ = slice(c4 * 32, (c4 + 1) * 32)
                    tc_ = c4 * 32 + lastoff
                    for j, h in enumerate(hs):
                        po_ = py[:, j * 128 + c4 * 32:j * 128 + (c4 + 1) * 32]
                        P.mm(po_, VT[:, h, :], sc2[:, j, cs_], start=True, stop=False)
                        P.mm(po_, Tb[:, h, :], QT[:, h, cs_], start=False, stop=True)
                    yield
                    for j, h in enumerate(hs):
                        P.stt(Tf[:, h, :], Tf[:, h, :], E1[:, h, tc_:tc_ + 1], pqs[j][:, c4 * 128:(c4 + 1) * 128],
                              ALU.mult, ALU.add)
                        P.cp(Tb[:, h, :], Tf[:, h, :], eng="act")
                    yield
                P.cp(yv[:, pr_ * 2:pr_ * 2 + 2, :].re("p a b -> p (a b)"), py[:, 0:256], eng="act")
            dv = V(yscr[d].h[:, t0:t0 + 128].rearrange("(h c) t -> c h t", c=128), yscr[d].key)
            P.ld(vk(dv, g), yv[:, :, :], q="pool")
            yield

        for d in range(2):
            P.memset(TL[d]["Tf"][:, :, :], 0.0)
            P.memset(TL[d]["Tb"][:, :, :], 0.0)
        for i in range(NG):
            gens = [grp(0, i), grp(1, NG - 1 - i)]
            alive = [True, True]
            while any(alive):
                for k in range(2):
                    if alive[k]:
                        try:
                            next(gens[k])
                        except StopIteration:
                            alive[k] = False
        P.barrier()
        yf, yb, gf, t2p = T3("hyf", F32), T3("hyb", F32), T3("hgf", F32), T3("ht2p", F32)
        sqb = T3("hsq", BF16)
        yob = T3("hyob", BF16)
        for g in range(NG):
            t0 = g * 128
            P.ld(yf[:, :, :], V(yhfT.h[:, t0:t0 + 128].rearrange("(h c) t -> c h t", c=128), yhfT.key))
            P.ld(yb[:, :, :], V(yhbT.h[:, t0:t0 + 128].rearrange("(h c) t -> c h t", c=128), yhbT.key))
            P.ld(gf[:, :, :], rows(4096, t0))
            P.tt(yf[:, :, :], yf[:, :, :], yb[:, :, :], ALU.add)
            P.act(sqb[:, :, :], yf[:, :, :], AF.Square)
            for b in range(2):
                ps = P.ps()
                P.mm(ps[:, :], ones_b, sqb[:, b * 4:(b + 1) * 4, :].re("p a b -> p (a b)"))
                P.act(t2p[:, b * 4:(b + 1) * 4, :].re("p a b -> p (a b)"), ps[:, :], AF.Ln, bias=cst[:, C_EPS:C_EPS + 1], scale=1.0 / 128)
            P.act(t2p[:, :, :], t2p[:, :, :], AF.Exp, scale=-0.5)
            P.stt(yf[:, :, :], yf[:, :, :], ng[:, 0:1], t2p[:, :, :], ALU.mult, ALU.mult)
            P.act(gf[:, :, :], gf[:, :, :], AF.Silu)
            P.tt(yob[:, :, :], yf[:, :, :], gf[:, :, :], ALU.mult)
            P.ld(vk(V(yT[1].h[:, t0:t0 + 128].rearrange("(h c) t -> c h t", c=128), yT[1].key), g), yob[:, :, :], q="pool")
        P.phase_end()

    MB = RWKV_W + HGRN_W
    qn_s = P.dram("qn_s", [8, 128, S], BF16)
    qr_s = P.dram("qr_s", [8, 64, S], BF16)
    kn_s = P.dram("kn_s", [8, 128, S], BF16)
    kr_s = P.dram("kr_s", [64, S], BF16)
    v_s = P.dram("v_s", [S, 1024], BF16)
    TWO_PI = 6.283185307179586
    PI = 3.141592653589793

    def phase_M1(l):
        wqf = P.sb("wqf", [128, 6, 1536], F32)
        P.ld(wqf[:, :, :], V(Wd["mla_w_uq"][l].ap.rearrange("(c p) m -> p c m", p=128), Wd["mla_w_uq"].key))
        wq = P.sb("wq", [128, 6, 1536], BF16)
        P.cp(wq[:, 0:3, :], wqf[:, 0:3, :])
        P.cp(wq[:, 3:6, :], wqf[:, 3:6, :], eng="act")
        wkf = P.sb("wkf", [128, 4, 2048], F32)
        P.ld(wkf[:, :, :], V(Wd["mla_w_ukv"][l].ap.rearrange("(c p) m -> p c m", p=128), Wd["mla_w_ukv"].key))
        wk = P.sb("wk", [128, 4, 2048], BF16)
        P.cp(wk[:, 0:2, :], wkf[:, 0:2, :])
        P.cp(wk[:, 2:4, :], wkf[:, 2:4, :], eng="act")
        qg = P.sb("qg", [128, 6], F32)
        load_vec(qg, Wd["mla_q_norm_g"][l], 6)
        kg = P.sb("kg", [128, 4], F32)
        load_vec(kg, Wd["mla_kv_norm_g"][l], 4)
        cq = P.sb("cq", [128, 6, 512], F32)
        ckv = P.sb("ckv", [128, 4, 512], F32)
        krf = P.sb("krf", [64, 512], F32)
        cqn = P.sb("cqn", [128, 6, 512], BF16)
        ckvn = P.sb("ckvn", [128, 4, 512], BF16)
        sq = P.sb("msq", [128, 6, 512], BF16)
        rstd = P.sb("mrstd", [128, 512], F32)
        posi = P.sb("posi", [64, 512], I32)
        ang = P.sb("ang", [64, 512], F32)
        am = P.sb("am", [64, 512], F32)
        cosv = P.sb("cosv", [64, 512], F32)
        sinv = P.sb("sinv", [64, 512], F32)
        xr = P.sb("xr", [64, 512], F32)
        r1 = P.sb("r1", [64, 512], F32)
        r2 = P.sb("r2", [64, 512], F32)
        ob = [P.sb("mob", [128, 512], BF16) for _ in range(4)]
        oi = [0]
        rot = cst[0:64, C_ROT:C_ROT + 64]
        invf = cst[0:64, C_IF:C_IF + 1]

        def rope(xsrc_ps, dst_bf):
            P.cp(xr[:, :], xsrc_ps, eng="act")
            ps = P.ps()
            P.mm(ps[0:64, :], rot, xr[:, :])
            P.tt(r1[:, :], xr[:, :], cosv[:, :], ALU.mult)
            P.tt(r2[:, :], ps[0:64, :], sinv[:, :], ALU.mult)
            P.tt(dst_bf, r1[:, :], r2[:, :], ALU.add)

        for n in range(NT):
            n0 = n * 512
            tsl = slice(n0, n0 + 512)
            P.ld(cq[:, :, :], V(zT.h[MB:MB + 768, tsl].rearrange("(c p) t -> p c t", p=128), zT.key))
            P.ld(ckv[:, :, :], V(zT.h[MB + 768:MB + 1280, tsl].rearrange("(c p) t -> p c t", p=128), zT.key))
            P.ld(krf[:, :], zT[MB + 1280:MB + 1344, tsl])
            rmsnorm_tile(cq, 6, 512, qg, lambda c: cqn[:, c, :], Q_LORA, sq, rstd)
            rmsnorm_tile(ckv, 4, 512, kg, lambda c: ckvn[:, c, :], KV_LORA, sq, rstd)
            P.ld(posi[:, :], V(pos_in.h[0:1, tsl].partition_broadcast(64), pos_in.key))
            P.cp(ang[:, :], posi[:, :])
            P.tt(ang[:, :], ang[:, :], invf.bc([64, 512]), ALU.mult)
            for (dstv, offs) in ((sinv, 0.0), (cosv, 0.25)):
                P.ts(am[:, :], ang[:, :], 1.0 / TWO_PI, ALU.mult, offs, ALU.add)
                P.cp(posi[:, :], am[:, :])
                P.cp(r1[:, :], posi[:, :])
                P.tt(am[:, :], am[:, :], r1[:, :], ALU.subtract)
                P.act(dstv[:, :], am[:, :], AF.Sin, scale=6.283185)
            o = ob[oi[0] % 4]; oi[0] += 1
            ps = P.ps()
            P.mm(ps[0:64, :], ident_f[0:64, 0:64], krf[:, :])
            rope(ps[0:64, :], o[0:64, :])
            P.ld(vk(kr_s[:, tsl], n), o[0:64, :], q="pool")
            for h in range(8):
                ps = P.ps()
                for c in range(6):
                    P.mm(ps[:, :], wq[:, c, 192 * h:192 * h + 128], cqn[:, c, :], start=(c == 0), stop=(c == 5))
                o = ob[oi[0] % 4]; oi[0] += 1
                P.cp(o[:, :], ps[:, :], eng=evac_eng())
                P.ld(vk(qn_s[h, :, tsl], (h, n)), o[:, :], q="pool")
                ps = P.ps()
                for c in range(6):
                    P.mm(ps[0:64, :], wq[:, c, 192 * h + 128:192 * h + 192], cqn[:, c, :], start=(c == 0), stop=(c == 5))
                o = ob[oi[0] % 4]; oi[0] += 1
                rope(ps[0:64, :], o[0:64, :])
                P.ld(vk(qr_s[h, :, tsl], (h, n)), o[0:64, :], q="pool")
                ps = P.ps()
                for c in range(4):
                    P.mm(ps[:, :], wk[:, c, 256 * h:256 * h + 128], ckvn[:, c, :], start=(c == 0), stop=(c == 3))
                o = ob[oi[0] % 4]; oi[0] += 1
                P.cp(o[:, :], ps[:, :], eng=evac_eng())
                P.ld(vk(kn_s[h, :, tsl], (h, n)), o[:, :], q="pool")
            for j in range(4):
                for hb_ in range(2):
                    ps = P.ps()
                    for c in range(4):
                        rv = V(wk.h[:, c, :].rearrange("p (h two v) -> p h two v", two=2, v=128)[:, hb_ * 4:(hb_ + 1) * 4, 1, :], wk.key)
                        P.mm(ps[:, :].re("p (h v) -> p h v", h=4), ckvn[:, c, j * 128:(j + 1) * 128], rv,
                             start=(c == 0), stop=(c == 3))
                    o = ob[oi[0] % 4]; oi[0] += 1
                    P.cp(o[:, :], ps[:, :], eng=evac_eng())
                    P.ld(vk(v_s[n0 + j * 128:n0 + (j + 1) * 128, hb_ * 512:(hb_ + 1) * 512], (n, j, hb_)), o[:, :], q="pool")
        P.phase_end()

    def phase_M2(l):
        SCALE = 192.0 ** -0.5
        kr = P.sb("kr", [64, S], BF16)
        P.ld(kr[:, :], kr_s[:, :])
        qn = P.sb("qn", [128, S], BF16)
        qr = P.sb("qr", [64, S], BF16)
        kn = P.sb("kn", [128, S], BF16)
        vh = P.sb("vh", [128, NG, 128], BF16)
        pt = [P.sb("pt", [128, 512], BF16) for _ in range(3)]
        rden = P.sb("rden", [128, 512], F32)
        oo = [P.sb("oo", [128, 512], BF16) for _ in range(2)]
        pti = 0
        P.ps_pool = [0, 1, 2, 3, 4, 5]
        po, pd = P.psb[6], P.psb[7]
        for h in range(8):
            P.ld(qn[:, :], qn_s[h, :, :])
            P.ld(qr[:, :], qr_s[h, :, :])
            P.ld(kn[:, :], kn_s[h, :, :])
            P.ld(vh[:, :, :], V(v_s.h[:, h * 128:(h + 1) * 128].rearrange("(g p) v -> p g v", p=128), v_s.key))
            for n in range(NT):
                tsl = slice(n * 512, (n + 1) * 512)
                def scores(g_, tsl=tsl):
                    ks = slice(g_ * 128, (g_ + 1) * 128)
                    ps_ = P.ps()
                    P.mm(ps_[:, :], kn[:, ks], qn[:, tsl], start=True, stop=False)
                    P.mm(ps_[:, :], kr[:, ks], qr[:, tsl], start=False, stop=True)
                    return ps_
                ps_next = scores(0)
                for g in range(NG):
                    ps = ps_next
                    if g + 1 < NG:
                        ps_next = scores(g + 1)
                    p_ = pt[pti % 3]; pti += 1
                    P.act(p_[:, :], ps[:, :], AF.Exp, scale=SCALE)
                    P.mm(po[:, :], vh[:, g, :], p_[:, :], start=(g == 0), stop=(g == NG - 1))
                    P.mm(pd[:, :], ones_b, p_[:, :], start=(g == 0), stop=(g == NG - 1))
                P.op("dve", (lambda e, o=rden[:, :].ap, i=pd[:, :].ap: e.reciprocal(o, i)), reads=[pd.key], writes=[rden.key])
                o = oo[n % 2]
                P.tt(o[:, :], po[:, :], rden[:, :], ALU.mult)
                P.ld(vk(yT[2][h * 128:(h + 1) * 128, tsl], (h, n)), o[:, :], q="pool")
        P.ps_pool = list(range(8))
        P.phase_end()

    GB = RWKV_W + HGRN_W + MLA_W

    def phase_GF(l, last):
        g2 = P.sb("ln2g", [128, KC], F32)
        load_vec(g2, Wd["ln2_g"][l], KC)
        gfin = P.sb("fing", [128, KC], F32)
        load_vec(gfin, fin_g[:], KC)
        hb = P.sb("hb", [128, KC, 512], F32)
        hn2 = P.sb("hn2", [128, KC, 512], BF16)
        sq = P.sb("sq2", [128, KC, 512], BF16)
        rstd = P.sb("rstd2", [128, 512], F32)
        wbig = [P.sb("wGb", [128, 64, 128], BF16) for _ in range(2)]
        wbufs = [P.sb("wG", [128, KC, 128], BF16) for _ in range(3)]
        gz = [P.sb("gz", [128, 512], F32) for _ in range(3)]
        acc = P.sb("acc", [128, 512], F32)
        tmpf = [P.sb("tmpf", [128, 512], F32) for _ in range(2)]
        ptf = P.sb("ptf", [128, 4, PLE], F32)
        ptb = P.sb("ptb", [128, 2, 512], BF16)
        ot = P.sb("ot", [128, 512], F32)
        big0 = P.sb_off
        ys = [P.sb("ys%d" % n, [128, 8, 512], BF16) for n in range(3)]
        mixed = P.sb("mixed", [128, KC, 512], BF16)
        P.sb_off = big0
        h1 = P.sb("h1", [128, 64, 512], BF16)
        wi = [0]

        def wb():
            wi[0] += 1
            return wbufs[wi[0] % 3]

        def wq():
            return "sp" if wi[0] % 2 else "pool"

        for n in range(NT):
            tsl = slice(n * 512, (n + 1) * 512)
            P.ld(hb[:, :, :], V(hT.h[:, tsl].rearrange("(c p) t -> p c t", p=128), hT.key))
            for b in range(3):
                P.ld(ys[b][:, :, :], V(yT[b].h[:, tsl].rearrange("(c p) t -> p c t", p=128), yT[b].key))
            P.ld(ptf[:, :, :], V(p_in.h[l, tsl, :].rearrange("(j p) f -> p j f", p=128), p_in.key))
            for c in range(2):
                ps = P.ps()
                for j in range(4):
                    P.mm(ps[:, j * 128:(j + 1) * 128], ptf[:, j, c * 128:(c + 1) * 128], ident_f)
                P.cp(ptb[:, c, :], ps[:, :], eng=evac_eng())
            for m in range(KC):
                for b in range(3):
                    w = wb()
                    P.ld(w[:, 0:8, :], WS["w_b%d" % b][m, :, :, :], q=wq())
                    P.ld(gz[b][:, :], zT[GB + b * 2048 + m * 128:GB + b * 2048 + (m + 1) * 128, tsl])
                    ps = P.ps()
                    for c in range(8):
                        P.mm(ps[:, :], w[:, c, :], ys[b][:, c, :], start=(c == 0), stop=(c == 7))
                    P.act(gz[b][:, :], gz[b][:, :], AF.Sigmoid)
                    if b == 0:
                        P.tt(acc[:, :], ps[:, :], gz[b][:, :], ALU.mult)
                    else:
                        t = tmpf[b % 2]
                        P.tt(t[:, :], ps[:, :], gz[b][:, :], ALU.mult)
                        if b == 1:
                            P.tt(acc[:, :], acc[:, :], t[:, :], ALU.add)
                        else:
                            P.tt(mixed[:, m, :], acc[:, :], t[:, :], ALU.add)
            for m in range(KC):
                w = wb()
                P.ld(w[:, 0:KC, :], WS["w_o"][m, :, :, :], q=wq())
                ps = P.ps()
                for c in range(KC):
                    P.mm(ps[:, :], w[:, c, :], mixed[:, c, :], start=(c == 0), stop=(c == KC - 1))
                P.tt(hb[:, m, :], hb[:, m, :], ps[:, :], ALU.add)
            P.barrier()
            rmsnorm_tile(hb, KC, 512, g2, lambda c: hn2[:, c, :], D_MODEL, sq, rstd)
            for m in range(64):
                w = wb()
                P.ld(w[:, 0:KC, :], WS["w_mlp1"][m, :, :, :], q=wq())
                ps = P.ps()
                for c in range(KC):
                    P.mm(ps[:, :], w[:, c, :], hn2[:, c, :], start=(c == 0), stop=(c == KC - 1))
                t = tmpf[m % 2]
                P.act(t[:, :], ps[:, :], AF.Relu)
                P.tt(h1[:, m, :], t[:, :], t[:, :], ALU.mult)
            for m in range(KC):
                w = wbig[m % 2]
                P.ld(w[:, :, :], WS["w_mlp2"][m, :, :, :], q=("sp" if m % 2 else "pool"))
                ps = P.ps()
                for c in range(64):
                    P.mm(ps[:, :], w[:, c, :], h1[:, c, :], start=(c == 0), stop=(c == 63))
                P.tt(hb[:, m, :], hb[:, m, :], ps[:, :], ALU.add)
            for c in range(KC):
                P.cp(hn2[:, c, :], hb[:, c, :], eng=evac_eng())
            for m in range(KC):
                w = wb()
                P.ld(w[:, 0:KC, :], WS["w_pg"][m, :, :, :], q=wq())
                w2_ = wb()
                P.ld(w2_[:, 0:2, :], WS["w_pe"][m, :, :, :])
                ps = P.ps()
                for c in range(KC):
                    P.mm(ps[:, :], w[:, c, :], hn2[:, c, :], start=(c == 0), stop=(c == KC - 1))
                pe_ = P.ps()
                for c in range(2):
                    P.mm(pe_[:, :], w2_[:, c, :], ptb[:, c, :], start=(c == 0), stop=(c == 1))
                t = tmpf[m % 2]
                P.act(t[:, :], ps[:, :], AF.Sigmoid)
                P.tt(t[:, :], t[:, :], pe_[:, :], ALU.mult)
                P.tt(hb[:, m, :], hb[:, m, :], t[:, :], ALU.add)
            if not last:
                P.ld(vk(V(hT.h[:, tsl].rearrange("(c p) t -> p c t", p=128), hT.key), ("o", n)), hb[:, :, :], q="pool")
            else:
                ps = P.ps()
                for c in range(KC):
                    P.act(sq[:, c, :], hb[:, c, :], AF.Square)
                for c in range(KC):
                    P.mm(ps[:, :], ones_b, sq[:, c, :], start=(c == 0), stop=(c == KC - 1))
                P.act(rstd[:, :], ps[:, :], AF.Sqrt, bias=cst[:, C_EPS:C_EPS + 1], scale=1.0 / D_MODEL)
                P.recip(rstd[:, :], rstd[:, :])
                for c in range(KC):
                    P.stt(hb[:, c, :], hb[:, c, :], gfin[:, c:c + 1], rstd[:, :], ALU.mult, ALU.mult)
                for j in range(4):
                    for c4 in range(KC // 4):
                        ps = P.ps()
                        for q in range(4):
                            c = c4 * 4 + q
                            P.mm(ps[:, q * 128:(q + 1) * 128], hb[:, c, j * 128:(j + 1) * 128], ident_f)
                        P.cp(ot[:, :], ps[:, :], eng=evac_eng())
                        P.ld(vk(out_d[n * 512 + j * 128:n * 512 + (j + 1) * 128, c4 * 512:(c4 + 1) * 512], (n, j, c4)),
                             ot[:, :], q="pool")
            P.barrier()
        P.phase_end()

    phase_in()
    for l in range(L):
        if stop == "in":
            break
        precast_layer(l)
        phase_A(l)
        if stop == "A":
            break
        phase_R(l)
        if stop in ("R", "Rs", "Re", "Rt", "Rg", "Ri", "Ru", "Rx", "Rx1"):
            break
        phase_H(l)
        if stop == "H":
            break
        phase_M1(l)
        if stop == "M1":
            break
        phase_M2(l)
        if stop == "M2":
            break
        phase_GF(l, l == L - 1)
    P.emit()
    es.close()
    return nc, P


C_ID = 0
C_ONE = 128
C_RM1F = 256
C_RM1B = 768
C_RM2F = 1280
C_RM2B = 1408
C_TRIF = 1536
C_TRIB = 1664
C_HMF = 1792
C_HMB = 1920
C_CM = 2048
C_IF = 2052
C_PI = 2053
C_EPS = 2054
C_GEPS = 2055
C_ROT = 2056
CONST_W = 2128


def make_consts():
    c = np.zeros((128, CONST_W), np.float32)
    r = np.arange(128)[:, None]
    q = np.arange(128)[None, :]
    c[:, C_ID:C_ID + 128] = (r == q)
    c[:, C_ONE:C_ONE + 128] = 1.0
    su, iu = (r < q).astype(np.float32), (r <= q).astype(np.float32)
    sl, il = (r > q).astype(np.float32), (r >= q).astype(np.float32)
    c[:, C_RM1F:C_RM1F + 512] = np.concatenate([-su, -iu, su, iu], axis=1)
    c[:, C_RM1B:C_RM1B + 512] = np.concatenate([-sl, -il, sl, il], axis=1)
    c[:, C_RM2F:C_RM2F + 128] = -sl
    c[:, C_RM2B:C_RM2B + 128] = -su
    c[:, C_TRIF:C_TRIF + 128] = iu
    c[:, C_TRIB:C_TRIB + 128] = il
    same = ((r // 32) == (q // 32)).astype(np.float32)
    c[:, C_HMF:C_HMF + 128] = iu * same
    c[:, C_HMB:C_HMB + 128] = il * same
    c[:, C_CM:C_CM + 4] = ((np.arange(128)[:, None] // 32) == np.arange(4)[None, :])
    inv_freq = (1.0 / (np.float32(10000.0) ** (np.arange(0, 64, 2, dtype=np.float32) / np.float32(64)))).astype(np.float32)
    c[:, C_IF] = inv_freq[np.arange(128) % 32]
    c[:, C_PI] = np.float32(np.pi)
    c[:, C_EPS] = NORM_EPS
    c[:, C_GEPS] = GN_EPS
    R = np.zeros((64, 64), np.float32)
    for m in range(32):
        R[m, m + 32] = -1.0
        R[m + 32, m] = 1.0
    c[0:64, C_ROT:C_ROT + 64] = R.T
    return c


_CACHE = {}


def run(inputs, S, L, ncores):
    key = (S, L)
    if key not in _CACHE:
        _CACHE[key] = build(S, L)
    nc, P = _CACHE[key]
    consts = make_consts()
    in_maps = []
    for b in range(ncores):
        m = {"x": np.ascontiguousarray(inputs["x"][b]),
             "p": np.ascontiguousarray(inputs["p"][:, b]),
             "positions": np.ascontiguousarray(inputs["positions"][b:b + 1]).astype(np.int32),
             "final_g": np.ascontiguousarray(inputs["final_g"]),
             "consts": consts}
        for nm in W_SHAPES:
            m[nm] = np.ascontiguousarray(inputs[nm])
        in_maps.append(m)
    res = run_bass_kernel_spmd(nc, in_maps, core_ids=list(range(ncores)))
    return res


def kernel(**inputs):
    inputs = {k: np.asarray(v) for k, v in inputs.items()}
    B, S, _ = inputs["x"].shape
    L = inputs["w_in"].shape[0]
    res = run(inputs, S, L, B)
    out = np.stack([res.results[b]["out"] for b in range(B)], axis=0)
    return out.astype(np.float32)
```
